# Optimizing a Trainium2 kernel written in Bass

```python
import math
import jax, jax.numpy as jnp
from jax import lax
import numpy as np

D_MODEL = 1024
BATCH = 8
SEQ = 2048
DEPTH = 2
DEC_BATCH = 128
DEC_SEQ = 8
PAST_LEN = 16384
PAGE_SIZE = 128

EPS = 1e-6
D_FF = 2816
C_POOL = D_MODEL // 4
C_GLA = D_MODEL // 4
C_HGRN = D_MODEL // 4
C_SSM = D_MODEL // 4
POOL_WINDOWS = (2, 4, 8, 16)
POOL_GC = C_POOL // 4
POOL_PAST = 16 - 1
GLA_H = 4
GLA_DV = C_GLA // GLA_H
GLA_DK = GLA_DV // 2
GLA_RANK = 16
GLA_TAU = 16.0
GLA_CHUNK = 64
HG_H = 4
HG_DV = C_HGRN // HG_H
HG_DK = 64
HG_CHUNK = 64
SSM_P = 64
SSM_H = C_SSM // SSM_P
SSM_G = 2
SSM_HG = SSM_H // SSM_G
SSM_N = 128
SSM_CONV = 4
SSM_CONV_DIM = C_SSM + 2 * SSM_G * SSM_N
SSM_CHUNK = 64
IN_SIZES = (C_POOL,
            GLA_H * GLA_DK, GLA_H * GLA_DK, C_GLA, C_GLA, GLA_RANK,
            HG_H * HG_DK, HG_H * HG_DK, C_HGRN, C_HGRN,
            C_SSM, SSM_CONV_DIM, SSM_H)
N_IN = sum(IN_SIZES)

kernel_name = 'hybrid_pool_gla_hgrn2_ssd_macaron_step'


def rms_norm(x, g, eps=EPS):
    xf = x.astype(jnp.float32)
    y = xf * lax.rsqrt(jnp.mean(xf * xf, axis=-1, keepdims=True) + eps)
    return (y * g.astype(jnp.float32)).astype(x.dtype)


def swiglu(x, w_gate, w_up, w_down):
    return (jax.nn.silu(x @ w_gate) * (x @ w_up)) @ w_down


def _pad_time(a, t_pad):
    return jnp.pad(a, [(0, 0), (0, t_pad - a.shape[1])] + [(0, 0)] * (a.ndim - 2))


def pool_mixer(xp, buf, pos0, pool_w, pool_scale):
    B, T, C = xp.shape
    xe = jnp.concatenate([buf.astype(xp.dtype), xp], axis=1)
    cs = jnp.cumsum(xe.astype(jnp.float32), axis=1)
    cs = jnp.concatenate([jnp.zeros((B, 1, C), jnp.float32), cs], axis=1)
    pos = pos0 + jnp.arange(T)
    upper = cs[:, POOL_PAST + 1:POOL_PAST + 1 + T]
    diffs = []
    for gi, w in enumerate(POOL_WINDOWS):
        sl = slice(gi * POOL_GC, (gi + 1) * POOL_GC)
        lower = cs[:, POOL_PAST + 1 - w:POOL_PAST + 1 - w + T, sl]
        cnt = jnp.minimum(pos + 1, w).astype(jnp.float32)[None, :, None]
        diffs.append(((upper[..., sl] - lower) / cnt).astype(xp.dtype) - xp[..., sl])
    d = jnp.stack(diffs, axis=2)
    y = jnp.einsum('btgc,gcd->btgd', d, pool_w).reshape(B, T, C) * pool_scale
    return y, xe[:, -POOL_PAST:]


def chunked_gla(q, k, v, log_a, s0, chunk):
    B, T, H, K = q.shape
    V = v.shape[-1]
    L = min(chunk, T)
    n = -(-T // L)
    Tp = n * L

    def prep(a):
        a = _pad_time(a.astype(jnp.float32), Tp)
        return a.reshape((B, n, L) + a.shape[2:]).swapaxes(0, 1)

    causal = jnp.tril(jnp.ones((L, L), dtype=bool))[None, :, :, None, None]

    def step(S, inp):
        qc, kc, vc, gc = inp
        b = jnp.cumsum(gc, axis=1)
        o = jnp.einsum('blhk,bhkv->blhv', qc * jnp.exp(b), S)
        dec = jnp.exp(jnp.where(causal, b[:, :, None] - b[:, None, :], -jnp.inf))
        att = jnp.einsum('bihk,bjhk,bijhk->bijh', qc, kc, dec)
        o = o + jnp.einsum('bijh,bjhv->bihv', att, vc)
        w_end = jnp.exp(b[:, -1:] - b)
        S = S * jnp.exp(b[:, -1])[..., None] + jnp.einsum('bjhk,bjhv->bhkv', kc * w_end, vc)
        return S, o

    S, o = lax.scan(step, s0.astype(jnp.float32), (prep(q), prep(k), prep(v), prep(log_a)))
    o = o.swapaxes(0, 1).reshape(B, Tp, H, V)[:, :T]
    return o.astype(q.dtype), S.astype(s0.dtype)


def chunked_ssd(x, dt, A, Bm, Cm, s0, chunk):
    B, T, G, Hg, P = x.shape
    L = min(chunk, T)
    n = -(-T // L)
    Tp = n * L
    dt = dt.astype(jnp.float32)
    log_a = dt * A.astype(jnp.float32)
    xdt = x.astype(jnp.float32) * dt[..., None]

    def prep(a):
        a = _pad_time(a.astype(jnp.float32), Tp)
        return a.reshape((B, n, L) + a.shape[2:]).swapaxes(0, 1)

    causal = jnp.tril(jnp.ones((L, L), dtype=bool))[None, :, :, None, None]

    def step(S, inp):
        xc, ac, bc, cc = inp
        cum = jnp.cumsum(ac, axis=1)
        decay = jnp.exp(jnp.where(causal, cum[:, :, None] - cum[:, None, :], -jnp.inf))
        cb = jnp.einsum('bign,bjgn->bijg', cc, bc)
        y = jnp.einsum('bijg,bijgh,bjghp->bighp', cb, decay, xc)
        y = y + jnp.einsum('bign,bghpn->bighp', cc, S) * jnp.exp(cum)[..., None]
        w_end = jnp.exp(cum[:, -1:] - cum)
        S = S * jnp.exp(cum[:, -1])[..., None, None] + jnp.einsum('bjgn,bjgh,bjghp->bghpn', bc, w_end, xc)
        return S, y

    S, y = lax.scan(step, s0.astype(jnp.float32), (prep(xdt), prep(log_a), prep(Bm), prep(Cm)))
    y = y.swapaxes(0, 1).reshape(B, Tp, G, Hg, P)[:, :T]
    return y, S.astype(s0.dtype)


def run_layer(x, pool_buf, gla_s, hgrn_s, ssm_s, conv_buf, pos0, lb,
              ffn1_norm, ffn1_w_gate, ffn1_w_up, ffn1_w_down, mix_norm, w_in, pool_w, pool_scale,
              gla_w_gate, gla_gate_bias, gla_norm, hgrn_norm, ssm_conv_w, ssm_conv_b, ssm_dt_bias,
              ssm_A_log, ssm_D, ssm_norm, w_out, ffn2_norm, ffn2_w_gate, ffn2_w_up, ffn2_w_down):
    B, T, _ = x.shape
    x = x + 0.5 * swiglu(rms_norm(x, ffn1_norm), ffn1_w_gate, ffn1_w_up, ffn1_w_down)
    h = rms_norm(x, mix_norm)
    proj = h @ w_in
    split_at = np.cumsum(IN_SIZES)[:-1].tolist()
    (p_x, g_q, g_k, g_v, g_r, g_lr, r_q, r_f, r_i, r_g,
     s_z, s_xbc, s_dt) = jnp.split(proj, split_at, axis=-1)

    o_pool, new_pool = pool_mixer(p_x, pool_buf, pos0, pool_w, pool_scale)

    q = g_q.reshape(B, T, GLA_H, GLA_DK) * (GLA_DK ** -0.5)
    k = g_k.reshape(B, T, GLA_H, GLA_DK)
    v = g_v.reshape(B, T, GLA_H, GLA_DV)
    gate_logit = (g_lr @ gla_w_gate + gla_gate_bias).astype(jnp.float32)
    log_alpha = (jax.nn.log_sigmoid(gate_logit) / GLA_TAU).reshape(B, T, GLA_H, GLA_DK)
    o, new_gla = chunked_gla(q, k, v, log_alpha, gla_s, GLA_CHUNK)
    o_gla = (rms_norm(o, gla_norm) * jax.nn.silu(g_r.reshape(B, T, GLA_H, GLA_DV))).reshape(B, T, C_GLA)

    hq = jax.nn.silu(r_q).reshape(B, T, HG_H, HG_DK)
    lbh = lb.reshape(HG_H, HG_DK)
    zf = r_f.astype(jnp.float32).reshape(B, T, HG_H, HG_DK)
    log_f = jnp.logaddexp(jnp.log(lbh), jnp.log1p(-lbh) + jax.nn.log_sigmoid(zf))
    hk = -jnp.expm1(log_f)
    hi = r_i.reshape(B, T, HG_H, HG_DV)
    o, new_hgrn = chunked_gla(hq, hk, hi, log_f, hgrn_s, HG_CHUNK)
    o_hgrn = (rms_norm(o, hgrn_norm) * jax.nn.silu(r_g.reshape(B, T, HG_H, HG_DV))).reshape(B, T, C_HGRN)

    xe = jnp.concatenate([conv_buf.astype(s_xbc.dtype), s_xbc], axis=1)
    conv = lax.conv_general_dilated(xe, ssm_conv_w[:, None, :].astype(xe.dtype), window_strides=(1,),
                                    padding='VALID', dimension_numbers=('NWC', 'WIO', 'NWC'),
                                    feature_group_count=SSM_CONV_DIM)
    conv = jax.nn.silu(conv + ssm_conv_b)
    new_conv = xe[:, -(SSM_CONV - 1):]
    xs, Bm, Cm = jnp.split(conv, [C_SSM, C_SSM + SSM_G * SSM_N], axis=-1)
    xs = xs.reshape(B, T, SSM_G, SSM_HG, SSM_P)
    Bm = Bm.reshape(B, T, SSM_G, SSM_N)
    Cm = Cm.reshape(B, T, SSM_G, SSM_N)
    dt = jax.nn.softplus(s_dt.astype(jnp.float32) + ssm_dt_bias.astype(jnp.float32)).reshape(B, T, SSM_G, SSM_HG)
    A = -jnp.exp(ssm_A_log.astype(jnp.float32)).reshape(SSM_G, SSM_HG)
    y, new_ssm = chunked_ssd(xs, dt, A, Bm, Cm, ssm_s.reshape(B, SSM_G, SSM_HG, SSM_P, SSM_N), SSM_CHUNK)
    y = y + ssm_D.astype(jnp.float32).reshape(SSM_G, SSM_HG, 1) * xs.astype(jnp.float32)
    y = y.reshape(B, T, C_SSM).astype(x.dtype) * jax.nn.silu(s_z)
    o_ssm = rms_norm(y.reshape(B, T, SSM_G, C_SSM // SSM_G),
                     ssm_norm.reshape(SSM_G, C_SSM // SSM_G)).reshape(B, T, C_SSM)
    new_ssm = new_ssm.reshape(B, SSM_H, SSM_P, SSM_N)

    mix = jnp.concatenate([o_pool, o_gla, o_hgrn, o_ssm], axis=-1) @ w_out
    x = x + mix
    x = x + 0.5 * swiglu(rms_norm(x, ffn2_norm), ffn2_w_gate, ffn2_w_up, ffn2_w_down)
    return x, new_pool, new_gla, new_hgrn, new_ssm, new_conv


def setup_inputs(seed: int = 0) -> dict:
    key = jax.random.key(seed)
    ks = list(jax.random.split(key, 48))

    def nrm(shape, s):
        return jax.random.normal(ks.pop(), shape, jnp.float32) * s

    def gain(shape):
        return 1.0 + nrm(shape, 0.02)

    inp = {}
    inp['x_prompt'] = nrm((BATCH, SEQ, D_MODEL), 1.0)
    inp['x_sample'] = nrm((DEC_BATCH, DEC_SEQ, D_MODEL), 1.0)
    inp['state_pool'] = nrm((DEPTH, DEC_BATCH, POOL_PAST, C_POOL), 1.0)
    inp['state_gla'] = nrm((DEPTH, DEC_BATCH, GLA_H, GLA_DK, GLA_DV), 0.5)
    inp['state_hgrn'] = nrm((DEPTH, DEC_BATCH, HG_H, HG_DK, HG_DV), 0.5)
    inp['state_ssm'] = nrm((DEPTH, DEC_BATCH, SSM_H, SSM_P, SSM_N), 0.3)
    inp['state_conv'] = nrm((DEPTH, DEC_BATCH, SSM_CONV - 1, SSM_CONV_DIM), 1.0)
    inp['ffn1_norm'] = gain((DEPTH, D_MODEL))
    inp['ffn1_w_gate'] = nrm((DEPTH, D_MODEL, D_FF), D_MODEL ** -0.5)
    inp['ffn1_w_up'] = nrm((DEPTH, D_MODEL, D_FF), D_MODEL ** -0.5)
    inp['ffn1_w_down'] = nrm((DEPTH, D_FF, D_MODEL), D_FF ** -0.5)
    inp['mix_norm'] = gain((DEPTH, D_MODEL))
    inp['w_in'] = nrm((DEPTH, D_MODEL, N_IN), D_MODEL ** -0.5)
    inp['pool_w'] = nrm((DEPTH, len(POOL_WINDOWS), POOL_GC, POOL_GC), POOL_GC ** -0.5)
    inp['pool_scale'] = gain((DEPTH, C_POOL))
    inp['gla_w_gate'] = nrm((DEPTH, GLA_RANK, GLA_H * GLA_DK), GLA_RANK ** -0.5)
    inp['gla_gate_bias'] = nrm((DEPTH, GLA_H * GLA_DK), 0.1)
    inp['gla_norm'] = gain((DEPTH, GLA_DV))
    inp['hgrn_lb_logits'] = nrm((DEPTH, HG_H * HG_DK), 0.5)
    inp['hgrn_norm'] = gain((DEPTH, HG_DV))
    inp['ssm_conv_w'] = nrm((DEPTH, SSM_CONV, SSM_CONV_DIM), 0.5)
    inp['ssm_conv_b'] = nrm((DEPTH, SSM_CONV_DIM), 0.02)
    dt0 = jnp.exp(jax.random.uniform(ks.pop(), (DEPTH, SSM_H), jnp.float32,
                                     minval=math.log(1e-3), maxval=math.log(1e-1)))
    inp['ssm_dt_bias'] = dt0 + jnp.log(-jnp.expm1(-dt0))
    inp['ssm_A_log'] = jnp.log(jax.random.uniform(ks.pop(), (DEPTH, SSM_H), jnp.float32, minval=1.0, maxval=16.0))
    inp['ssm_D'] = gain((DEPTH, SSM_H))
    inp['ssm_norm'] = gain((DEPTH, C_SSM))
    inp['w_out'] = nrm((DEPTH, D_MODEL, D_MODEL), D_MODEL ** -0.5)
    inp['ffn2_norm'] = gain((DEPTH, D_MODEL))
    inp['ffn2_w_gate'] = nrm((DEPTH, D_MODEL, D_FF), D_MODEL ** -0.5)
    inp['ffn2_w_up'] = nrm((DEPTH, D_MODEL, D_FF), D_MODEL ** -0.5)
    inp['ffn2_w_down'] = nrm((DEPTH, D_FF, D_MODEL), D_FF ** -0.5)
    inp['final_norm'] = gain((D_MODEL,))
    return inp


def reference(x_prompt, x_sample, state_pool, state_gla, state_hgrn, state_ssm, state_conv,
              ffn1_norm, ffn1_w_gate, ffn1_w_up, ffn1_w_down, mix_norm, w_in, pool_w, pool_scale,
              gla_w_gate, gla_gate_bias, gla_norm, hgrn_lb_logits, hgrn_norm,
              ssm_conv_w, ssm_conv_b, ssm_dt_bias, ssm_A_log, ssm_D, ssm_norm, w_out,
              ffn2_norm, ffn2_w_gate, ffn2_w_up, ffn2_w_down, final_norm):
    lb_cum = jnp.cumsum(jax.nn.softmax(hgrn_lb_logits.astype(jnp.float32), axis=0), axis=0)
    lower_bounds = lb_cum - lb_cum[0:1]

    bp = x_prompt.shape[0]
    hp, hs = x_prompt, x_sample
    pp, ps, gp, gs, rp, rs, sp, ss, cp, cs = ([] for _ in range(10))
    for l in range(DEPTH):
        lw = (ffn1_norm[l], ffn1_w_gate[l], ffn1_w_up[l], ffn1_w_down[l], mix_norm[l], w_in[l],
              pool_w[l], pool_scale[l], gla_w_gate[l], gla_gate_bias[l], gla_norm[l], hgrn_norm[l],
              ssm_conv_w[l], ssm_conv_b[l], ssm_dt_bias[l], ssm_A_log[l], ssm_D[l], ssm_norm[l], w_out[l],
              ffn2_norm[l], ffn2_w_gate[l], ffn2_w_up[l], ffn2_w_down[l])
        hp, a, b, c, d, e = run_layer(
            hp,
            jnp.zeros((bp, POOL_PAST, C_POOL), state_pool.dtype),
            jnp.zeros((bp, GLA_H, GLA_DK, GLA_DV), state_gla.dtype),
            jnp.zeros((bp, HG_H, HG_DK, HG_DV), state_hgrn.dtype),
            jnp.zeros((bp, SSM_H, SSM_P, SSM_N), state_ssm.dtype),
            jnp.zeros((bp, SSM_CONV - 1, SSM_CONV_DIM), state_conv.dtype),
            0, lower_bounds[l], *lw)
        pp.append(a); gp.append(b); rp.append(c); sp.append(d); cp.append(e)
        hs, a, b, c, d, e = run_layer(
            hs, state_pool[l], state_gla[l], state_hgrn[l], state_ssm[l], state_conv[l],
            PAST_LEN, lower_bounds[l], *lw)
        ps.append(a); gs.append(b); rs.append(c); ss.append(d); cs.append(e)

    y_prompt = rms_norm(hp, final_norm)
    y_sample = rms_norm(hs, final_norm)
    return (y_prompt, y_sample,
            jnp.stack(pp), jnp.stack(ps),
            jnp.stack(gp), jnp.stack(gs),
            jnp.stack(rp), jnp.stack(rs),
            jnp.stack(sp), jnp.stack(ss),
            jnp.stack(cp), jnp.stack(cs))
```

```python
import contextlib
import numpy as np
import concourse.bass as bass
import concourse.mybir as mybir
from concourse.bass_utils import run_bass_kernel_spmd

F32 = mybir.dt.float32
BF16 = mybir.dt.bfloat16
AF = mybir.ActivationFunctionType
ALU = mybir.AluOpType
AX = mybir.AxisListType

NCORES = 8
D = 1024
DFF = 2816
NIN = 3092
TP = 2048
NSEQ = 16
TSQ = 8
T = TP + NSEQ * TSQ
NT = T // 128
EPS = 1e-6
DEPTH = 2
C_GQ, C_GK = 256, 384
NINW = NIN - 256
C_PX, C_GV, C_GR, C_GLR = 0, 256, 512, 768
C_RQ, C_RF, C_RI, C_RG, C_SZ, C_XBC, C_DT = 784, 1040, 1296, 1552, 1808, 2064, 2832


class V:
    __slots__ = ("ap", "keys")

    def __init__(self, ap, keys):
        self.ap = ap
        self.keys = tuple(keys) if isinstance(keys, list) else (keys,)

    def _mk(self, ap):
        return V(ap, list(self.keys))

    def __getitem__(self, idx):
        return self._mk(self.ap[idx])

    def v(self, fn):
        return self._mk(fn(self.ap))

    def k(self, *keys):
        return V(self.ap, list(keys))

    def r(self, pat, **kw):
        return self._mk(self.ap.rearrange(pat, **kw))

    def bc(self, shape):
        return self._mk(self.ap.broadcast_to(list(shape)))

    def us(self, axis):
        return self._mk(self.ap.unsqueeze(axis))


def _ap(x):
    return x.ap if isinstance(x, V) else x


def _keys(xs):
    ks = []
    for x in xs:
        if isinstance(x, V):
            ks.extend(x.keys)
    return ks


class Ref:
    __slots__ = ("id",)


class Sched:
    ENG = ("pe", "act", "dve", "pool", "sp")

    def __init__(self, nc, sem_epoch=12000, n_dma_sems=8):
        self.nc = nc
        self.ops = []
        self.last_w = {}
        self.readers = {}
        self.sem_epoch = sem_epoch
        self.n_dma_sems = n_dma_sems
        self.dma_open = []
        self.strict = STRICT_SAME_ENGINE
        self.cur = None

    def add(self, eng, fn, reads=(), writes=(), dma=0, extra=()):
        if POOL_AS_DVE and eng == "pool" and not dma:
            eng = "dve"
        if self.cur is not None:
            ref = Ref()
            self.cur.append((eng, fn, reads, writes, dma, extra, ref))
            return ref
        return self._add(eng, fn, reads, writes, dma, extra)

    @contextlib.contextmanager
    def stream(self):
        lst = []
        prev, self.cur = self.cur, lst
        try:
            yield lst
        finally:
            self.cur = prev

    def merge(self, streams, weights=None):
        if weights is None:
            weights = [1.0] * len(streams)
        keep = [i for i, st in enumerate(streams) if st]
        weights = [weights[i] for i in keep]
        streams = [streams[i] for i in keep]
        pos = [0] * len(streams)
        total = sum(len(st) for st in streams)
        for _ in range(total):
            bi, bv = -1, None
            for i, st in enumerate(streams):
                if pos[i] < len(st):
                    v = (pos[i] + 0.5) / len(st) * weights[i]
                    if bv is None or v < bv:
                        bi, bv = i, v
            eng, fn, reads, writes, dma, extra, ref = streams[bi][pos[bi]]
            pos[bi] += 1
            ref.id = self._add(eng, fn, reads, writes, dma, extra)

    def _add(self, eng, fn, reads=(), writes=(), dma=0, extra=()):
        rk = _keys(reads)
        wk = _keys(writes)
        if PSUM_EXCLUSIVE:
            pk = [k for k in rk if isinstance(k, tuple) and k[0] == "ps"]
            if pk:
                rk = [k for k in rk if k not in pk]
                wk = list(wk) + [k for k in pk if k not in wk]
        deps = {}
        for d in extra:
            deps[d.id if isinstance(d, Ref) else d] = True
        for k in rk:
            w = self.last_w.get(k)
            if w is not None:
                deps[w] = True
        for k in wk:
            w = self.last_w.get(k)
            if w is not None:
                deps.setdefault(w, False)
            for r in self.readers.get(k, ()):
                deps.setdefault(r, False)
        oid = len(self.ops)
        self.ops.append(dict(eng=eng, fn=fn, deps=deps, dma=dma, sig=None, needed=False))
        for k in wk:
            self.last_w[k] = oid
            self.readers[k] = []
        for k in rk:
            if k not in wk:
                lst = self.readers.setdefault(k, [])
                if not dma:
                    for j in range(len(lst)):
                        oj = self.ops[lst[j]]
                        if oj["eng"] == eng and not oj["dma"]:
                            lst[j] = oid
                            break
                    else:
                        lst.append(oid)
                else:
                    lst.append(oid)
        if dma:
            self.dma_open.append(oid)
        return oid

    def emit(self, final_wait_ops=()):
        nc = self.nc
        ops = self.ops
        final_wait_ops = [r.id if isinstance(r, Ref) else r for r in final_wait_ops]
        for i, op in enumerate(ops):
            for d, raw in op["deps"].items():
                od = ops[d]
                if od["dma"] or op["dma"]:
                    od["needed"] = True
                elif od["eng"] != op["eng"]:
                    od["needed"] = True
                elif (raw or self.strict) and op["eng"] in ("act", "dve", "pool"):
                    od["needed"] = True
        for i in final_wait_ops:
            ops[i]["needed"] = True
        stack = contextlib.ExitStack()
        semcount = [0]

        def newsem(name):
            semcount[0] += 1
            return stack.enter_context(nc.semaphore(f"{name}_{semcount[0]}"))

        with stack:
            cur = {e: None for e in self.ENG}
            cnt = {e: 0 for e in self.ENG}
            dma_sems = {}
            dma_rr = {}
            for i, op in enumerate(ops):
                e = op["eng"]
                if op["dma"]:
                    if e not in dma_sems:
                        dma_sems[e] = [[newsem("dq" + e), 0] for _ in range(self.n_dma_sems)]
                        dma_rr[e] = 0
                    slot = dma_sems[e][dma_rr[e] % self.n_dma_sems]
                    dma_rr[e] += 1
                    prev = slot[1]
                    slot[1] += 16 * op["dma"]
                    op["sig"] = (slot[0], slot[1])
                    op["dma_prev"] = (slot[0], prev)
                elif op["needed"]:
                    if cur[e] is None or cnt[e] >= self.sem_epoch:
                        cur[e] = newsem("s" + e)
                        cnt[e] = 0
                    cnt[e] += 1
                    op["sig"] = (cur[e], cnt[e])
            self.n_sems = semcount[0]
            per_eng = {e: [i for i, op in enumerate(ops) if op["eng"] == e] for e in self.ENG}

            def run_engine(e, eng):
                seen = {}

                def wait(sem, val):
                    if val <= 0:
                        return
                    key = id(sem)
                    if seen.get(key, 0) >= val:
                        return
                    seen[key] = val
                    eng.wait_ge(sem, val)

                for i in per_eng[e]:
                    op = ops[i]
                    for d, raw in op["deps"].items():
                        od = ops[d]
                        if od["sig"] is None:
                            continue
                        same = (od["eng"] == e) and not od["dma"]
                        if same and not op["dma"]:
                            if e == "pe" or not (raw or self.strict):
                                continue
                        wait(*od["sig"])
                    if op["dma"]:
                        wait(*op["dma_prev"])
                    r = op["fn"](eng)
                    if op["dma"]:
                        sem, _ = op["sig"]
                        for ins in r:
                            ins.then_inc(sem, 16)
                    elif op["sig"] is not None:
                        r.then_inc(op["sig"][0], 1)
                if e == "sp":
                    for i in final_wait_ops:
                        wait(*ops[i]["sig"])

            with nc.Block() as block:
                block.tensor(lambda eng: run_engine("pe", eng))
                block.scalar(lambda eng: run_engine("act", eng))
                block.vector(lambda eng: run_engine("dve", eng))
                block.gpsimd(lambda eng: run_engine("pool", eng))
                block.sync(lambda eng: run_engine("sp", eng))

    def mm(self, out, lhsT, rhs, start=True, stop=True):
        return self.add("pe", lambda e: e.matmul(_ap(out), _ap(lhsT), _ap(rhs), start=start, stop=stop),
                        reads=[lhsT, rhs], writes=[out])

    def tr(self, out, in_, ident):
        return self.add("pe", lambda e: e.transpose(_ap(out), _ap(in_), _ap(ident)),
                        reads=[in_, ident], writes=[out])

    def act(self, out, in_, func, bias=None, scale=1.0):
        kw = {}
        if bias is not None:
            kw["bias"] = _ap(bias)
        sc = _ap(scale)
        return self.add("act", lambda e: e.activation(_ap(out), _ap(in_), func, scale=sc, **kw),
                        reads=[in_, bias, scale], writes=[out])

    def tt(self, out, in0, in1, op, eng="dve"):
        return self.add(eng, lambda e: e.tensor_tensor(_ap(out), _ap(in0), _ap(in1), op),
                        reads=[in0, in1], writes=[out])

    def ts(self, out, in0, s1, op0, eng="dve"):
        return self.add(eng, lambda e: e.tensor_scalar(_ap(out), _ap(in0), _ap(s1), None, op0),
                        reads=[in0, s1], writes=[out])

    def ts2(self, out, in0, s1, s2, op0, op1, eng="dve"):
        a, b, c, d = _ap(out), _ap(in0), _ap(s1), _ap(s2)
        return self.add(eng, lambda e: e.tensor_scalar(a, b, c, d, op0, op1), reads=[in0, s1, s2], writes=[out])

    def stt(self, out, in0, scalar, in1, op0, op1, eng="dve"):
        return self.add(eng, lambda e: e.scalar_tensor_tensor(_ap(out), _ap(in0), _ap(scalar), _ap(in1), op0, op1),
                        reads=[in0, scalar, in1], writes=[out])

    def copy(self, out, in_, eng="dve"):
        if eng == "act":
            return self.add(eng, lambda e: e.copy(_ap(out), _ap(in_)), reads=[in_], writes=[out])
        return self.add(eng, lambda e: e.tensor_copy(_ap(out), _ap(in_)), reads=[in_], writes=[out])

    def recip(self, out, in_):
        return self.add("dve", lambda e: e.reciprocal(_ap(out), _ap(in_)), reads=[in_], writes=[out])

    def rsum(self, out, in_, eng="dve"):
        return self.add(eng, lambda e: e.reduce_sum(_ap(out), _ap(in_), AX.X), reads=[in_], writes=[out])

    def scan(self, out, d0, d1, init, op0, op1, eng="dve"):
        return self.add(eng, lambda e: e.tensor_tensor_scan(_ap(out), _ap(d0), _ap(d1), init, op0, op1),
                        reads=[d0, d1], writes=[out])

    def memset(self, out, val, eng="dve"):
        return self.add(eng, lambda e: e.memset(_ap(out), val), reads=[], writes=[out])

    def dma(self, pairs, eng="sp", reads=(), writes=()):
        rd = list(reads) + [p[1] for p in pairs]
        wr = list(writes) + [p[0] for p in pairs]

        def fn(e):
            return [e.dma_start(out=_ap(o), in_=_ap(i)) for (o, i) in pairs]
        return self.add(eng, fn, reads=rd, writes=wr, dma=len(pairs))


class Arena:
    def __init__(self, h32, hbf, nbytes, tag):
        self.h32, self.hbf, self.nbytes, self.tag = h32, hbf, nbytes, tag
        self.off = 0
        self.offs = {}

    def alloc(self, name, shape, dt=F32):
        esz = 4 if dt == F32 else 2
        n = 1
        for s in shape[1:]:
            n *= s
        nb = (n * esz + 3) // 4 * 4
        assert self.off + nb <= self.nbytes, (self.tag, name, self.off, nb, self.nbytes)
        self.offs[name] = self.off
        if dt == F32:
            ap = self.h32[:, self.off // 4: self.off // 4 + n]
        else:
            ap = self.hbf[:, self.off // 2: self.off // 2 + n]
        self.off += nb
        return self._view(ap, shape, (self.tag, name))

    def alias(self, base, shape, dt, keys):
        off = self.offs[base]
        n = 1
        for s_ in shape[1:]:
            n *= s_
        if dt == F32:
            ap = self.h32[:, off // 4: off // 4 + n]
        else:
            ap = self.hbf[:, off // 2: off // 2 + n]
        return self._view(ap, shape, keys)

    def _view(self, ap, shape, keys):
        if len(shape) == 3:
            ap = ap.rearrange("p (a b) -> p a b", b=shape[2])
        elif len(shape) == 4:
            ap = ap.rearrange("p (a b c) -> p a b c", b=shape[2], c=shape[3])
        if shape[0] < 128:
            ap = ap[0:shape[0]]
        if isinstance(keys, list):
            return V(ap, keys)
        return V(ap, [keys, self.tag] if SER_ARENA else keys)


ARENA_BYTES = 62464 + 4096
STRICT_SAME_ENGINE = True
import os as _os
SER_ARENA = bool(int(_os.environ.get('SER_ARENA', '0')))
POOL_AS_DVE = bool(int(_os.environ.get('POOL_AS_DVE', '0')))
PSUM_EXCLUSIVE = bool(int(_os.environ.get('PSUM_EXCLUSIVE', '1')))
MERGE_W = [float(x) for x in _os.environ.get('MERGE_W', '1,1,1,1,1').split(',')]
SER_PS = bool(int(_os.environ.get('SER_PS', '0')))
DEFAULT_CFG = dict(stage=99, tmax=17, layers=2, ffn1=True, mixer=True, ffn2=True, pool=True, gla=True, hgrn=True, ssd=True)


def build(cfg=None):
    cfg = dict(DEFAULT_CFG, **(cfg or {}))
    nc = bass.Bass("TRN2", target_bir_lowering=False)
    din = {}

    def inp(name, shape):
        din[name] = nc.dram_tensor(name, list(shape), F32, kind="ExternalInput").ap()
        return din[name]

    def outp(name, shape):
        return nc.dram_tensor(name, list(shape), F32, kind="ExternalOutput").ap()

    x_prompt = inp("x_prompt", [TP, D])
    x_sample = inp("x_sample", [NSEQ * TSQ, D])
    state_pool = inp("state_pool", [DEPTH, NSEQ, 15, 256])
    state_gla = inp("state_gla", [DEPTH, NSEQ, 4, 32, 64])
    state_hgrn = inp("state_hgrn", [DEPTH, NSEQ, 4, 64, 64])
    state_ssm = inp("state_ssm", [DEPTH, NSEQ, 4, 64, 128])
    state_conv = inp("state_conv", [DEPTH, NSEQ, 3, 768])
    ffn_norm = [inp("ffn1_norm", [DEPTH, D]), inp("ffn2_norm", [DEPTH, D])]
    ffn_wg = [inp("ffn1_w_gate", [DEPTH, D, DFF]), inp("ffn2_w_gate", [DEPTH, D, DFF])]
    ffn_wu = [inp("ffn1_w_up", [DEPTH, D, DFF]), inp("ffn2_w_up", [DEPTH, D, DFF])]
    ffn_wd = [inp("ffn1_w_down", [DEPTH, DFF, D]), inp("ffn2_w_down", [DEPTH, DFF, D])]
    mix_norm = inp("mix_norm", [DEPTH, D])
    w_in = inp("w_in", [DEPTH, D, NIN])
    pool_w = inp("pool_w", [DEPTH, 4, 64, 64])
    pool_scale = inp("pool_scale", [DEPTH, 256])
    gla_w_gate = inp("gla_w_gate", [DEPTH, 16, 128])
    gla_gate_bias = inp("gla_gate_bias", [DEPTH, 128])
    gla_norm = inp("gla_norm", [DEPTH, 64])
    hgrn_lb_logits = inp("hgrn_lb_logits", [DEPTH, 256])
    hgrn_norm = inp("hgrn_norm", [DEPTH, 64])
    ssm_conv_w = inp("ssm_conv_w", [DEPTH, 4, 768])
    ssm_conv_b = inp("ssm_conv_b", [DEPTH, 768])
    ssm_dt_bias = inp("ssm_dt_bias", [DEPTH, 4])
    ssm_A_log = inp("ssm_A_log", [DEPTH, 4])
    ssm_D = inp("ssm_D", [DEPTH, 4])
    ssm_norm = inp("ssm_norm", [DEPTH, 256])
    w_out = inp("w_out", [DEPTH, D, D])
    final_norm = inp("final_norm", [D])
    c_all = inp("c_all", [128, 9, 128])

    y_p = outp("y_p", [TP, D])
    y_s = outp("y_s", [NSEQ * TSQ, D])
    pool_p = outp("pool_p", [DEPTH, 1, 15, 256])
    pool_s = outp("pool_s", [DEPTH, NSEQ, 15, 256])
    gla_p = outp("gla_p", [DEPTH, 1, 4, 32, 64])
    gla_s = outp("gla_s", [DEPTH, NSEQ, 4, 32, 64])
    hg_p = outp("hg_p", [DEPTH, 1, 4, 64, 64])
    hg_s = outp("hg_s", [DEPTH, NSEQ, 4, 64, 64])
    ssm_p = outp("ssm_p", [DEPTH, 1, 4, 64, 128])
    ssm_s = outp("ssm_s", [DEPTH, NSEQ, 4, 64, 128])
    conv_p = outp("conv_p", [DEPTH, 1, 3, 768])
    conv_s = outp("conv_s", [DEPTH, NSEQ, 3, 768])

    S = Sched(nc)
    out_ops = []
    st = contextlib.ExitStack()
    with st:
        def sbt(name, shape, dt=F32):
            return st.enter_context(nc.sbuf_tensor(name, shape, dt))

        Xh = sbt("X", [128, 8, T])
        RBh = sbt("RB", [128, 8 * NINW], BF16)
        WOUTh = sbt("WOUT", [128, 8, D], BF16)
        WQKh = sbt("WQK", [128, 8, 512], BF16)
        ARh = sbt("AR", [128, ARENA_BYTES // 4])
        CONh = sbt("CON", [128, 9, 128])
        PVh = sbt("PV", [128, 128])
        MISCh = sbt("MISC", [128, 64])
        BARh = sbt("BAR", [128, 8])
        ONESBh = sbt("ONESB", [128, 128], BF16)
        ONESFh = sbt("ONESF", [128, 128])
        PSh = [st.enter_context(nc.psum_tensor(f"ps{i}", [128, 512], F32)) for i in range(8)]

        ARbf = ARh.bitcast(BF16)

        def Xv(k0, k1, c0, n):
            keys = [("X", t) for t in range(c0 // 128, (c0 + n + 127) // 128)]
            return V(Xh[:, k0:k1, c0:c0 + n], keys)

        XNap = RBh[:, 0:8 * T].rearrange("p (k t) -> p k t", t=T)

        def XNv(k, c0, n):
            keys = [("XN", t) for t in range(c0 // 128, (c0 + n + 127) // 128)]
            return V(XNap[:, k, c0:c0 + n], keys)

        WIN = V(RBh[:, :].rearrange("p (k n) -> p k n", n=NINW), "WIN")
        WOUT = V(WOUTh[:], "WOUT")
        WQK = V(WQKh[:], "WQK")
        CON = V(CONh[:], "CON")
        IDENT = CON[:, 0, :]
        UF = CON[:, 1, :]
        NEGM = CON[:, 2, :]
        RMS = {128: CON[:, 3, :], 64: CON[:, 4, :], 8: CON[:, 5, :]}
        INVT = CON[:, 6:8, :]
        BD64 = CON[:, 8, :]
        PV = V(PVh[:], "PV")
        MISC = V(MISCh[:], "MISC")
        ONESB = V(ONESBh[:], "ONESB")
        ONESF = V(ONESFh[:], "ONESF")
        EPS6 = MISC[:, 0:1]
        ONE1 = MISC[:, 1:2]
        LBv = MISC[:, 4:8].r("p (l t) -> p l t", t=2)
        OMLv = MISC[:, 8:12].r("p (l t) -> p l t", t=2)
        NGB = MISC[:, 12:16].r("p (l t) -> p l t", t=2)
        GBraw = MISC[:, 16:20].r("p (l t) -> p l t", t=2)
        BARS = V(BARh[:], "BAR")
        PS = [V(PSh[i][:], [("ps", i), "ps"] if SER_PS else ("ps", i)) for i in range(8)]
        bank_set = [list(range(8))]
        psn = {}

        def nb():
            bs = bank_set[0]
            k = tuple(bs)
            i = psn.get(k, 0)
            psn[k] = i + 1
            return PS[bs[i % len(bs)]]

        barn = [0]

        def barrier():
            n = barn[0]
            barn[0] += 1
            marks = []
            marks.append(S.add("pe", lambda e: e.matmul(PSh[7][:, 0:1], ONESBh[:, 0:128], ONESBh[:, 0:1], start=True, stop=True),
                               reads=[ONESB], writes=[PS[7]]))
            marks.append(S.add("act", lambda e: e.copy(BARh[:, 0:1], MISCh[:, 0:1]), reads=[MISC], writes=[BARS[:, 0:1].k(("bar", "act"))]))
            marks.append(S.add("dve", lambda e: e.memset(BARh[:, 1:2], 0.0), writes=[BARS[:, 1:2].k(("bar", "dve"))]))
            marks.append(S.add("pool", lambda e: e.memset(BARh[:, 2:3], 0.0), writes=[BARS[:, 2:3].k(("bar", "pool"))]))
            ext = marks + [d for d in S.dma_open]
            S.dma_open = []
            S.add("pe", lambda e: e.matmul(PSh[7][:, 0:1], ONESBh[:, 0:128], ONESBh[:, 0:1], start=True, stop=True),
                  reads=[ONESB], writes=[PS[7]], extra=ext)
            S.add("act", lambda e: e.copy(BARh[:, 3:4], MISCh[:, 0:1]), reads=[MISC], writes=[BARS[:, 3:4].k(("bar2", "act"))], extra=ext)
            S.add("dve", lambda e: e.memset(BARh[:, 4:5], 0.0), writes=[BARS[:, 4:5].k(("bar2", "dve"))], extra=ext)
            S.add("pool", lambda e: e.memset(BARh[:, 5:6], 0.0), writes=[BARS[:, 5:6].k(("bar2", "pool"))], extra=ext)
            S.add("sp", lambda e: e.nop(), extra=ext)

        def arena(tag):
            return Arena(ARh, ARbf, ARENA_BYTES, tag)

        S.dma([(CON, c_all)])
        S.memset(MISC, 0.0)
        S.memset(EPS6, EPS)
        S.memset(ONE1, 1.0)
        S.memset(ONESB, 1.0)
        S.memset(ONESF, 1.0)
        A0 = arena("setup")
        RAW = A0.alloc("RAW", [128, 128])
        S.memset(RAW, 0.0)
        rows = []
        R_FFN = {}
        r = 0
        for l in range(DEPTH):
            for w in range(2):
                rows.append((RAW[r:r + 8, :], ffn_norm[w][l].rearrange("(k p) -> k p", p=128)))
                R_FFN[(l, w)] = r
                r += 8
        R_MIX = {}
        for l in range(DEPTH):
            rows.append((RAW[r:r + 8, :], mix_norm[l].rearrange("(k p) -> k p", p=128)))
            R_MIX[l] = r
            r += 8
        rows.append((RAW[r:r + 8, :], final_norm.rearrange("(k p) -> k p", p=128)))
        R_FIN = r
        r += 8
        R_PSC = r
        rows.append((RAW[r:r + 4, :], pool_scale.rearrange("l (t p) -> (l t) p", p=128)))
        r += 4
        R_CB = r
        rows.append((RAW[r:r + 12, :], ssm_conv_b.rearrange("l (j p) -> (l j) p", p=128)))
        r += 12
        R_CW = r
        rows.append((RAW[r:r + 48, :], ssm_conv_w.rearrange("l d (j p) -> (l d j) p", p=128)))
        r += 48
        R_LB = r
        rows.append((RAW[r:r + 4, :], hgrn_lb_logits.rearrange("l (t p) -> (l t) p", p=128)))
        r += 4
        assert r <= 128
        S.dma(rows)
        bk = nb()
        S.tr(bk[:, 0:128], RAW, IDENT)
        S.copy(PV, bk[:, 0:128])
        gbp = []
        for l in range(DEPTH):
            for h in range(4):
                pt, h2 = h // 2, h % 2
                gbp.append((GBraw[h2 * 64:h2 * 64 + 32, l, pt:pt + 1],
                            gla_gate_bias[l, h * 32:(h + 1) * 32].rearrange("(p o) -> p o", o=1)))
        S.dma(gbp)
        S.ts(NGB, GBraw, -1.0, ALU.mult)
        S.memset(LBv[:, 0, :], 0.0)
        S.memset(OMLv[:, 0, :], 1.0)
        TMPL = MISC[:, 20:22]
        S.tt(TMPL, PV[:, R_LB:R_LB + 2], PV[:, R_LB + 2:R_LB + 4], ALU.subtract)
        S.act(TMPL, TMPL, AF.Exp)
        S.ts(TMPL, TMPL, 1.0, ALU.add)
        S.recip(LBv[:, 1, :], TMPL)
        S.stt(OMLv[:, 1, :], LBv[:, 1, :], -1.0, ONE1.bc([128, 2]), ALU.mult, ALU.add)

        STG = [A0.alloc(f"STG{i}", [128, D]) for i in range(4)]
        for t in range(NT):
            src = x_prompt[t * 128:(t + 1) * 128, :] if t < 16 else x_sample[:, :]
            sg = STG[t % 4]
            S.dma([(sg, src)])
            for hb in range(2):
                bk = nb()
                for j in range(4):
                    k = hb * 4 + j
                    S.tr(bk[:, j * 128:(j + 1) * 128], sg[:, k * 128:(k + 1) * 128], IDENT)
                S.copy(Xv(hb * 4, hb * 4 + 4, t * 128, 128), bk.r("p (a b) -> p a b", b=128),
                       eng="act" if hb == 0 else "dve")
        barrier()

        TGS = [(0, 512), (512, 512), (1024, 512), (1536, 512), (2048, 128)]

        def norm_all(SQ, LNT, RSTD, grow):
            for (c0, n) in TGS:
                S.act(SQ[:, :, 0:n], Xv(0, 8, c0, n), AF.Square)
                bk = nb()
                for k in range(8):
                    S.mm(bk[:, 0:n], ONESB, SQ[:, k, 0:n], start=(k == 0), stop=(k == 7))
                S.act(LNT[:, 0:n], bk[:, 0:n], AF.Ln, bias=EPS6, scale=1.0 / D)
                S.act(RSTD[:, 0:n], LNT[:, 0:n], AF.Exp, scale=-0.5)
                for k in range(8):
                    S.stt(XNv(k, c0, n), Xv(k, k + 1, c0, n)[:, 0, :], PV[:, grow + k:grow + k + 1],
                          RSTD[:, 0:n], ALU.mult, ALU.mult)

        GRP = [(g * 512, 4) for g in range(5)] + [(2560, 2)]

        def ffn(l, w):
            A = arena(f"ffn{l}{w}")
            WG = [A.alloc(f"WG{i}", [128, 8, 512], BF16) for i in range(2)]
            WU = [A.alloc(f"WU{i}", [128, 8, 512], BF16) for i in range(2)]
            WD = [A.alloc(f"WD{i}", [128, 4, D], BF16) for i in range(2)]
            wg_d = ffn_wg[w][l].rearrange("(k p) n -> p k n", p=128)
            wu_d = ffn_wu[w][l].rearrange("(k p) n -> p k n", p=128)
            wd_d = ffn_wd[w][l]

            def load(g):
                c0, nch = GRP[g]
                s = g % 2
                ncol = nch * 128
                S.dma([(WG[s][:, :, 0:ncol], wg_d[:, :, c0:c0 + ncol]),
                       (WU[s][:, :, 0:ncol], wu_d[:, :, c0:c0 + ncol]),
                       (WD[s][:, 0:nch, :], wd_d[c0:c0 + ncol, :].rearrange("(c p) n -> p c n", p=128))],
                      eng="pool")

            HBall = A.alloc("HB", [128, 8, 512], BF16)
            HB = [HBall[:, 4 * i:4 * i + 4, :].k((A.tag, "HB%d" % i)) for i in range(2)]
            SGT = [A.alloc(f"SG{i}", [128, 512]) for i in range(2)]
            load(0)
            norm_all(HBall.k((A.tag, "HB0"), (A.tag, "HB1")), SGT[0], SGT[1], R_FFN[(l, w)])
            for g in range(len(GRP)):
                if g + 1 < len(GRP):
                    load(g + 1)
                c0g, nch = GRP[g]
                s = g % 2
                for tgi, (c0, n) in enumerate(TGS):
                    hb = HB[tgi % 2]
                    for c in range(nch):
                        pg = nb()
                        pu = nb()
                        for k in range(8):
                            S.mm(pg[:, 0:n], WG[s][:, k, c * 128:(c + 1) * 128], XNv(k, c0, n), start=(k == 0), stop=(k == 7))
                        for k in range(8):
                            S.mm(pu[:, 0:n], WU[s][:, k, c * 128:(c + 1) * 128], XNv(k, c0, n), start=(k == 0), stop=(k == 7))
                        sgt = SGT[c % 2]
                        S.act(sgt[:, 0:n], pg[:, 0:n], AF.Silu)
                        S.tt(hb[:, c, 0:n], sgt[:, 0:n], pu[:, 0:n], ALU.mult)
                    for m in range(8):
                        py = nb()
                        for c in range(nch):
                            S.mm(py[:, 0:n], WD[s][:, c, m * 128:(m + 1) * 128], hb[:, c, 0:n], start=(c == 0), stop=(c == nch - 1))
                        xv = Xv(m, m + 1, c0, n)[:, 0, :]
                        S.stt(xv, py[:, 0:n], 0.5, xv, ALU.mult, ALU.add)
            barrier()

        def mixer(l):
            A = arena(f"mix{l}")
            for k in range(8):
                S.dma([(WIN[:, k, 0:256], w_in[l, k * 128:(k + 1) * 128, 0:256]),
                       (WIN[:, k, 256:NINW], w_in[l, k * 128:(k + 1) * 128, 512:NIN])], eng="pool")
            S.memset(WQK, 0.0, eng="pool")
            for k in range(8):
                S.dma([(WQK[:, k, 0:256].r("p (h c) -> p h c", c=64)[:, :, 0:32],
                        w_in[l, k * 128:(k + 1) * 128, C_GQ:C_GQ + 128].rearrange("p (h c) -> p h c", c=32)),
                       (WQK[:, k, 256:512].r("p (h c) -> p h c", c=64)[:, :, 0:32],
                        w_in[l, k * 128:(k + 1) * 128, C_GK:C_GK + 128].rearrange("p (h c) -> p h c", c=32))],
                      eng="pool")
            S.dma([(WOUT, w_out[l].rearrange("(k p) n -> p k n", p=128))], eng="pool")
            PWBD = A.alloc("PWBD", [128, 2, 128], BF16)
            S.memset(PWBD, 0.0)
            S.dma([(PWBD[g2 * 64:(g2 + 1) * 64, pt, g2 * 64:(g2 + 1) * 64], pool_w[l, 2 * pt + g2])
                   for pt in range(2) for g2 in range(2)], eng="pool")
            GWG = A.alloc("GWG", [128, 256], BF16)
            S.memset(GWG, 0.0)
            S.dma([(GWG[0:16, :].r("p (h c) -> p h c", c=64)[:, :, 0:32],
                    gla_w_gate[l].rearrange("p (h c) -> p h c", c=32))], eng="pool")
            GN = A.alloc("GN", [128, 64])
            HN = A.alloc("HN", [128, 64])
            SN = A.alloc("SN", [128, 256])
            SM4 = A.alloc("SM4", [128, 16])
            DTB, NEGA, DSK = SM4[:, 0:4], SM4[:, 4:8], SM4[:, 8:12]
            S.dma([(GN, gla_norm[l].partition_broadcast(128)), (HN, hgrn_norm[l].partition_broadcast(128)),
                   (SN, ssm_norm[l].partition_broadcast(128)), (DTB, ssm_dt_bias[l].partition_broadcast(128)),
                   (NEGA, ssm_A_log[l].partition_broadcast(128)), (DSK, ssm_D[l].partition_broadcast(128))])
            S.act(NEGA, NEGA, AF.Exp)
            S.ts(NEGA, NEGA, -1.0, ALU.mult)
            CW = PV[:, R_CW + l * 24:R_CW + (l + 1) * 24].r("p (d j) -> p d j", j=6)
            CB = PV[:, R_CB + l * 6:R_CB + (l + 1) * 6]
            PSC = PV[:, R_PSC + l * 2:R_PSC + (l + 1) * 2]
            GROW = R_MIX[l]

            tg = A.tag
            XNTs = [A.alloc("XNT0", [128, 8, 128], BF16), A.alloc("XNT1", [128, 8, 128], BF16)]
            XNT = XNTs[0]
            NS = A.alloc("NS", [128, 4, 128])
            SQ = A.alias("NS", [128, 8, 128], BF16, (tg, "NS"))
            RSTD = A.alloc("RSTD", [128, 128])
            MIXs = [[A.alloc(f"MIX{b}{i}", [128, 2, 128], BF16) for i in range(4)] for b in range(2)]
            MIX = MIXs[0]
            PXB = A.alloc("PXB", [128, 2 * 16 * 24])
            SA = A.alloc("SA", [128, 2 * 8 * 24])
            SBb = A.alloc("SBb", [128, 2 * 8 * 24])
            DPL = A.alloc("DPL", [128, 2, 128], BF16)
            PT = A.alloc("PT", [128, 2, 128])
            P_F = [None, None, PT]
            P_TT = [SA[:, 0:256], SBb[:, 0:256], None, PT.r("p a b -> p (a b)")]
            G_F = [A.alloc("GF0", [128, 4, 128]), A.alloc("GF1", [128, 4, 128]), A.alloc("GF2", [128, 2, 128]),
                   A.alloc("GF3", [128, 4, 128]), A.alloc("GF4", [128, 2, 128])]
            G_FB = [A.alloc("GFB0", [128, 4, 128], BF16), None, A.alloc("GFB2", [128, 1, 128], BF16), A.alloc("GFB3", [128, 4, 128], BF16)]
            G_TT = [None] + [A.alloc(f"GT{i}", [128, 256]) for i in range(1, 4)]
            G_TT[0] = G_TT[3]
            G_TB = [A.alloc("GTB0", [128, 256], BF16), A.alloc("GTB1", [128, 256], BF16), A.alloc("GTB2", [128, 512], BF16),
                    A.alloc("GTB3", [128, 256], BF16)]
            GQAB = A.alloc("GQAB", [128, 4, 128], BF16)
            G_SMT = A.alloc("GSMT", [128, 64])
            XBC = A.alloc("XBC", [128, 6 * 16 * 11])
            CV = A.alloc("CV", [128, 6, 128])
            UA_ = A.alloc("UA", [128, 4, 128])
            DD_ = A.alloc("DDM", [128, 4, 128])
            S_F = [UA_, DD_, UA_, UA_, DD_]
            S_FB = [None, A.alloc("SFB1", [128, 4, 128], BF16), A.alloc("SFB2", [128, 4, 128], BF16)]
            S_TT = [A.alloc(f"ST{i}", [128, 256]) for i in range(4)]
            S_TB = [A.alloc("STB0", [128, 256], BF16), A.alloc("STB1", [128, 256], BF16), A.alloc("STB2", [128, 512], BF16),
                    A.alloc("STB3", [128, 256], BF16)]
            S_SMT = A.alloc("SSMT", [128, 64])
            UAf = UA_.r("p a b -> p (a b)")
            DDf = DD_.r("p a b -> p (a b)")
            GLS = A.alloc("GLS", [128, 2, 64])
            GLSb = A.alloc("GLSb", [128, 2, 128], BF16)
            HGS = A.alloc("HGS", [128, 2, 64])
            HGSb = A.alloc("HGSb", [128, 2, 128], BF16)
            SST = A.alloc("SST", [128, 256])
            SSTb = A.alloc("SSTb", [128, 256], BF16)
            N0 = A.alloc("N0", [128, 2, 128])
            GLS2, GLSb2 = A.alloc("GLS2", [128, 2, 64]), A.alloc("GLSb2", [128, 2, 128], BF16)
            HGS2, HGSb2 = A.alloc("HGS2", [128, 2, 64]), A.alloc("HGSb2", [128, 2, 128], BF16)
            N0s = [N0, A.alloc("N02", [128, 2, 128])]
            GLSs, GLSbs, HGSs, HGSbs = [GLS, GLS2], [GLSb, GLSb2], [HGS, HGS2], [HGSb, HGSb2]
            for z in (GLS, GLSb, HGS, HGSb, SST, SSTb, PXB, XBC, G_FB[0], GQAB, G_TB[3], GLS2, GLSb2, HGS2, HGSb2):
                S.memset(z, 0.0, eng="pool")
            F, FB, TT, TB, SMT = G_F, G_FB, G_TT, G_TB, G_SMT

            def proj_fm(dst, wv):
                for k in range(8):
                    S.mm(dst, wv[:, k, :], XNT[:, k, :], start=(k == 0), stop=(k == 7))

            def proj_tm(dst, o0, Lc, c0w, ncol):
                for k in range(8):
                    S.mm(dst, XNT[:, k, o0:o0 + Lc], WIN[:, k, c0w:c0w + ncol], start=(k == 0), stop=(k == 7))

            def silu_tm(dst, src_ps, Lc, tmp):
                S.act(tmp, src_ps, AF.Exp, scale=-1.0)
                S.act(tmp, tmp, AF.Ln, bias=ONE1[0:Lc])
                S.act(tmp, tmp, AF.Exp, scale=-1.0)
                S.tt(dst, src_ps, tmp, ALU.mult)

            def to_fm(res, Lc, o0, mc):
                bk = nb()
                for j in range(2):
                    S.tr(bk[:, j * Lc:(j + 1) * Lc], res[0:Lc, j * 128:(j + 1) * 128], IDENT[0:Lc, 0:Lc])
                S.copy(MIX[mc // 2][:, :, o0:o0 + Lc], bk[:, 0:2 * Lc].r("p (a b) -> p a b", b=Lc), eng="act")

            def gla_like(name, t, Qf, Kf, GSf, sg, c0w, Sst, Sstb, normw, mc, Lc, st_in, st_out, st_pout, Ls=None):
                is_s = (t == 16)
                Ls = Ls or Lc
                nch = 128 // Lc
                nsub = Lc // Ls
                nseg = 128 // Ls
                assert nsub in (1, 2)
                F, FB, TT, TB, SMT = G_F, G_FB, G_TT, G_TB, G_SMT
                BP, EB, ENB, DD, KH = F[0][:, 0:2, :], F[0][:, 2:4, :], F[1][:, 0:2, :], F[1][:, 2:4, :], F[2][:, 0:2, :]
                QT, KT = FB[3][:, 0:2, :], FB[3][:, 2:4, :]
                QBD = FB[0].r("p (a b) c -> p a b c", b=2)
                QTA, QTB = GQAB[:, 0:2, :], GQAB[:, 2:4, :]
                KHBB = TB[3]
                DEC = SMT[:, 0:32]
                RM = RMS[Ls]
                for pt in range(2):
                    S.scan(BP[:, pt, :], RM, GSf[:, pt, :], 0.0, ALU.mult, ALU.add)
                S.act(EB, BP, AF.Exp, scale=sg)
                S.act(ENB, BP, AF.Exp, scale=-sg)
                S.tt(QT, Qf, EB, ALU.mult)
                for h2 in range(2):
                    r0 = h2 * 64
                    S.copy(QBD[r0:r0 + 64, :, h2, :], QT[r0:r0 + 64, :, :], eng="act")
                if nsub == 2:
                    S.copy(QTA[:, :, 0:64], QT[:, :, 0:64], eng="act")
                    S.copy(QTB[:, :, 64:128], QT[:, :, 64:128], eng="act")
                S.tt(KT, Kf, ENB, ALU.mult)
                BPc = BP.r("p a (c l) -> p (a c) l", l=Ls)
                S.tt(DD.r("p a (c l) -> p (a c) l", l=Ls), BPc, BPc[:, :, Ls - 1:Ls].bc([128, 2 * nseg, Ls]), ALU.subtract)
                S.act(DD, DD, AF.Exp, scale=-sg)
                S.tt(KH, Kf, DD, ALU.mult)
                S.act(DEC[:, 0:2 * nseg].us(2), BPc[:, :, Ls - 1:Ls], AF.Exp, scale=sg)
                Sst_l, Sstb_l = Sst, Sstb
                for ci in range(nch):
                    o0 = ci * Lc
                    if isinstance(Sst_l, list):
                        Sst, Sstb = Sst_l[ci % 2], Sstb_l[ci % 2]
                    if is_s:
                        if ci == 0:
                            st_in(0)
                        if ci + 1 < nch:
                            st_in(ci + 1)
                    VB, KHB, ATT = TB[0][0:Lc, 0:256], TB[1][0:Lc, 0:256], TB[2][0:Lc, :]
                    EG, SR, OS, SQO = TT[0][0:Lc, 0:256], TT[1][0:Lc, 0:256], TT[2][0:Lc, 0:256], TT[3][0:Lc, 0:256]
                    pa = nb()
                    proj_tm(pa[0:Lc, 0:512], o0, Lc, c0w, 512)
                    S.copy(VB, pa[0:Lc, 0:256], eng="act")
                    silu_tm(SR, pa[0:Lc, 256:512], Lc, EG)
                    pb = nb()
                    for pt in range(2):
                        S.tr(pb[0:Lc, pt * 128:(pt + 1) * 128], KH[:, pt, o0:o0 + Lc], IDENT)
                    S.copy(KHB, pb[0:Lc, 0:256], eng="act")
                    if nsub == 2:
                        S.copy(KHBB[64:128, :], pb[64:128, 0:256], eng="act")
                    pc = nb()
                    for pt in range(2):
                        S.mm(pc[0:Lc, pt * 2 * Lc:(pt + 1) * 2 * Lc].r("p (a b) -> p a b", b=Lc), KT[:, pt, o0:o0 + Lc], QBD[:, pt, :, o0:o0 + Lc])
                    ATT3 = ATT[:, 0:4 * Lc].r("p (h l) -> p h l", l=Lc)
                    msk = UF[0:Lc, 0:Lc] if nsub == 1 else BD64
                    S.tt(ATT3, pc[0:Lc, 0:4 * Lc].r("p (h l) -> p h l", l=Lc), msk.us(1).bc([Lc, 4, Lc]), ALU.mult)
                    pd = nb()
                    for h in range(4):
                        S.mm(pd[0:Lc, h * 64:(h + 1) * 64], ATT3[:, h, :], VB[:, h * 64:(h + 1) * 64])
                    S.copy(OS, pd[0:Lc, 0:256], eng="act")
                    for si in range(nsub):
                        seg = ci * nsub + si
                        qm = QT if nsub == 1 else (QTA, QTB)[si]
                        pd2 = nb()
                        for pt in range(2):
                            S.mm(pd2[0:Lc, pt * 128:(pt + 1) * 128], qm[:, pt, o0:o0 + Lc], Sstb[:, pt, :])
                        S.tt(OS, OS, pd2[0:Lc, 0:256], ALU.add)
                        pe_ = nb()
                        for pt in range(2):
                            if nsub == 1:
                                kk, vv = KHB[:, pt * 128:(pt + 1) * 128], VB[:, pt * 128:(pt + 1) * 128]
                            elif si == 0:
                                kk, vv = KHB[0:64, pt * 128:(pt + 1) * 128], VB[0:64, pt * 128:(pt + 1) * 128]
                            else:
                                kk, vv = KHBB[:, pt * 128:(pt + 1) * 128], VB[:, pt * 128:(pt + 1) * 128]
                            S.mm(pe_[:, pt * 128:(pt + 1) * 128], kk, vv)
                        for pt in range(2):
                            for h2 in range(2):
                                r0 = h2 * 64
                                sv = Sst[r0:r0 + 64, pt, :]
                                S.stt(sv, sv, DEC[r0:r0 + 64, pt * nseg + seg:pt * nseg + seg + 1],
                                      pe_[r0:r0 + 64, pt * 128 + r0:pt * 128 + r0 + 64], ALU.mult, ALU.add)
                        if is_s:
                            st_out(ci)
                        else:
                            for h2 in range(2):
                                r0 = h2 * 64
                                S.copy(Sstb[r0:r0 + 64, :, r0:r0 + 64], Sst[r0:r0 + 64, :, :], eng="act")
                            if t == 15 and ci == nch - 1 and si == nsub - 1:
                                st_pout()
                    S.tt(SQO, OS, OS, ALU.mult)
                    SS = SMT[0:Lc, 32:36]
                    S.rsum(SS, SQO.r("p (h v) -> p h v", v=64))
                    S.act(SS, SS, AF.Ln, bias=EPS6[0:Lc], scale=1.0 / 64)
                    S.act(SS, SS, AF.Exp, scale=-0.5)
                    OS3 = OS.r("p (h v) -> p h v", v=64)
                    S.tt(OS3, OS3, SS.us(2).bc([Lc, 4, 64]), ALU.mult)
                    S.tt(OS3, OS3, normw[0:Lc, :].us(1).bc([Lc, 4, 64]), ALU.mult)
                    S.tt(SQO, OS, SR, ALU.mult)
                    to_fm(SQO, Lc, o0, mc)

            def ssd(t, Lc):
                is_s = (t == 16)
                nch = 128 // Lc
                F, FB, TT, TB, SMT = S_F, S_FB, S_TT, S_TB, S_SMT
                BCB = FB[1]
                S.copy(BCB, CV[:, 2:6, :], eng="act")
                for ci in range(nch):
                    o0 = ci * Lc
                    sl = slice(o0, o0 + Lc)
                    SM = SMT[0:Lc, 36:52]
                    DT, AT, CUMT, WE = SM[:, 0:4], SM[:, 4:8], SM[:, 8:12], SM[:, 12:16]
                    DECE = SMT[:, 52:56].k(("mixs", "DECE"))
                    DCOL = SMT[:, 56:58].k(("mixs", "DCOL"))
                    UA, DDm, EC = F[3], F[4], F[2]
                    ATT, XDT, XDTW, BTB = TB[2][0:Lc, :], TB[0][0:Lc, 0:256], TB[1][0:Lc, 0:256], TB[3][0:Lc, 0:256]
                    CT = FB[2]
                    XST, EG, SR, Y1 = TT[0][0:Lc, 0:256], TT[1][0:Lc, 0:256], TT[2][0:Lc, 0:256], TT[3][0:Lc, 0:256]
                    N0 = N0s[ci % 2] if is_s else N0s[0]
                    if is_s:
                        if ci == 0:
                            S.dma([(N0s[0], state_ssm[l, 0].rearrange("(g h2) p n -> (h2 p) g n", g=2))])
                        if ci + 1 < nch:
                            S.dma([(N0s[(ci + 1) % 2], state_ssm[l, ci + 1].rearrange("(g h2) p n -> (h2 p) g n", g=2))])
                        pq = nb()
                        for g in range(2):
                            S.tr(pq[:, g * 128:(g + 1) * 128], N0[:, g, :], IDENT)
                        S.copy(SSTb, pq[:, 0:256], eng="act")
                    pa = nb()
                    proj_tm(pa[0:Lc, 0:256], o0, Lc, C_SZ, 256)
                    proj_tm(pa[0:Lc, 256:260], o0, Lc, C_DT, 4)
                    S.tt(DT, pa[0:Lc, 256:260], DTB[0:Lc, :], ALU.add)
                    S.act(DT, DT, AF.Exp)
                    S.act(DT, DT, AF.Ln, bias=ONE1[0:Lc])
                    S.tt(AT, DT, NEGA[0:Lc, :], ALU.mult)
                    silu_tm(SR, pa[0:Lc, 0:256], Lc, EG)
                    pb = nb()
                    S.mm(pb[0:Lc, 0:4], UF[0:Lc, 0:Lc], AT)
                    S.copy(CUMT, pb[0:Lc, 0:4])
                    UA3 = UA[0:Lc, :, 0:Lc]
                    S.tt(UA3, UF[0:Lc, 0:Lc].us(1).bc([Lc, 4, Lc]), AT.us(2).bc([Lc, 4, Lc]), ALU.mult)
                    pc = nb()
                    for h in range(4):
                        S.mm(pc[:, h * Lc:(h + 1) * Lc], ONESF[0:Lc, :], UA[0:Lc, h, 0:Lc])
                    pc3 = pc[:, 0:4 * Lc].r("p (h l) -> p h l", l=Lc)
                    for h in range(4):
                        S.stt(DDm[0:Lc, h, 0:Lc], pc3[0:Lc, h, :], CUMT[:, h:h + 1], NEGM[0:Lc, 0:Lc], ALU.subtract, ALU.add)
                    S.act(DDm[0:Lc, :, 0:Lc], DDm[0:Lc, :, 0:Lc], AF.Exp)
                    pd = nb()
                    for g in range(2):
                        S.mm(pd[0:Lc, g * Lc:(g + 1) * Lc], BCB[:, g, sl], BCB[:, 2 + g, sl])
                    ATT3 = ATT[:, 0:4 * Lc].r("p (h l) -> p h l", l=Lc)
                    for g in range(2):
                        S.tt(ATT3[:, 2 * g:2 * g + 2, :], DDm[0:Lc, 2 * g:2 * g + 2, 0:Lc],
                             pd[0:Lc, g * Lc:(g + 1) * Lc].us(1).bc([Lc, 2, Lc]), ALU.mult)
                    pe_ = nb()
                    for j in range(2):
                        S.tr(pe_[0:Lc, j * 128:(j + 1) * 128], CV[:, j, sl], IDENT)
                    for g in range(2):
                        S.tr(pe_[0:Lc, 256 + g * 128:256 + (g + 1) * 128], CV[:, 2 + g, sl], IDENT)
                    S.copy(XST, pe_[0:Lc, 0:256], eng="act")
                    S.copy(BTB, pe_[0:Lc, 256:512], eng="act")
                    XDT3 = XDT.r("p (h v) -> p h v", v=64)
                    S.tt(XDT3, XST.r("p (h v) -> p h v", v=64), DT.us(2).bc([Lc, 4, 64]), ALU.mult)
                    S.tt(WE, pc3[0:Lc, :, Lc - 1], CUMT, ALU.subtract)
                    S.act(WE, WE, AF.Exp)
                    S.tt(XDTW.r("p (h v) -> p h v", v=64), XDT3, WE.us(2).bc([Lc, 4, 64]), ALU.mult)
                    S.act(DECE, pc3[:, :, Lc - 1], AF.Exp)
                    S.act(EC[:, :, 0:Lc], pc3, AF.Exp)
                    for g in range(2):
                        S.tt(CT[:, 2 * g:2 * g + 2, 0:Lc], EC[:, 2 * g:2 * g + 2, 0:Lc],
                             BCB[:, 2 + g, sl].us(1).bc([128, 2, Lc]), ALU.mult)
                    pf = nb()
                    for h in range(4):
                        S.mm(pf[0:Lc, h * 64:(h + 1) * 64], ATT3[:, h, :], XDT[:, h * 64:(h + 1) * 64], start=True, stop=False)
                        S.mm(pf[0:Lc, h * 64:(h + 1) * 64], CT[:, h, 0:Lc], SSTb[:, h * 64:(h + 1) * 64], start=False, stop=True)
                    pg = nb()
                    if not is_s:
                        for g in range(2):
                            S.mm(pg[:, g * 128:(g + 1) * 128], BTB[:, g * 128:(g + 1) * 128], XDTW[:, g * 128:(g + 1) * 128])
                        SST3 = SST.r("p (h v) -> p h v", v=64)
                        S.tt(SST3, SST3, DECE.us(2).bc([128, 4, 64]), ALU.mult)
                        S.tt(SST, SST, pg[:, 0:256], ALU.add)
                        S.copy(SSTb, SST, eng="act")
                        if t == 15:
                            pq = nb()
                            for g in range(2):
                                S.tr(pq[:, g * 128:(g + 1) * 128], SST[:, g * 128:(g + 1) * 128], IDENT)
                            S.copy(N0, pq[:, 0:256].r("p (g n) -> p g n", n=128))
                            out_ops.append(S.dma([(ssm_p[l, 0].rearrange("(g h2) p n -> (h2 p) g n", g=2), N0)]))
                    else:
                        for g in range(2):
                            S.mm(pg[:, g * 128:(g + 1) * 128], XDTW[:, g * 128:(g + 1) * 128], BTB[:, g * 128:(g + 1) * 128])
                        DE2 = DECE.r("p (g h) -> p g h", h=2)
                        S.copy(DCOL[0:64, :], DE2[0:64, :, 0])
                        S.copy(DCOL[64:128, :], DE2[64:128, :, 1])
                        for g in range(2):
                            S.stt(N0[:, g, :], N0[:, g, :], DCOL[:, g:g + 1], pg[:, g * 128:(g + 1) * 128], ALU.mult, ALU.add)
                        out_ops.append(S.dma([(ssm_s[l, ci].rearrange("(g h2) p n -> (h2 p) g n", g=2), N0)]))
                    Y13 = Y1.r("p (h v) -> p h v", v=64)
                    S.tt(Y13, XST.r("p (h v) -> p h v", v=64), DSK[0:Lc, :].us(2).bc([Lc, 4, 64]), ALU.mult)
                    S.tt(Y1, Y1, pf[0:Lc, 0:256], ALU.add)
                    S.tt(Y1, Y1, SR, ALU.mult)
                    S.tt(EG, Y1, Y1, ALU.mult)
                    SS = SMT[0:Lc, 32:34]
                    S.rsum(SS, EG.r("p (g v) -> p g v", v=128))
                    S.act(SS, SS, AF.Ln, bias=EPS6[0:Lc], scale=1.0 / 128)
                    S.act(SS, SS, AF.Exp, scale=-0.5)
                    Y1g = Y1.r("p (g v) -> p g v", v=128)
                    S.tt(Y1g, Y1g, SS.us(2).bc([Lc, 2, 128]), ALU.mult)
                    S.tt(SR, Y1, SN[0:Lc, :], ALU.mult)
                    to_fm(SR, Lc, o0, 6)

            def tile_norm(tt_):
                cc = tt_ * 128
                xo = XNTs[tt_ % 2]
                S.act(SQ, Xv(0, 8, cc, 128), AF.Square)
                bk = nb()
                for k in range(8):
                    S.mm(bk[:, 0:128], ONESB, SQ[:, k, :], start=(k == 0), stop=(k == 7))
                S.act(RSTD, bk[:, 0:128], AF.Ln, bias=EPS6, scale=1.0 / D)
                S.act(RSTD, RSTD, AF.Exp, scale=-0.5)
                for hb in range(2):
                    S.tt(NS, Xv(hb * 4, hb * 4 + 4, cc, 128), PV[:, GROW + hb * 4:GROW + hb * 4 + 4].us(2).bc([128, 4, 128]), ALU.mult)
                    S.tt(xo[:, hb * 4:hb * 4 + 4, :], NS, RSTD.us(1).bc([128, 4, 128]), ALU.mult)

            ntile = min(NT, cfg['tmax'])
            pend_W = [[]]
            tile_norm(0)
            for t in range(ntile):
                is_s = (t == 16)
                c0 = t * 128
                XNT = XNTs[t % 2]
                MIX = MIXs[t % 2]

                if is_s:
                    PX5 = PXB.r("p (h a s c) -> p h a s c", h=2, a=2, s=8, c=24)
                    XB4 = XBC.r("p (j s c) -> p j s c", s=16, c=11)
                else:
                    PX3 = PXB[:, 0:288].r("p (a c) -> p a c", c=144)
                    XB3 = XBC[:, 0:786].r("p (j c) -> p j c", c=131)
                GQ, GK, GSP = G_F[3][:, 0:2, :], G_F[3][:, 2:4, :], G_F[4][:, 0:2, :]
                HQ, HK, HS = G_F[3][:, 0:2, :], G_F[3][:, 2:4, :], G_F[4][:, 0:2, :]
                st_P = S.stream()
                lst_P = st_P.__enter__()
                bank_set[0] = [6]
                F, FB, TT, TB, SMT = P_F, None, P_TT, None, None
                if is_s and cfg["pool"]:
                    for hh in range(2):
                        hp = TT[hh][0:120, 0:256]
                        S.dma([(hp, state_pool[l, hh * 8:(hh + 1) * 8].rearrange("s r c -> (s r) c"))])
                        bk = nb()
                        for pt in range(2):
                            S.tr(bk[:, pt * 120:(pt + 1) * 120], hp[:, pt * 128:(pt + 1) * 128], IDENT[0:120, 0:120])
                        for pt in range(2):
                            S.copy(PX5[:, hh, pt, :, 1:16],
                                   bk[:, pt * 120:(pt + 1) * 120].r("p (s r) -> p s r", r=15))
                p1 = nb()
                if cfg["pool"]:
                    for j in range(2):
                        proj_fm(p1[:, j * 128:(j + 1) * 128], WIN[:, :, C_PX + j * 128:C_PX + (j + 1) * 128])
                    if is_s:
                        for pt in range(2):
                            for hh in range(2):
                                S.copy(PX5[:, hh, pt, :, 16:24],
                                       p1[:, pt * 128 + hh * 64:pt * 128 + (hh + 1) * 64].r("p (s c) -> p s c", c=8), eng="act")
                    else:
                        S.copy(PX3[:, :, 16:144], p1[:, 0:256].r("p (a c) -> p a c", c=128), eng="act")

                if cfg["pool"]:
                    for hh in ((0, 1) if is_s else (None,)):
                        if is_s:
                            NSq, LEN = 8, 24
                            Xp = PXB[:, hh * 384:(hh + 1) * 384].r("p (a c) -> p a c", c=LEN)
                            dcol = slice(hh * 64, (hh + 1) * 64)
                        else:
                            NSq, LEN = 1, 144
                            Xp = PXB[:, 0:288].r("p (a c) -> p a c", c=LEN)
                            dcol = slice(0, 128)
                        nel = 2 * NSq * LEN
                        Sa = SA[:, 0:nel].r("p (a c) -> p a c", c=LEN)
                        Sb_ = SBb[:, 0:nel].r("p (a c) -> p a c", c=LEN)
                        n = LEN - 16

                        def dgrp(Sw, gi):
                            pt, g2 = gi // 2, gi % 2
                            r0 = g2 * 64
                            w = (2, 4, 8, 16)[gi]
                            src = Sw[r0:r0 + 64, pt * NSq:(pt + 1) * NSq, 16:LEN]
                            xx = Xp[r0:r0 + 64, pt * NSq:(pt + 1) * NSq, 16:LEN]
                            dst = DPL[r0:r0 + 64, pt, dcol].r("p (s c) -> p s c", c=n)
                            tmp = PT[r0:r0 + 64, pt, dcol].r("p (s c) -> p s c", c=n)
                            if t == 0:
                                S.tt(tmp, src, INVT[r0:r0 + 64, pt, :].r("p (s c) -> p s c", c=n), ALU.mult, eng="pool")
                                S.tt(dst, tmp, xx, ALU.subtract, eng="pool")
                            else:
                                S.stt(dst, src, 1.0 / w, xx, ALU.mult, ALU.subtract)

                        S.tt(Sa[:, :, 1:LEN], Xp[:, :, 1:LEN], Xp[:, :, 0:LEN - 1], ALU.add, eng="pool")
                        dgrp(Sa, 0)
                        S.tt(Sb_[:, :, 3:LEN], Sa[:, :, 3:LEN], Sa[:, :, 1:LEN - 2], ALU.add, eng="pool")
                        dgrp(Sb_, 1)
                        S.tt(Sa[:, :, 7:LEN], Sb_[:, :, 7:LEN], Sb_[:, :, 3:LEN - 4], ALU.add, eng="pool")
                        dgrp(Sa, 2)
                        S.tt(Sb_[:, :, 15:LEN], Sa[:, :, 15:LEN], Sa[:, :, 7:LEN - 8], ALU.add, eng="pool")
                        dgrp(Sb_, 3)
                    bk = nb()
                    for pt in range(2):
                        S.mm(bk[:, pt * 128:(pt + 1) * 128], PWBD[:, pt, :], DPL[:, pt, :])
                    S.tt(MIX[0], bk[:, 0:256].r("p (a c) -> p a c", c=128), PSC.us(2).bc([128, 2, 128]), ALU.mult)
                    if not is_s:
                        S.copy(PX3[:, :, 0:16], PX3[:, :, 128:144], eng="pool")
                    if t >= 15:
                        bk = nb()
                        proj_tm(bk[:, 0:256], 0, 128, C_PX, 256)
                        PXT = TT[3][:, 0:256]
                        S.copy(PXT, bk[:, 0:256], eng="act")
                        if t == 15:
                            out_ops.append(S.dma([(pool_p[l, 0], PXT[113:128, :])]))
                        else:
                            out_ops.append(S.dma([(pool_s[l, s, 7:15, :], PXT[8 * s:8 * s + 8, :]) for s in range(NSEQ)]))
                            out_ops.append(S.dma([(pool_s[l, :, 0:7, :], state_pool[l, :, 8:15, :])]))
                else:
                    S.memset(MIX[0], 0.0)
                st_P.__exit__(None, None, None)

                st_G = S.stream()
                lst_G = st_G.__enter__()
                bank_set[0] = [0, 1]
                F, FB, TT, TB, SMT = G_F, G_FB, G_TT, G_TB, G_SMT
                if cfg["gla"]:
                    p1g = nb()
                    for j in range(2):
                        proj_fm(p1g[:, j * 128:(j + 1) * 128], WQK[:, :, j * 128:(j + 1) * 128])
                    S.act(GQ, p1g[:, 0:256].r("p (a c) -> p a c", c=128), AF.Identity, scale=32.0 ** -0.5)
                    p2 = nb()
                    for j in range(2):
                        proj_fm(p2[:, j * 128:(j + 1) * 128], WQK[:, :, 256 + j * 128:256 + (j + 1) * 128])
                    proj_fm(p2[:, 256:384], WIN[:, :, C_GLR:C_GLR + 128])
                    S.copy(GK, p2[:, 0:256].r("p (a c) -> p a c", c=128), eng="act")
                    GLR = G_FB[2][0:16, 0, :]
                    S.copy(GLR, p2[0:16, 256:384])
                    p3 = nb()
                    for pt in range(2):
                        S.mm(p3[:, pt * 128:(pt + 1) * 128], GWG[0:16, pt * 128:(pt + 1) * 128], GLR)
                    for pt in range(2):
                        S.act(GSP[:, pt, :], p3[:, pt * 128:(pt + 1) * 128], AF.Exp, bias=NGB[:, l, pt:pt + 1], scale=-1.0)
                    S.act(GSP, GSP, AF.Ln, bias=ONE1)


                    def gl_in(s):
                        gs_, gb_ = GLSs[s % 2], GLSbs[s % 2]
                        S.dma([(gs_[h2 * 64:h2 * 64 + 32, :, :], state_gla[l, s, h2::2].rearrange("t k v -> k t v")) for h2 in range(2)])
                        for h2 in range(2):
                            S.copy(gb_[h2 * 64:h2 * 64 + 64, :, h2 * 64:h2 * 64 + 64], gs_[h2 * 64:h2 * 64 + 64, :, :], eng="act")

                    def gl_out(s):
                        out_ops.append(S.dma([(gla_s[l, s, h2::2].rearrange("t k v -> k t v"), GLSs[s % 2][h2 * 64:h2 * 64 + 32, :, :]) for h2 in range(2)]))

                    def gl_pout():
                        out_ops.append(S.dma([(gla_p[l, 0, h2::2].rearrange("t k v -> k t v"), GLS[h2 * 64:h2 * 64 + 32, :, :]) for h2 in range(2)]))

                    gla_like("gla", t, GQ, GK, GSP, -1.0 / 16, C_GV, GLSs if is_s else GLS, GLSbs if is_s else GLSb, GN, 2,
                             8 if is_s else 128, gl_in, gl_out, gl_pout)
                else:
                    S.memset(MIX[1], 0.0)

                if cfg["hgrn"]:
                    p4 = nb()
                    for j in range(4):
                        proj_fm(p4[:, j * 128:(j + 1) * 128], WIN[:, :, C_RQ + j * 128:C_RQ + (j + 1) * 128])
                    p43 = p4.r("p (a c) -> p a c", c=128)
                    E4, R4 = F[0], F[1]
                    S.act(E4, p43, AF.Exp, scale=-1.0)
                    S.act(R4, E4, AF.Ln, bias=ONE1)
                    S.act(R4, R4, AF.Exp, scale=-1.0)
                    S.tt(HQ, p43[:, 0:2, :], R4[:, 0:2, :], ALU.mult)
                    for pt in range(2):
                        S.act(HS[:, pt, :], R4[:, 2 + pt, :], AF.Ln, bias=LBv[:, l, pt:pt + 1], scale=OMLv[:, l, pt:pt + 1])
                        S.stt(HK[:, pt, :], E4[:, 2 + pt, :], OMLv[:, l, pt:pt + 1], R4[:, 2 + pt, :], ALU.mult, ALU.mult)

                    def hg_in(s):
                        hs_, hb_ = HGSs[s % 2], HGSbs[s % 2]
                        S.dma([(hs_, state_hgrn[l, s].rearrange("(t h2) k v -> (h2 k) t v", t=2))])
                        for h2 in range(2):
                            S.copy(hb_[h2 * 64:h2 * 64 + 64, :, h2 * 64:h2 * 64 + 64], hs_[h2 * 64:h2 * 64 + 64, :, :], eng="act")

                    def hg_out(s):
                        out_ops.append(S.dma([(hg_s[l, s].rearrange("(t h2) k v -> (h2 k) t v", t=2), HGSs[s % 2])]))

                    def hg_pout():
                        out_ops.append(S.dma([(hg_p[l, 0].rearrange("(t h2) k v -> (h2 k) t v", t=2), HGS)]))

                    gla_like("hgrn", t, HQ, HK, HS, 1.0, C_RI, HGSs if is_s else HGS, HGSbs if is_s else HGSb, HN, 4, 8 if is_s else 128, hg_in, hg_out, hg_pout,
                             Ls=(8 if is_s else 64))
                else:
                    S.memset(MIX[2], 0.0)
                st_G.__exit__(None, None, None)

                st_S = S.stream()
                lst_S = st_S.__enter__()
                bank_set[0] = [3, 4, 5]
                F, FB, TT, TB, SMT = S_F, S_FB, S_TT, S_TB, S_SMT
                if is_s and cfg["ssd"]:
                    HCa, HCb = UAf[0:48, 0:512], DDf[0:48, 0:256]
                    scv = state_conv[l].rearrange("s r c -> (s r) c")
                    S.dma([(HCa, scv[:, 0:512]), (HCb, scv[:, 512:768])])
                    for half in range(2):
                        bk = nb()
                        for jj in range(3):
                            j = half * 3 + jj
                            hsrc = HCa[:, j * 128:(j + 1) * 128] if j < 4 else HCb[:, (j - 4) * 128:(j - 3) * 128]
                            S.tr(bk[:, jj * 48:(jj + 1) * 48], hsrc, IDENT[0:48, 0:48])
                        for jj in range(3):
                            j = half * 3 + jj
                            S.copy(XB4[:, j, :, 0:3], bk[:, jj * 48:(jj + 1) * 48].r("p (s r) -> p s r", r=3))

                if cfg["ssd"]:
                    p5, p6 = nb(), nb()
                    for j in range(6):
                        dstb = p5[:, j * 128:(j + 1) * 128] if j < 4 else p6[:, (j - 4) * 128:(j - 3) * 128]
                        proj_fm(dstb, WIN[:, :, C_XBC + j * 128:C_XBC + (j + 1) * 128])
                    if is_s:
                        for j in range(6):
                            srcb = p5[:, j * 128:(j + 1) * 128] if j < 4 else p6[:, (j - 4) * 128:(j - 3) * 128]
                            S.copy(XB4[:, j, :, 3:11], srcb.r("p (s c) -> p s c", c=8), eng=("act" if j % 2 else "dve"))
                        NSq, LEN = 16, 11
                    else:
                        S.copy(XB3[:, 0:4, 3:131], p5.r("p (a c) -> p a c", c=128), eng="act")
                        S.copy(XB3[:, 4:6, 3:131], p6[:, 0:256].r("p (a c) -> p a c", c=128))
                        NSq, LEN = 1, 131
                    n = LEN - 3
                    for j in range(6):
                        xb = XBC[:, j * NSq * LEN:(j + 1) * NSq * LEN].r("p (s c) -> p s c", c=LEN)
                        cv = CV[:, j, :].r("p (s c) -> p s c", c=n)
                        S.ts2(cv, xb[:, :, 0:n], CW[:, 0, j:j + 1], CB[:, j:j + 1], ALU.mult, ALU.add)
                        for d in range(1, 4):
                            S.stt(cv, xb[:, :, d:d + n], CW[:, d, j:j + 1], cv, ALU.mult, ALU.add)
                    EA, EBf = UAf, DDf
                    CVf = CV.r("p a b -> p (a b)")
                    for hh, Eh in enumerate((EA, EBf)):
                        cvh = CVf[:, hh * 384:(hh + 1) * 384]
                        eh = Eh[:, 0:384]
                        S.act(eh, cvh, AF.Exp, scale=-1.0)
                        S.act(eh, eh, AF.Ln, bias=ONE1)
                        S.act(eh, eh, AF.Exp, scale=-1.0)
                        S.tt(cvh, cvh, eh, ALU.mult)
                    if not is_s:
                        S.copy(XB3[:, :, 0:3], XB3[:, :, 128:131], eng="pool")
                    if t >= 15:
                        bk1, bk2 = nb(), nb()
                        proj_tm(bk1[:, 0:512], 0, 128, C_XBC, 512)
                        proj_tm(bk2[:, 0:256], 0, 128, C_XBC + 512, 256)
                        XBTa, XBTb = UAf, DDf[:, 0:256]
                        S.copy(XBTa, bk1, eng="act")
                        S.copy(XBTb, bk2[:, 0:256], eng="act")
                        if t == 15:
                            out_ops.append(S.dma([(conv_p[l, 0, :, 0:512], XBTa[125:128, :]), (conv_p[l, 0, :, 512:768], XBTb[125:128, :])]))
                        else:
                            out_ops.append(S.dma([(conv_s[l, s, :, 0:512], XBTa[8 * s + 5:8 * s + 8, :]) for s in range(NSEQ)]))
                            out_ops.append(S.dma([(conv_s[l, s, :, 512:768], XBTb[8 * s + 5:8 * s + 8, :]) for s in range(NSEQ)]))
                    ssd(t, 8 if is_s else 128)
                else:
                    S.memset(MIX[3], 0.0)
                st_S.__exit__(None, None, None)
                lst_N = []
                if t + 1 < ntile:
                    with S.stream() as lst_N:
                        bank_set[0] = [2]
                        tile_norm(t + 1)

                lst_W = pend_W[0]
                pend_W[0] = []
                bank_set[0] = list(range(8))
                S.merge([lst_P, lst_G, lst_S, lst_N, lst_W], weights=MERGE_W)
                with S.stream() as lw:
                    bank_set[0] = [7]
                    mixt = MIXs[t % 2]
                    for hb in range(2):
                        bk = nb()
                        for j in range(4):
                            m = hb * 4 + j
                            for c in range(8):
                                S.mm(bk[:, j * 128:(j + 1) * 128], WOUT[:, c, m * 128:(m + 1) * 128], mixt[c // 2][:, c % 2, :], start=(c == 0), stop=(c == 7))
                        xv = Xv(hb * 4, hb * 4 + 4, c0, 128)
                        S.tt(xv, xv, bk.r("p (a c) -> p a c", c=128), ALU.add)
                pend_W[0] = lw
                bank_set[0] = list(range(8))
            S.merge([pend_W[0]])
            barrier()

        for l in range(cfg["layers"]):
            if cfg["ffn1"]:
                ffn(l, 0)
            if cfg["mixer"]:
                mixer(l)
            if cfg["ffn2"]:
                ffn(l, 1)

        A = arena("fin")
        NFS = 3
        FSQ = [A.alloc(f"SQ{i}", [128, 8, 128], BF16) for i in range(NFS)]
        FRS = [A.alloc(f"RSTD{i}", [128, 128]) for i in range(NFS)]
        FXG = [[A.alloc(f"XG{i}{h}", [128, 4, 128]) for h in range(2)] for i in range(NFS)]
        FYO = [A.alloc(f"YO{i}", [128, D]) for i in range(NFS)]
        fstreams = []
        for i in range(NFS):
            with S.stream() as fl:
                bank_set[0] = [[0, 1], [2, 3], [4, 5]][i]
                SQ, RSTD, XG, yo = FSQ[i], FRS[i], FXG[i], FYO[i]
                for t in range(i, NT, NFS):
                    c0 = t * 128
                    S.act(SQ, Xv(0, 8, c0, 128), AF.Square)
                    bk = nb()
                    for k in range(8):
                        S.mm(bk[:, 0:128], ONESB, SQ[:, k, :], start=(k == 0), stop=(k == 7))
                    S.act(RSTD, bk[:, 0:128], AF.Ln, bias=EPS6, scale=1.0 / D)
                    S.act(RSTD, RSTD, AF.Exp, scale=-0.5)
                    for hb in range(2):
                        S.tt(XG[hb], Xv(hb * 4, hb * 4 + 4, c0, 128), PV[:, R_FIN + hb * 4:R_FIN + hb * 4 + 4].us(2).bc([128, 4, 128]), ALU.mult)
                        S.tt(XG[hb], XG[hb], RSTD.us(1).bc([128, 4, 128]), ALU.mult)
                        bk = nb()
                        for j in range(4):
                            S.tr(bk[:, j * 128:(j + 1) * 128], XG[hb][:, j, :], IDENT)
                        S.copy(yo[:, hb * 512:(hb + 1) * 512], bk, eng="act" if hb == 0 else "dve")
                    dst = y_p[t * 128:(t + 1) * 128, :] if t < 16 else y_s[:, :]
                    out_ops.append(S.dma([(dst, yo)]))
            fstreams.append(fl)
        bank_set[0] = list(range(8))
        S.merge(fstreams)

        S.emit(final_wait_ops=out_ops)
    return nc


def _consts():
    c = np.zeros((128, 9, 128), np.float32)
    c[:, 0, :] = np.eye(128, dtype=np.float32)
    j = np.arange(128)[:, None]
    i = np.arange(128)[None, :]
    c[:, 1, :] = (j <= i).astype(np.float32)
    c[:, 2, :] = np.where(j <= i, 0.0, -30000.0).astype(np.float32)
    for idx, L in ((3, 128), (4, 64), (5, 8)):
        c[:, idx, :] = (np.arange(128) % L != 0).astype(np.float32)[None, :]
    c[:, 8, :] = ((j <= i) & ((j // 64) == (i // 64))).astype(np.float32)
    tpos = np.arange(128)[None, :] + 1.0
    p = np.arange(128)[:, None]
    for pt in range(2):
        w = np.where(p < 64, (2, 8)[pt], (4, 16)[pt]).astype(np.float32)
        c[:, 6 + pt, :] = 1.0 / np.minimum(tpos, w)
    return c


_NC_CACHE = {}


def kernel(**inputs):
    if "nc" not in _NC_CACHE:
        _NC_CACHE["nc"] = build()
    nc = _NC_CACHE["nc"]
    f = lambda a: np.ascontiguousarray(np.asarray(a, dtype=np.float32))
    inp = {k: f(v) for k, v in inputs.items()}
    cst = _consts()
    in_maps = []
    for c in range(NCORES):
        m = {}
        for k, v in inp.items():
            if k == "x_prompt":
                m[k] = np.ascontiguousarray(v[c])
            elif k == "x_sample":
                m[k] = np.ascontiguousarray(v[NSEQ * c:NSEQ * (c + 1)].reshape(NSEQ * TSQ, D))
            elif k.startswith("state_"):
                m[k] = np.ascontiguousarray(v[:, NSEQ * c:NSEQ * (c + 1)])
            else:
                m[k] = v
        m["c_all"] = cst
        in_maps.append(m)
    res = run_bass_kernel_spmd(nc, in_maps, core_ids=list(range(NCORES)))
    R = res.results
    cat = lambda name, ax: np.concatenate([np.asarray(R[c][name]) for c in range(NCORES)], axis=ax)
    y_prompt = np.stack([np.asarray(R[c]["y_p"]) for c in range(NCORES)], axis=0)
    y_sample = np.concatenate([np.asarray(R[c]["y_s"]).reshape(NSEQ, TSQ, D) for c in range(NCORES)], axis=0)
    outs = (y_prompt, y_sample,
            cat("pool_p", 1), cat("pool_s", 1), cat("gla_p", 1), cat("gla_s", 1),
            cat("hg_p", 1), cat("hg_s", 1), cat("ssm_p", 1), cat("ssm_s", 1),
            cat("conv_p", 1), cat("conv_s", 1))
    return tuple(np.ascontiguousarray(o.astype(np.float32)) for o in outs)
```

```python
import contextlib
import numpy as np
import concourse.bass as bass
import concourse.mybir as mybir
from concourse.bass_utils import run_bass_kernel_spmd

F32 = mybir.dt.float32
BF16 = mybir.dt.bfloat16
AF = mybir.ActivationFunctionType
ALU = mybir.AluOpType
AX = mybir.AxisListType

NCORES = 8
D = 1024
DFF = 2816
NIN = 3092
TP = 2048
NSEQ = 16
TSQ = 8
T = TP + NSEQ * TSQ
NT = T // 128
EPS = 1e-6
DEPTH = 2
C_GQ, C_GK = 256, 384
NINW = NIN - 256
C_PX, C_GV, C_GR, C_GLR = 0, 256, 512, 768
C_RQ, C_RF, C_RI, C_RG, C_SZ, C_XBC, C_DT = 784, 1040, 1296, 1552, 1808, 2064, 2832


class V:
    __slots__ = ("ap", "keys")

    def __init__(self, ap, keys):
        self.ap = ap
        self.keys = tuple(keys) if isinstance(keys, list) else (keys,)

    def _mk(self, ap):
        return V(ap, list(self.keys))

    def __getitem__(self, idx):
        return self._mk(self.ap[idx])

    def v(self, fn):
        return self._mk(fn(self.ap))

    def k(self, *keys):
        return V(self.ap, list(keys))

    def r(self, pat, **kw):
        return self._mk(self.ap.rearrange(pat, **kw))

    def bc(self, shape):
        return self._mk(self.ap.broadcast_to(list(shape)))

    def us(self, axis):
        return self._mk(self.ap.unsqueeze(axis))


def _ap(x):
    return x.ap if isinstance(x, V) else x


def _keys(xs):
    ks = []
    for x in xs:
        if isinstance(x, V):
            ks.extend(x.keys)
    return ks


class Ref:
    __slots__ = ("id",)


class Sched:
    ENG = ("pe", "act", "dve", "pool", "sp")

    def __init__(self, nc, sem_epoch=12000, n_dma_sems=8):
        self.nc = nc
        self.ops = []
        self.last_w = {}
        self.readers = {}
        self.sem_epoch = sem_epoch
        self.n_dma_sems = n_dma_sems
        self.dma_open = []
        self.strict = STRICT_SAME_ENGINE
        self.cur = None

    def add(self, eng, fn, reads=(), writes=(), dma=0, extra=()):
        if POOL_AS_DVE and eng == "pool" and not dma:
            eng = "dve"
        if self.cur is not None:
            ref = Ref()
            self.cur.append((eng, fn, reads, writes, dma, extra, ref))
            return ref
        return self._add(eng, fn, reads, writes, dma, extra)

    @contextlib.contextmanager
    def stream(self):
        lst = []
        prev, self.cur = self.cur, lst
        try:
            yield lst
        finally:
            self.cur = prev

    def merge(self, streams, weights=None):
        if weights is None:
            weights = [1.0] * len(streams)
        keep = [i for i, st in enumerate(streams) if st]
        weights = [weights[i] for i in keep]
        streams = [streams[i] for i in keep]
        pos = [0] * len(streams)
        total = sum(len(st) for st in streams)
        for _ in range(total):
            bi, bv = -1, None
            for i, st in enumerate(streams):
                if pos[i] < len(st):
                    v = (pos[i] + 0.5) / len(st) * weights[i]
                    if bv is None or v < bv:
                        bi, bv = i, v
            eng, fn, reads, writes, dma, extra, ref = streams[bi][pos[bi]]
            pos[bi] += 1
            ref.id = self._add(eng, fn, reads, writes, dma, extra)

    def _add(self, eng, fn, reads=(), writes=(), dma=0, extra=()):
        rk = _keys(reads)
        wk = _keys(writes)
        if PSUM_EXCLUSIVE:
            pk = [k for k in rk if isinstance(k, tuple) and k[0] == "ps"]
            if pk:
                rk = [k for k in rk if k not in pk]
                wk = list(wk) + [k for k in pk if k not in wk]
        deps = {}
        for d in extra:
            deps[d.id if isinstance(d, Ref) else d] = True
        for k in rk:
            w = self.last_w.get(k)
            if w is not None:
                deps[w] = True
        for k in wk:
            w = self.last_w.get(k)
            if w is not None:
                deps.setdefault(w, False)
            for r in self.readers.get(k, ()):
                deps.setdefault(r, False)
        oid = len(self.ops)
        self.ops.append(dict(eng=eng, fn=fn, deps=deps, dma=dma, sig=None, needed=False))
        for k in wk:
            self.last_w[k] = oid
            self.readers[k] = []
        for k in rk:
            if k not in wk:
                lst = self.readers.setdefault(k, [])
                if not dma:
                    for j in range(len(lst)):
                        oj = self.ops[lst[j]]
                        if oj["eng"] == eng and not oj["dma"]:
                            lst[j] = oid
                            break
                    else:
                        lst.append(oid)
                else:
                    lst.append(oid)
        if dma:
            self.dma_open.append(oid)
        return oid

    def emit(self, final_wait_ops=()):
        nc = self.nc
        ops = self.ops
        final_wait_ops = [r.id if isinstance(r, Ref) else r for r in final_wait_ops]
        for i, op in enumerate(ops):
            for d, raw in op["deps"].items():
                od = ops[d]
                if od["dma"] or op["dma"]:
                    od["needed"] = True
                elif od["eng"] != op["eng"]:
                    od["needed"] = True
                elif (raw or self.strict) and op["eng"] in ("act", "dve", "pool"):
                    od["needed"] = True
        for i in final_wait_ops:
            ops[i]["needed"] = True
        stack = contextlib.ExitStack()
        semcount = [0]

        def newsem(name):
            semcount[0] += 1
            return stack.enter_context(nc.semaphore(f"{name}_{semcount[0]}"))

        with stack:
            cur = {e: None for e in self.ENG}
            cnt = {e: 0 for e in self.ENG}
            dma_sems = {}
            dma_rr = {}
            for i, op in enumerate(ops):
                e = op["eng"]
                if op["dma"]:
                    if e not in dma_sems:
                        dma_sems[e] = [[newsem("dq" + e), 0] for _ in range(self.n_dma_sems)]
                        dma_rr[e] = 0
                    slot = dma_sems[e][dma_rr[e] % self.n_dma_sems]
                    dma_rr[e] += 1
                    prev = slot[1]
                    slot[1] += 16 * op["dma"]
                    op["sig"] = (slot[0], slot[1])
                    op["dma_prev"] = (slot[0], prev)
                elif op["needed"]:
                    if cur[e] is None or cnt[e] >= self.sem_epoch:
                        cur[e] = newsem("s" + e)
                        cnt[e] = 0
                    cnt[e] += 1
                    op["sig"] = (cur[e], cnt[e])
            self.n_sems = semcount[0]
            per_eng = {e: [i for i, op in enumerate(ops) if op["eng"] == e] for e in self.ENG}

            def run_engine(e, eng):
                seen = {}

                def wait(sem, val):
                    if val <= 0:
                        return
                    key = id(sem)
                    if seen.get(key, 0) >= val:
                        return
                    seen[key] = val
                    eng.wait_ge(sem, val)

                for i in per_eng[e]:
                    op = ops[i]
                    for d, raw in op["deps"].items():
                        od = ops[d]
                        if od["sig"] is None:
                            continue
                        same = (od["eng"] == e) and not od["dma"]
                        if same and not op["dma"]:
                            if e == "pe" or not (raw or self.strict):
                                continue
                        wait(*od["sig"])
                    if op["dma"]:
                        wait(*op["dma_prev"])
                    r = op["fn"](eng)
                    if op["dma"]:
                        sem, _ = op["sig"]
                        for ins in r:
                            ins.then_inc(sem, 16)
                    elif op["sig"] is not None:
                        r.then_inc(op["sig"][0], 1)
                if e == "sp":
                    for i in final_wait_ops:
                        wait(*ops[i]["sig"])

            with nc.Block() as block:
                block.tensor(lambda eng: run_engine("pe", eng))
                block.scalar(lambda eng: run_engine("act", eng))
                block.vector(lambda eng: run_engine("dve", eng))
                block.gpsimd(lambda eng: run_engine("pool", eng))
                block.sync(lambda eng: run_engine("sp", eng))

    def mm(self, out, lhsT, rhs, start=True, stop=True):
        return self.add("pe", lambda e: e.matmul(_ap(out), _ap(lhsT), _ap(rhs), start=start, stop=stop),
                        reads=[lhsT, rhs], writes=[out])

    def tr(self, out, in_, ident):
        return self.add("pe", lambda e: e.transpose(_ap(out), _ap(in_), _ap(ident)),
                        reads=[in_, ident], writes=[out])

    def act(self, out, in_, func, bias=None, scale=1.0):
        kw = {}
        if bias is not None:
            kw["bias"] = _ap(bias)
        sc = _ap(scale)
        return self.add("act", lambda e: e.activation(_ap(out), _ap(in_), func, scale=sc, **kw),
                        reads=[in_, bias, scale], writes=[out])

    def tt(self, out, in0, in1, op, eng="dve"):
        return self.add(eng, lambda e: e.tensor_tensor(_ap(out), _ap(in0), _ap(in1), op),
                        reads=[in0, in1], writes=[out])

    def ts(self, out, in0, s1, op0, eng="dve"):
        return self.add(eng, lambda e: e.tensor_scalar(_ap(out), _ap(in0), _ap(s1), None, op0),
                        reads=[in0, s1], writes=[out])

    def ts2(self, out, in0, s1, s2, op0, op1, eng="dve"):
        a, b, c, d = _ap(out), _ap(in0), _ap(s1), _ap(s2)
        return self.add(eng, lambda e: e.tensor_scalar(a, b, c, d, op0, op1), reads=[in0, s1, s2], writes=[out])

    def stt(self, out, in0, scalar, in1, op0, op1, eng="dve"):
        return self.add(eng, lambda e: e.scalar_tensor_tensor(_ap(out), _ap(in0), _ap(scalar), _ap(in1), op0, op1),
                        reads=[in0, scalar, in1], writes=[out])

    def copy(self, out, in_, eng="dve"):
        if eng == "act":
            return self.add(eng, lambda e: e.copy(_ap(out), _ap(in_)), reads=[in_], writes=[out])
        return self.add(eng, lambda e: e.tensor_copy(_ap(out), _ap(in_)), reads=[in_], writes=[out])

    def recip(self, out, in_):
        return self.add("dve", lambda e: e.reciprocal(_ap(out), _ap(in_)), reads=[in_], writes=[out])

    def rsum(self, out, in_, eng="dve"):
        return self.add(eng, lambda e: e.reduce_sum(_ap(out), _ap(in_), AX.X), reads=[in_], writes=[out])

    def scan(self, out, d0, d1, init, op0, op1, eng="dve"):
        return self.add(eng, lambda e: e.tensor_tensor_scan(_ap(out), _ap(d0), _ap(d1), init, op0, op1),
                        reads=[d0, d1], writes=[out])

    def memset(self, out, val, eng="dve"):
        return self.add(eng, lambda e: e.memset(_ap(out), val), reads=[], writes=[out])

    def dma(self, pairs, eng="sp", reads=(), writes=()):
        rd = list(reads) + [p[1] for p in pairs]
        wr = list(writes) + [p[0] for p in pairs]

        def fn(e):
            return [e.dma_start(out=_ap(o), in_=_ap(i)) for (o, i) in pairs]
        return self.add(eng, fn, reads=rd, writes=wr, dma=len(pairs))


class Arena:
    def __init__(self, h32, hbf, nbytes, tag):
        self.h32, self.hbf, self.nbytes, self.tag = h32, hbf, nbytes, tag
        self.off = 0
        self.offs = {}

    def alloc(self, name, shape, dt=F32):
        esz = 4 if dt == F32 else 2
        n = 1
        for s in shape[1:]:
            n *= s
        nb = (n * esz + 3) // 4 * 4
        assert self.off + nb <= self.nbytes, (self.tag, name, self.off, nb, self.nbytes)
        self.offs[name] = self.off
        if dt == F32:
            ap = self.h32[:, self.off // 4: self.off // 4 + n]
        else:
            ap = self.hbf[:, self.off // 2: self.off // 2 + n]
        self.off += nb
        return self._view(ap, shape, (self.tag, name))

    def alias(self, base, shape, dt, keys):
        off = self.offs[base]
        n = 1
        for s_ in shape[1:]:
            n *= s_
        if dt == F32:
            ap = self.h32[:, off // 4: off // 4 + n]
        else:
            ap = self.hbf[:, off // 2: off // 2 + n]
        return self._view(ap, shape, keys)

    def _view(self, ap, shape, keys):
        if len(shape) == 3:
            ap = ap.rearrange("p (a b) -> p a b", b=shape[2])
        elif len(shape) == 4:
            ap = ap.rearrange("p (a b c) -> p a b c", b=shape[2], c=shape[3])
        if shape[0] < 128:
            ap = ap[0:shape[0]]
        if isinstance(keys, list):
            return V(ap, keys)
        return V(ap, [keys, self.tag] if SER_ARENA else keys)


ARENA_BYTES = 62464 + 4096
STRICT_SAME_ENGINE = True
import os as _os
SER_ARENA = bool(int(_os.environ.get('SER_ARENA', '0')))
POOL_AS_DVE = bool(int(_os.environ.get('POOL_AS_DVE', '0')))
PSUM_EXCLUSIVE = bool(int(_os.environ.get('PSUM_EXCLUSIVE', '1')))
MERGE_W = [float(x) for x in _os.environ.get('MERGE_W', '1,1,1,1,1').split(',')]
SER_PS = bool(int(_os.environ.get('SER_PS', '0')))
DEFAULT_CFG = dict(stage=99, tmax=17, layers=2, ffn1=True, mixer=True, ffn2=True, pool=True, gla=True, hgrn=True, ssd=True)


def build(cfg=None):
    cfg = dict(DEFAULT_CFG, **(cfg or {}))
    nc = bass.Bass("TRN2", target_bir_lowering=False)
    din = {}

    def inp(name, shape):
        din[name] = nc.dram_tensor(name, list(shape), F32, kind="ExternalInput").ap()
        return din[name]

    def outp(name, shape):
        return nc.dram_tensor(name, list(shape), F32, kind="ExternalOutput").ap()

    x_prompt = inp("x_prompt", [TP, D])
    x_sample = inp("x_sample", [NSEQ * TSQ, D])
    state_pool = inp("state_pool", [DEPTH, NSEQ, 15, 256])
    state_gla = inp("state_gla", [DEPTH, NSEQ, 4, 32, 64])
    state_hgrn = inp("state_hgrn", [DEPTH, NSEQ, 4, 64, 64])
    state_ssm = inp("state_ssm", [DEPTH, NSEQ, 4, 64, 128])
    state_conv = inp("state_conv", [DEPTH, NSEQ, 3, 768])
    ffn_norm = [inp("ffn1_norm", [DEPTH, D]), inp("ffn2_norm", [DEPTH, D])]
    ffn_wg = [inp("ffn1_w_gate", [DEPTH, D, DFF]), inp("ffn2_w_gate", [DEPTH, D, DFF])]
    ffn_wu = [inp("ffn1_w_up", [DEPTH, D, DFF]), inp("ffn2_w_up", [DEPTH, D, DFF])]
    ffn_wd = [inp("ffn1_w_down", [DEPTH, DFF, D]), inp("ffn2_w_down", [DEPTH, DFF, D])]
    mix_norm = inp("mix_norm", [DEPTH, D])
    w_in = inp("w_in", [DEPTH, D, NIN])
    pool_w = inp("pool_w", [DEPTH, 4, 64, 64])
    pool_scale = inp("pool_scale", [DEPTH, 256])
    gla_w_gate = inp("gla_w_gate", [DEPTH, 16, 128])
    gla_gate_bias = inp("gla_gate_bias", [DEPTH, 128])
    gla_norm = inp("gla_norm", [DEPTH, 64])
    hgrn_lb_logits = inp("hgrn_lb_logits", [DEPTH, 256])
    hgrn_norm = inp("hgrn_norm", [DEPTH, 64])
    ssm_conv_w = inp("ssm_conv_w", [DEPTH, 4, 768])
    ssm_conv_b = inp("ssm_conv_b", [DEPTH, 768])
    ssm_dt_bias = inp("ssm_dt_bias", [DEPTH, 4])
    ssm_A_log = inp("ssm_A_log", [DEPTH, 4])
    ssm_D = inp("ssm_D", [DEPTH, 4])
    ssm_norm = inp("ssm_norm", [DEPTH, 256])
    w_out = inp("w_out", [DEPTH, D, D])
    final_norm = inp("final_norm", [D])
    c_all = inp("c_all", [128, 9, 128])

    y_p = outp("y_p", [TP, D])
    y_s = outp("y_s", [NSEQ * TSQ, D])
    pool_p = outp("pool_p", [DEPTH, 1, 15, 256])
    pool_s = outp("pool_s", [DEPTH, NSEQ, 15, 256])
    gla_p = outp("gla_p", [DEPTH, 1, 4, 32, 64])
    gla_s = outp("gla_s", [DEPTH, NSEQ, 4, 32, 64])
    hg_p = outp("hg_p", [DEPTH, 1, 4, 64, 64])
    hg_s = outp("hg_s", [DEPTH, NSEQ, 4, 64, 64])
    ssm_p = outp("ssm_p", [DEPTH, 1, 4, 64, 128])
    ssm_s = outp("ssm_s", [DEPTH, NSEQ, 4, 64, 128])
    conv_p = outp("conv_p", [DEPTH, 1, 3, 768])
    conv_s = outp("conv_s", [DEPTH, NSEQ, 3, 768])

    S = Sched(nc)
    out_ops = []
    st = contextlib.ExitStack()
    with st:
        def sbt(name, shape, dt=F32):
            return st.enter_context(nc.sbuf_tensor(name, shape, dt))

        Xh = sbt("X", [128, 8, T])
        RBh = sbt("RB", [128, 8 * NINW], BF16)
        WOUTh = sbt("WOUT", [128, 8, D], BF16)
        WQKh = sbt("WQK", [128, 8, 512], BF16)
        ARh = sbt("AR", [128, ARENA_BYTES // 4])
        CONh = sbt("CON", [128, 9, 128])
        PVh = sbt("PV", [128, 128])
        MISCh = sbt("MISC", [128, 64])
        BARh = sbt("BAR", [128, 8])
        ONESBh = sbt("ONESB", [128, 128], BF16)
        ONESFh = sbt("ONESF", [128, 128])
        PSh = [st.enter_context(nc.psum_tensor(f"ps{i}", [128, 512], F32)) for i in range(8)]

        ARbf = ARh.bitcast(BF16)

        def Xv(k0, k1, c0, n):
            keys = [("X", t) for t in range(c0 // 128, (c0 + n + 127) // 128)]
            return V(Xh[:, k0:k1, c0:c0 + n], keys)

        XNap = RBh[:, 0:8 * T].rearrange("p (k t) -> p k t", t=T)

        def XNv(k, c0, n):
            keys = [("XN", t) for t in range(c0 // 128, (c0 + n + 127) // 128)]
            return V(XNap[:, k, c0:c0 + n], keys)

        WIN = V(RBh[:, :].rearrange("p (k n) -> p k n", n=NINW), "WIN")
        WOUT = V(WOUTh[:], "WOUT")
        WQK = V(WQKh[:], "WQK")
        CON = V(CONh[:], "CON")
        IDENT = CON[:, 0, :]
        UF = CON[:, 1, :]
        NEGM = CON[:, 2, :]
        RMS = {128: CON[:, 3, :], 64: CON[:, 4, :], 8: CON[:, 5, :]}
        INVT = CON[:, 6:8, :]
        BD64 = CON[:, 8, :]
        PV = V(PVh[:], "PV")
        MISC = V(MISCh[:], "MISC")
        ONESB = V(ONESBh[:], "ONESB")
        ONESF = V(ONESFh[:], "ONESF")
        EPS6 = MISC[:, 0:1]
        ONE1 = MISC[:, 1:2]
        LBv = MISC[:, 4:8].r("p (l t) -> p l t", t=2)
        OMLv = MISC[:, 8:12].r("p (l t) -> p l t", t=2)
        NGB = MISC[:, 12:16].r("p (l t) -> p l t", t=2)
        GBraw = MISC[:, 16:20].r("p (l t) -> p l t", t=2)
        BARS = V(BARh[:], "BAR")
        PS = [V(PSh[i][:], [("ps", i), "ps"] if SER_PS else ("ps", i)) for i in range(8)]
        bank_set = [list(range(8))]
        psn = {}

        def nb():
            bs = bank_set[0]
            k = tuple(bs)
            i = psn.get(k, 0)
            psn[k] = i + 1
            return PS[bs[i % len(bs)]]

        barn = [0]

        def barrier():
            n = barn[0]
            barn[0] += 1
            marks = []
            marks.append(S.add("pe", lambda e: e.matmul(PSh[7][:, 0:1], ONESBh[:, 0:128], ONESBh[:, 0:1], start=True, stop=True),
                               reads=[ONESB], writes=[PS[7]]))
            marks.append(S.add("act", lambda e: e.copy(BARh[:, 0:1], MISCh[:, 0:1]), reads=[MISC], writes=[BARS[:, 0:1].k(("bar", "act"))]))
            marks.append(S.add("dve", lambda e: e.memset(BARh[:, 1:2], 0.0), writes=[BARS[:, 1:2].k(("bar", "dve"))]))
            marks.append(S.add("pool", lambda e: e.memset(BARh[:, 2:3], 0.0), writes=[BARS[:, 2:3].k(("bar", "pool"))]))
            ext = marks + [d for d in S.dma_open]
            S.dma_open = []
            S.add("pe", lambda e: e.matmul(PSh[7][:, 0:1], ONESBh[:, 0:128], ONESBh[:, 0:1], start=True, stop=True),
                  reads=[ONESB], writes=[PS[7]], extra=ext)
            S.add("act", lambda e: e.copy(BARh[:, 3:4], MISCh[:, 0:1]), reads=[MISC], writes=[BARS[:, 3:4].k(("bar2", "act"))], extra=ext)
            S.add("dve", lambda e: e.memset(BARh[:, 4:5], 0.0), writes=[BARS[:, 4:5].k(("bar2", "dve"))], extra=ext)
            S.add("pool", lambda e: e.memset(BARh[:, 5:6], 0.0), writes=[BARS[:, 5:6].k(("bar2", "pool"))], extra=ext)
            S.add("sp", lambda e: e.nop(), extra=ext)

        def arena(tag):
            return Arena(ARh, ARbf, ARENA_BYTES, tag)

        S.dma([(CON, c_all)])
        S.memset(MISC, 0.0)
        S.memset(EPS6, EPS)
        S.memset(ONE1, 1.0)
        S.memset(ONESB, 1.0)
        S.memset(ONESF, 1.0)
        A0 = arena("setup")
        RAW = A0.alloc("RAW", [128, 128])
        S.memset(RAW, 0.0)
        rows = []
        R_FFN = {}
        r = 0
        for l in range(DEPTH):
            for w in range(2):
                rows.append((RAW[r:r + 8, :], ffn_norm[w][l].rearrange("(k p) -> k p", p=128)))
                R_FFN[(l, w)] = r
                r += 8
        R_MIX = {}
        for l in range(DEPTH):
            rows.append((RAW[r:r + 8, :], mix_norm[l].rearrange("(k p) -> k p", p=128)))
            R_MIX[l] = r
            r += 8
        rows.append((RAW[r:r + 8, :], final_norm.rearrange("(k p) -> k p", p=128)))
        R_FIN = r
        r += 8
        R_PSC = r
        rows.append((RAW[r:r + 4, :], pool_scale.rearrange("l (t p) -> (l t) p", p=128)))
        r += 4
        R_CB = r
        rows.append((RAW[r:r + 12, :], ssm_conv_b.rearrange("l (j p) -> (l j) p", p=128)))
        r += 12
        R_CW = r
        rows.append((RAW[r:r + 48, :], ssm_conv_w.rearrange("l d (j p) -> (l d j) p", p=128)))
        r += 48
        R_LB = r
        rows.append((RAW[r:r + 4, :], hgrn_lb_logits.rearrange("l (t p) -> (l t) p", p=128)))
        r += 4
        assert r <= 128
        S.dma(rows)
        bk = nb()
        S.tr(bk[:, 0:128], RAW, IDENT)
        S.copy(PV, bk[:, 0:128])
        gbp = []
        for l in range(DEPTH):
            for h in range(4):
                pt, h2 = h // 2, h % 2
                gbp.append((GBraw[h2 * 64:h2 * 64 + 32, l, pt:pt + 1],
                            gla_gate_bias[l, h * 32:(h + 1) * 32].rearrange("(p o) -> p o", o=1)))
        S.dma(gbp)
        S.ts(NGB, GBraw, -1.0, ALU.mult)
        S.memset(LBv[:, 0, :], 0.0)
        S.memset(OMLv[:, 0, :], 1.0)
        TMPL = MISC[:, 20:22]
        S.tt(TMPL, PV[:, R_LB:R_LB + 2], PV[:, R_LB + 2:R_LB + 4], ALU.subtract)
        S.act(TMPL, TMPL, AF.Exp)
        S.ts(TMPL, TMPL, 1.0, ALU.add)
        S.recip(LBv[:, 1, :], TMPL)
        S.stt(OMLv[:, 1, :], LBv[:, 1, :], -1.0, ONE1.bc([128, 2]), ALU.mult, ALU.add)

        STG = [A0.alloc(f"STG{i}", [128, D]) for i in range(4)]
        for t in range(NT):
            src = x_prompt[t * 128:(t + 1) * 128, :] if t < 16 else x_sample[:, :]
            sg = STG[t % 4]
            S.dma([(sg, src)])
            for hb in range(2):
                bk = nb()
                for j in range(4):
                    k = hb * 4 + j
                    S.tr(bk[:, j * 128:(j + 1) * 128], sg[:, k * 128:(k + 1) * 128], IDENT)
                S.copy(Xv(hb * 4, hb * 4 + 4, t * 128, 128), bk.r("p (a b) -> p a b", b=128),
                       eng="act" if hb == 0 else "dve")
        barrier()

        TGS = [(0, 512), (512, 512), (1024, 512), (1536, 512), (2048, 128)]

        def norm_all(SQ, LNT, RSTD, grow):
            for (c0, n) in TGS:
                S.act(SQ[:, :, 0:n], Xv(0, 8, c0, n), AF.Square)
                bk = nb()
                for k in range(8):
                    S.mm(bk[:, 0:n], ONESB, SQ[:, k, 0:n], start=(k == 0), stop=(k == 7))
                S.act(LNT[:, 0:n], bk[:, 0:n], AF.Ln, bias=EPS6, scale=1.0 / D)
                S.act(RSTD[:, 0:n], LNT[:, 0:n], AF.Exp, scale=-0.5)
                for k in range(8):
                    S.stt(XNv(k, c0, n), Xv(k, k + 1, c0, n)[:, 0, :], PV[:, grow + k:grow + k + 1],
                          RSTD[:, 0:n], ALU.mult, ALU.mult)

        GRP = [(g * 512, 4) for g in range(5)] + [(2560, 2)]

        def ffn(l, w):
            A = arena(f"ffn{l}{w}")
            WG = [A.alloc(f"WG{i}", [128, 8, 512], BF16) for i in range(2)]
            WU = [A.alloc(f"WU{i}", [128, 8, 512], BF16) for i in range(2)]
            WD = [A.alloc(f"WD{i}", [128, 4, D], BF16) for i in range(2)]
            wg_d = ffn_wg[w][l].rearrange("(k p) n -> p k n", p=128)
            wu_d = ffn_wu[w][l].rearrange("(k p) n -> p k n", p=128)
            wd_d = ffn_wd[w][l]

            def load(g):
                c0, nch = GRP[g]
                s = g % 2
                ncol = nch * 128
                S.dma([(WG[s][:, :, 0:ncol], wg_d[:, :, c0:c0 + ncol]),
                       (WU[s][:, :, 0:ncol], wu_d[:, :, c0:c0 + ncol]),
                       (WD[s][:, 0:nch, :], wd_d[c0:c0 + ncol, :].rearrange("(c p) n -> p c n", p=128))],
                      eng="pool")

            HBall = A.alloc("HB", [128, 8, 512], BF16)
            HB = [HBall[:, 4 * i:4 * i + 4, :].k((A.tag, "HB%d" % i)) for i in range(2)]
            SGT = [A.alloc(f"SG{i}", [128, 512]) for i in range(2)]
            load(0)
            norm_all(HBall.k((A.tag, "HB0"), (A.tag, "HB1")), SGT[0], SGT[1], R_FFN[(l, w)])
            for g in range(len(GRP)):
                if g + 1 < len(GRP):
                    load(g + 1)
                c0g, nch = GRP[g]
                s = g % 2
                for tgi, (c0, n) in enumerate(TGS):
                    hb = HB[tgi % 2]
                    for c in range(nch):
                        pg = nb()
                        pu = nb()
                        for k in range(8):
                            S.mm(pg[:, 0:n], WG[s][:, k, c * 128:(c + 1) * 128], XNv(k, c0, n), start=(k == 0), stop=(k == 7))
                        for k in range(8):
                            S.mm(pu[:, 0:n], WU[s][:, k, c * 128:(c + 1) * 128], XNv(k, c0, n), start=(k == 0), stop=(k == 7))
                        sgt = SGT[c % 2]
                        S.act(sgt[:, 0:n], pg[:, 0:n], AF.Silu)
                        S.tt(hb[:, c, 0:n], sgt[:, 0:n], pu[:, 0:n], ALU.mult)
                    for m in range(8):
                        py = nb()
                        for c in range(nch):
                            S.mm(py[:, 0:n], WD[s][:, c, m * 128:(m + 1) * 128], hb[:, c, 0:n], start=(c == 0), stop=(c == nch - 1))
                        xv = Xv(m, m + 1, c0, n)[:, 0, :]
                        S.stt(xv, py[:, 0:n], 0.5, xv, ALU.mult, ALU.add)
            barrier()

        def mixer(l):
            A = arena(f"mix{l}")
            for k in range(8):
                S.dma([(WIN[:, k, 0:256], w_in[l, k * 128:(k + 1) * 128, 0:256]),
                       (WIN[:, k, 256:NINW], w_in[l, k * 128:(k + 1) * 128, 512:NIN])], eng="pool")
            S.memset(WQK, 0.0, eng="pool")
            for k in range(8):
                S.dma([(WQK[:, k, 0:256].r("p (h c) -> p h c", c=64)[:, :, 0:32],
                        w_in[l, k * 128:(k + 1) * 128, C_GQ:C_GQ + 128].rearrange("p (h c) -> p h c", c=32)),
                       (WQK[:, k, 256:512].r("p (h c) -> p h c", c=64)[:, :, 0:32],
                        w_in[l, k * 128:(k + 1) * 128, C_GK:C_GK + 128].rearrange("p (h c) -> p h c", c=32))],
                      eng="pool")
            S.dma([(WOUT, w_out[l].rearrange("(k p) n -> p k n", p=128))], eng="pool")
            PWBD = A.alloc("PWBD", [128, 2, 128], BF16)
            S.memset(PWBD, 0.0)
            S.dma([(PWBD[g2 * 64:(g2 + 1) * 64, pt, g2 * 64:(g2 + 1) * 64], pool_w[l, 2 * pt + g2])
                   for pt in range(2) for g2 in range(2)], eng="pool")
            GWG = A.alloc("GWG", [128, 256], BF16)
            S.memset(GWG, 0.0)
            S.dma([(GWG[0:16, :].r("p (h c) -> p h c", c=64)[:, :, 0:32],
                    gla_w_gate[l].rearrange("p (h c) -> p h c", c=32))], eng="pool")
            GN = A.alloc("GN", [128, 64])
            HN = A.alloc("HN", [128, 64])
            SN = A.alloc("SN", [128, 256])
            SM4 = A.alloc("SM4", [128, 16])
            DTB, NEGA, DSK = SM4[:, 0:4], SM4[:, 4:8], SM4[:, 8:12]
            S.dma([(GN, gla_norm[l].partition_broadcast(128)), (HN, hgrn_norm[l].partition_broadcast(128)),
                   (SN, ssm_norm[l].partition_broadcast(128)), (DTB, ssm_dt_bias[l].partition_broadcast(128)),
                   (NEGA, ssm_A_log[l].partition_broadcast(128)), (DSK, ssm_D[l].partition_broadcast(128))])
            S.act(NEGA, NEGA, AF.Exp)
            S.ts(NEGA, NEGA, -1.0, ALU.mult)
            CW = PV[:, R_CW + l * 24:R_CW + (l + 1) * 24].r("p (d j) -> p d j", j=6)
            CB = PV[:, R_CB + l * 6:R_CB + (l + 1) * 6]
            PSC = PV[:, R_PSC + l * 2:R_PSC + (l + 1) * 2]
            GROW = R_MIX[l]

            tg = A.tag
            XNTs = [A.alloc("XNT0", [128, 8, 128], BF16), A.alloc("XNT1", [128, 8, 128], BF16)]
            XNT = XNTs[0]
            NS = A.alloc("NS", [128, 4, 128])
            SQ = A.alias("NS", [128, 8, 128], BF16, (tg, "NS"))
            RSTD = A.alloc("RSTD", [128, 128])
            MIXs = [[A.alloc(f"MIX{b}{i}", [128, 2, 128], BF16) for i in range(4)] for b in range(2)]
            MIX = MIXs[0]
            PXB = A.alloc("PXB", [128, 2 * 16 * 24])
            SA = A.alloc("SA", [128, 2 * 8 * 24])
            SBb = A.alloc("SBb", [128, 2 * 8 * 24])
            DPL = A.alloc("DPL", [128, 2, 128], BF16)
            PT = A.alloc("PT", [128, 2, 128])
            P_F = [None, None, PT]
            P_TT = [SA[:, 0:256], SBb[:, 0:256], None, PT.r("p a b -> p (a b)")]
            G_F = [A.alloc("GF0", [128, 4, 128]), A.alloc("GF1", [128, 4, 128]), A.alloc("GF2", [128, 2, 128]),
                   A.alloc("GF3", [128, 4, 128]), A.alloc("GF4", [128, 2, 128])]
            G_FB = [A.alloc("GFB0", [128, 4, 128], BF16), None, A.alloc("GFB2", [128, 1, 128], BF16), A.alloc("GFB3", [128, 4, 128], BF16)]
            G_TT = [None] + [A.alloc(f"GT{i}", [128, 256]) for i in range(1, 4)]
            G_TT[0] = G_TT[3]
            G_TB = [A.alloc("GTB0", [128, 256], BF16), A.alloc("GTB1", [128, 256], BF16), A.alloc("GTB2", [128, 512], BF16),
                    A.alloc("GTB3", [128, 256], BF16)]
            GQAB = A.alloc("GQAB", [128, 4, 128], BF16)
            G_SMT = A.alloc("GSMT", [128, 64])
            XBC = A.alloc("XBC", [128, 6 * 16 * 11])
            CV = A.alloc("CV", [128, 6, 128])
            UA_ = A.alloc("UA", [128, 4, 128])
            DD_ = A.alloc("DDM", [128, 4, 128])
            S_F = [UA_, DD_, UA_, UA_, DD_]
            S_FB = [None, A.alloc("SFB1", [128, 4, 128], BF16), A.alloc("SFB2", [128, 4, 128], BF16)]
            S_TT = [A.alloc(f"ST{i}", [128, 256]) for i in range(4)]
            S_TB = [A.alloc("STB0", [128, 256], BF16), A.alloc("STB1", [128, 256], BF16), A.alloc("STB2", [128, 512], BF16),
                    A.alloc("STB3", [128, 256], BF16)]
            S_SMT = A.alloc("SSMT", [128, 64])
            UAf = UA_.r("p a b -> p (a b)")
            DDf = DD_.r("p a b -> p (a b)")
            GLS = A.alloc("GLS", [128, 2, 64])
            GLSb = A.alloc("GLSb", [128, 2, 128], BF16)
            HGS = A.alloc("HGS", [128, 2, 64])
            HGSb = A.alloc("HGSb", [128, 2, 128], BF16)
            SST = A.alloc("SST", [128, 256])
            SSTb = A.alloc("SSTb", [128, 256], BF16)
            N0 = A.alloc("N0", [128, 2, 128])
            for z in (GLS, GLSb, HGS, HGSb, SST, SSTb, PXB, XBC, G_FB[0], GQAB, G_TB[3]):
                S.memset(z, 0.0, eng="pool")
            F, FB, TT, TB, SMT = G_F, G_FB, G_TT, G_TB, G_SMT

            def proj_fm(dst, wv):
                for k in range(8):
                    S.mm(dst, wv[:, k, :], XNT[:, k, :], start=(k == 0), stop=(k == 7))

            def proj_tm(dst, o0, Lc, c0w, ncol):
                for k in range(8):
                    S.mm(dst, XNT[:, k, o0:o0 + Lc], WIN[:, k, c0w:c0w + ncol], start=(k == 0), stop=(k == 7))

            def silu_tm(dst, src_ps, Lc, tmp):
                S.act(tmp, src_ps, AF.Exp, scale=-1.0)
                S.act(tmp, tmp, AF.Ln, bias=ONE1[0:Lc])
                S.act(tmp, tmp, AF.Exp, scale=-1.0)
                S.tt(dst, src_ps, tmp, ALU.mult)

            def to_fm(res, Lc, o0, mc):
                bk = nb()
                for j in range(2):
                    S.tr(bk[:, j * Lc:(j + 1) * Lc], res[0:Lc, j * 128:(j + 1) * 128], IDENT[0:Lc, 0:Lc])
                S.copy(MIX[mc // 2][:, :, o0:o0 + Lc], bk[:, 0:2 * Lc].r("p (a b) -> p a b", b=Lc), eng="act")

            def gla_like(name, t, Qf, Kf, GSf, sg, c0w, Sst, Sstb, normw, mc, Lc, st_in, st_out, st_pout, Ls=None):
                is_s = (t == 16)
                Ls = Ls or Lc
                nch = 128 // Lc
                nsub = Lc // Ls
                nseg = 128 // Ls
                assert nsub in (1, 2)
                F, FB, TT, TB, SMT = G_F, G_FB, G_TT, G_TB, G_SMT
                BP, EB, ENB, DD, KH = F[0][:, 0:2, :], F[0][:, 2:4, :], F[1][:, 0:2, :], F[1][:, 2:4, :], F[2][:, 0:2, :]
                QT, KT = FB[3][:, 0:2, :], FB[3][:, 2:4, :]
                QBD = FB[0].r("p (a b) c -> p a b c", b=2)
                QTA, QTB = GQAB[:, 0:2, :], GQAB[:, 2:4, :]
                KHBB = TB[3]
                DEC = SMT[:, 0:32]
                RM = RMS[Ls]
                for pt in range(2):
                    S.scan(BP[:, pt, :], RM, GSf[:, pt, :], 0.0, ALU.mult, ALU.add)
                S.act(EB, BP, AF.Exp, scale=sg)
                S.act(ENB, BP, AF.Exp, scale=-sg)
                S.tt(QT, Qf, EB, ALU.mult)
                for h2 in range(2):
                    r0 = h2 * 64
                    S.copy(QBD[r0:r0 + 64, :, h2, :], QT[r0:r0 + 64, :, :], eng="act")
                if nsub == 2:
                    S.copy(QTA[:, :, 0:64], QT[:, :, 0:64], eng="act")
                    S.copy(QTB[:, :, 64:128], QT[:, :, 64:128], eng="act")
                S.tt(KT, Kf, ENB, ALU.mult)
                BPc = BP.r("p a (c l) -> p (a c) l", l=Ls)
                S.tt(DD.r("p a (c l) -> p (a c) l", l=Ls), BPc, BPc[:, :, Ls - 1:Ls].bc([128, 2 * nseg, Ls]), ALU.subtract)
                S.act(DD, DD, AF.Exp, scale=-sg)
                S.tt(KH, Kf, DD, ALU.mult)
                S.act(DEC[:, 0:2 * nseg].us(2), BPc[:, :, Ls - 1:Ls], AF.Exp, scale=sg)
                for ci in range(nch):
                    o0 = ci * Lc
                    if is_s:
                        st_in(ci)
                    VB, KHB, ATT = TB[0][0:Lc, 0:256], TB[1][0:Lc, 0:256], TB[2][0:Lc, :]
                    EG, SR, OS, SQO = TT[0][0:Lc, 0:256], TT[1][0:Lc, 0:256], TT[2][0:Lc, 0:256], TT[3][0:Lc, 0:256]
                    pa = nb()
                    proj_tm(pa[0:Lc, 0:512], o0, Lc, c0w, 512)
                    S.copy(VB, pa[0:Lc, 0:256], eng="act")
                    silu_tm(SR, pa[0:Lc, 256:512], Lc, EG)
                    pb = nb()
                    for pt in range(2):
                        S.tr(pb[0:Lc, pt * 128:(pt + 1) * 128], KH[:, pt, o0:o0 + Lc], IDENT)
                    S.copy(KHB, pb[0:Lc, 0:256], eng="act")
                    if nsub == 2:
                        S.copy(KHBB[64:128, :], pb[64:128, 0:256], eng="act")
                    pc = nb()
                    for pt in range(2):
                        S.mm(pc[0:Lc, pt * 2 * Lc:(pt + 1) * 2 * Lc].r("p (a b) -> p a b", b=Lc), KT[:, pt, o0:o0 + Lc], QBD[:, pt, :, o0:o0 + Lc])
                    ATT3 = ATT[:, 0:4 * Lc].r("p (h l) -> p h l", l=Lc)
                    msk = UF[0:Lc, 0:Lc] if nsub == 1 else BD64
                    S.tt(ATT3, pc[0:Lc, 0:4 * Lc].r("p (h l) -> p h l", l=Lc), msk.us(1).bc([Lc, 4, Lc]), ALU.mult)
                    pd = nb()
                    for h in range(4):
                        S.mm(pd[0:Lc, h * 64:(h + 1) * 64], ATT3[:, h, :], VB[:, h * 64:(h + 1) * 64])
                    S.copy(OS, pd[0:Lc, 0:256], eng="act")
                    for si in range(nsub):
                        seg = ci * nsub + si
                        qm = QT if nsub == 1 else (QTA, QTB)[si]
                        pd2 = nb()
                        for pt in range(2):
                            S.mm(pd2[0:Lc, pt * 128:(pt + 1) * 128], qm[:, pt, o0:o0 + Lc], Sstb[:, pt, :])
                        S.tt(OS, OS, pd2[0:Lc, 0:256], ALU.add)
                        pe_ = nb()
                        for pt in range(2):
                            if nsub == 1:
                                kk, vv = KHB[:, pt * 128:(pt + 1) * 128], VB[:, pt * 128:(pt + 1) * 128]
                            elif si == 0:
                                kk, vv = KHB[0:64, pt * 128:(pt + 1) * 128], VB[0:64, pt * 128:(pt + 1) * 128]
                            else:
                                kk, vv = KHBB[:, pt * 128:(pt + 1) * 128], VB[:, pt * 128:(pt + 1) * 128]
                            S.mm(pe_[:, pt * 128:(pt + 1) * 128], kk, vv)
                        for pt in range(2):
                            for h2 in range(2):
                                r0 = h2 * 64
                                sv = Sst[r0:r0 + 64, pt, :]
                                S.stt(sv, sv, DEC[r0:r0 + 64, pt * nseg + seg:pt * nseg + seg + 1],
                                      pe_[r0:r0 + 64, pt * 128 + r0:pt * 128 + r0 + 64], ALU.mult, ALU.add)
                        if is_s:
                            st_out(ci)
                        else:
                            for h2 in range(2):
                                r0 = h2 * 64
                                S.copy(Sstb[r0:r0 + 64, :, r0:r0 + 64], Sst[r0:r0 + 64, :, :], eng="act")
                            if t == 15 and ci == nch - 1 and si == nsub - 1:
                                st_pout()
                    S.tt(SQO, OS, OS, ALU.mult)
                    SS = SMT[0:Lc, 32:36]
                    S.rsum(SS, SQO.r("p (h v) -> p h v", v=64))
                    S.act(SS, SS, AF.Ln, bias=EPS6[0:Lc], scale=1.0 / 64)
                    S.act(SS, SS, AF.Exp, scale=-0.5)
                    OS3 = OS.r("p (h v) -> p h v", v=64)
                    S.tt(OS3, OS3, SS.us(2).bc([Lc, 4, 64]), ALU.mult)
                    S.tt(OS3, OS3, normw[0:Lc, :].us(1).bc([Lc, 4, 64]), ALU.mult)
                    S.tt(SQO, OS, SR, ALU.mult)
                    to_fm(SQO, Lc, o0, mc)

            def ssd(t, Lc):
                is_s = (t == 16)
                nch = 128 // Lc
                F, FB, TT, TB, SMT = S_F, S_FB, S_TT, S_TB, S_SMT
                BCB = FB[1]
                S.copy(BCB, CV[:, 2:6, :], eng="act")
                for ci in range(nch):
                    o0 = ci * Lc
                    sl = slice(o0, o0 + Lc)
                    SM = SMT[0:Lc, 36:52]
                    DT, AT, CUMT, WE = SM[:, 0:4], SM[:, 4:8], SM[:, 8:12], SM[:, 12:16]
                    DECE = SMT[:, 52:56].k(("mixs", "DECE"))
                    DCOL = SMT[:, 56:58].k(("mixs", "DCOL"))
                    UA, DDm, EC = F[3], F[4], F[2]
                    ATT, XDT, XDTW, BTB = TB[2][0:Lc, :], TB[0][0:Lc, 0:256], TB[1][0:Lc, 0:256], TB[3][0:Lc, 0:256]
                    CT = FB[2]
                    XST, EG, SR, Y1 = TT[0][0:Lc, 0:256], TT[1][0:Lc, 0:256], TT[2][0:Lc, 0:256], TT[3][0:Lc, 0:256]
                    if is_s:
                        S.dma([(N0, state_ssm[l, ci].rearrange("(g h2) p n -> (h2 p) g n", g=2))])
                        pq = nb()
                        for g in range(2):
                            S.tr(pq[:, g * 128:(g + 1) * 128], N0[:, g, :], IDENT)
                        S.copy(SSTb, pq[:, 0:256], eng="act")
                    pa = nb()
                    proj_tm(pa[0:Lc, 0:256], o0, Lc, C_SZ, 256)
                    proj_tm(pa[0:Lc, 256:260], o0, Lc, C_DT, 4)
                    S.tt(DT, pa[0:Lc, 256:260], DTB[0:Lc, :], ALU.add)
                    S.act(DT, DT, AF.Exp)
                    S.act(DT, DT, AF.Ln, bias=ONE1[0:Lc])
                    S.tt(AT, DT, NEGA[0:Lc, :], ALU.mult)
                    silu_tm(SR, pa[0:Lc, 0:256], Lc, EG)
                    pb = nb()
                    S.mm(pb[0:Lc, 0:4], UF[0:Lc, 0:Lc], AT)
                    S.copy(CUMT, pb[0:Lc, 0:4])
                    UA3 = UA[0:Lc, :, 0:Lc]
                    S.tt(UA3, UF[0:Lc, 0:Lc].us(1).bc([Lc, 4, Lc]), AT.us(2).bc([Lc, 4, Lc]), ALU.mult)
                    pc = nb()
                    for h in range(4):
                        S.mm(pc[:, h * Lc:(h + 1) * Lc], ONESF[0:Lc, :], UA[0:Lc, h, 0:Lc])
                    pc3 = pc[:, 0:4 * Lc].r("p (h l) -> p h l", l=Lc)
                    for h in range(4):
                        S.stt(DDm[0:Lc, h, 0:Lc], pc3[0:Lc, h, :], CUMT[:, h:h + 1], NEGM[0:Lc, 0:Lc], ALU.subtract, ALU.add)
                    S.act(DDm[0:Lc, :, 0:Lc], DDm[0:Lc, :, 0:Lc], AF.Exp)
                    pd = nb()
                    for g in range(2):
                        S.mm(pd[0:Lc, g * Lc:(g + 1) * Lc], BCB[:, g, sl], BCB[:, 2 + g, sl])
                    ATT3 = ATT[:, 0:4 * Lc].r("p (h l) -> p h l", l=Lc)
                    for g in range(2):
                        S.tt(ATT3[:, 2 * g:2 * g + 2, :], DDm[0:Lc, 2 * g:2 * g + 2, 0:Lc],
                             pd[0:Lc, g * Lc:(g + 1) * Lc].us(1).bc([Lc, 2, Lc]), ALU.mult)
                    pe_ = nb()
                    for j in range(2):
                        S.tr(pe_[0:Lc, j * 128:(j + 1) * 128], CV[:, j, sl], IDENT)
                    for g in range(2):
                        S.tr(pe_[0:Lc, 256 + g * 128:256 + (g + 1) * 128], CV[:, 2 + g, sl], IDENT)
                    S.copy(XST, pe_[0:Lc, 0:256], eng="act")
                    S.copy(BTB, pe_[0:Lc, 256:512], eng="act")
                    XDT3 = XDT.r("p (h v) -> p h v", v=64)
                    S.tt(XDT3, XST.r("p (h v) -> p h v", v=64), DT.us(2).bc([Lc, 4, 64]), ALU.mult)
                    S.tt(WE, pc3[0:Lc, :, Lc - 1], CUMT, ALU.subtract)
                    S.act(WE, WE, AF.Exp)
                    S.tt(XDTW.r("p (h v) -> p h v", v=64), XDT3, WE.us(2).bc([Lc, 4, 64]), ALU.mult)
                    S.act(DECE, pc3[:, :, Lc - 1], AF.Exp)
                    S.act(EC[:, :, 0:Lc], pc3, AF.Exp)
                    for g in range(2):
                        S.tt(CT[:, 2 * g:2 * g + 2, 0:Lc], EC[:, 2 * g:2 * g + 2, 0:Lc],
                             BCB[:, 2 + g, sl].us(1).bc([128, 2, Lc]), ALU.mult)
                    pf = nb()
                    for h in range(4):
                        S.mm(pf[0:Lc, h * 64:(h + 1) * 64], ATT3[:, h, :], XDT[:, h * 64:(h + 1) * 64], start=True, stop=False)
                        S.mm(pf[0:Lc, h * 64:(h + 1) * 64], CT[:, h, 0:Lc], SSTb[:, h * 64:(h + 1) * 64], start=False, stop=True)
                    pg = nb()
                    if not is_s:
                        for g in range(2):
                            S.mm(pg[:, g * 128:(g + 1) * 128], BTB[:, g * 128:(g + 1) * 128], XDTW[:, g * 128:(g + 1) * 128])
                        SST3 = SST.r("p (h v) -> p h v", v=64)
                        S.tt(SST3, SST3, DECE.us(2).bc([128, 4, 64]), ALU.mult)
                        S.tt(SST, SST, pg[:, 0:256], ALU.add)
                        S.copy(SSTb, SST, eng="act")
                        if t == 15:
                            pq = nb()
                            for g in range(2):
                                S.tr(pq[:, g * 128:(g + 1) * 128], SST[:, g * 128:(g + 1) * 128], IDENT)
                            S.copy(N0, pq[:, 0:256].r("p (g n) -> p g n", n=128))
                            out_ops.append(S.dma([(ssm_p[l, 0].rearrange("(g h2) p n -> (h2 p) g n", g=2), N0)]))
                    else:
                        for g in range(2):
                            S.mm(pg[:, g * 128:(g + 1) * 128], XDTW[:, g * 128:(g + 1) * 128], BTB[:, g * 128:(g + 1) * 128])
                        DE2 = DECE.r("p (g h) -> p g h", h=2)
                        S.copy(DCOL[0:64, :], DE2[0:64, :, 0])
                        S.copy(DCOL[64:128, :], DE2[64:128, :, 1])
                        for g in range(2):
                            S.stt(N0[:, g, :], N0[:, g, :], DCOL[:, g:g + 1], pg[:, g * 128:(g + 1) * 128], ALU.mult, ALU.add)
                        out_ops.append(S.dma([(ssm_s[l, ci].rearrange("(g h2) p n -> (h2 p) g n", g=2), N0)]))
                    Y13 = Y1.r("p (h v) -> p h v", v=64)
                    S.tt(Y13, XST.r("p (h v) -> p h v", v=64), DSK[0:Lc, :].us(2).bc([Lc, 4, 64]), ALU.mult)
                    S.tt(Y1, Y1, pf[0:Lc, 0:256], ALU.add)
                    S.tt(Y1, Y1, SR, ALU.mult)
                    S.tt(EG, Y1, Y1, ALU.mult)
                    SS = SMT[0:Lc, 32:34]
                    S.rsum(SS, EG.r("p (g v) -> p g v", v=128))
                    S.act(SS, SS, AF.Ln, bias=EPS6[0:Lc], scale=1.0 / 128)
                    S.act(SS, SS, AF.Exp, scale=-0.5)
                    Y1g = Y1.r("p (g v) -> p g v", v=128)
                    S.tt(Y1g, Y1g, SS.us(2).bc([Lc, 2, 128]), ALU.mult)
                    S.tt(SR, Y1, SN[0:Lc, :], ALU.mult)
                    to_fm(SR, Lc, o0, 6)

            def tile_norm(tt_):
                cc = tt_ * 128
                xo = XNTs[tt_ % 2]
                S.act(SQ, Xv(0, 8, cc, 128), AF.Square)
                bk = nb()
                for k in range(8):
                    S.mm(bk[:, 0:128], ONESB, SQ[:, k, :], start=(k == 0), stop=(k == 7))
                S.act(RSTD, bk[:, 0:128], AF.Ln, bias=EPS6, scale=1.0 / D)
                S.act(RSTD, RSTD, AF.Exp, scale=-0.5)
                for hb in range(2):
                    S.tt(NS, Xv(hb * 4, hb * 4 + 4, cc, 128), PV[:, GROW + hb * 4:GROW + hb * 4 + 4].us(2).bc([128, 4, 128]), ALU.mult)
                    S.tt(xo[:, hb * 4:hb * 4 + 4, :], NS, RSTD.us(1).bc([128, 4, 128]), ALU.mult)

            ntile = min(NT, cfg['tmax'])
            pend_W = [[]]
            tile_norm(0)
            for t in range(ntile):
                is_s = (t == 16)
                c0 = t * 128
                XNT = XNTs[t % 2]
                MIX = MIXs[t % 2]

                if is_s:
                    PX5 = PXB.r("p (h a s c) -> p h a s c", h=2, a=2, s=8, c=24)
                    XB4 = XBC.r("p (j s c) -> p j s c", s=16, c=11)
                else:
                    PX3 = PXB[:, 0:288].r("p (a c) -> p a c", c=144)
                    XB3 = XBC[:, 0:786].r("p (j c) -> p j c", c=131)
                GQ, GK, GSP = G_F[3][:, 0:2, :], G_F[3][:, 2:4, :], G_F[4][:, 0:2, :]
                HQ, HK, HS = G_F[3][:, 0:2, :], G_F[3][:, 2:4, :], G_F[4][:, 0:2, :]
                st_P = S.stream()
                lst_P = st_P.__enter__()
                bank_set[0] = [6]
                F, FB, TT, TB, SMT = P_F, None, P_TT, None, None
                if is_s and cfg["pool"]:
                    for hh in range(2):
                        hp = TT[hh][0:120, 0:256]
                        S.dma([(hp, state_pool[l, hh * 8:(hh + 1) * 8].rearrange("s r c -> (s r) c"))])
                        bk = nb()
                        for pt in range(2):
                            S.tr(bk[:, pt * 120:(pt + 1) * 120], hp[:, pt * 128:(pt + 1) * 128], IDENT[0:120, 0:120])
                        for pt in range(2):
                            S.copy(PX5[:, hh, pt, :, 1:16],
                                   bk[:, pt * 120:(pt + 1) * 120].r("p (s r) -> p s r", r=15))
                p1 = nb()
                if cfg["pool"]:
                    for j in range(2):
                        proj_fm(p1[:, j * 128:(j + 1) * 128], WIN[:, :, C_PX + j * 128:C_PX + (j + 1) * 128])
                    if is_s:
                        for pt in range(2):
                            for hh in range(2):
                                S.copy(PX5[:, hh, pt, :, 16:24],
                                       p1[:, pt * 128 + hh * 64:pt * 128 + (hh + 1) * 64].r("p (s c) -> p s c", c=8), eng="act")
                    else:
                        S.copy(PX3[:, :, 16:144], p1[:, 0:256].r("p (a c) -> p a c", c=128), eng="act")

                if cfg["pool"]:
                    for hh in ((0, 1) if is_s else (None,)):
                        if is_s:
                            NSq, LEN = 8, 24
                            Xp = PXB[:, hh * 384:(hh + 1) * 384].r("p (a c) -> p a c", c=LEN)
                            dcol = slice(hh * 64, (hh + 1) * 64)
                        else:
                            NSq, LEN = 1, 144
                            Xp = PXB[:, 0:288].r("p (a c) -> p a c", c=LEN)
                            dcol = slice(0, 128)
                        nel = 2 * NSq * LEN
                        Sa = SA[:, 0:nel].r("p (a c) -> p a c", c=LEN)
                        Sb_ = SBb[:, 0:nel].r("p (a c) -> p a c", c=LEN)
                        n = LEN - 16

                        def dgrp(Sw, gi):
                            pt, g2 = gi // 2, gi % 2
                            r0 = g2 * 64
                            w = (2, 4, 8, 16)[gi]
                            src = Sw[r0:r0 + 64, pt * NSq:(pt + 1) * NSq, 16:LEN]
                            xx = Xp[r0:r0 + 64, pt * NSq:(pt + 1) * NSq, 16:LEN]
                            dst = DPL[r0:r0 + 64, pt, dcol].r("p (s c) -> p s c", c=n)
                            tmp = PT[r0:r0 + 64, pt, dcol].r("p (s c) -> p s c", c=n)
                            if t == 0:
                                S.tt(tmp, src, INVT[r0:r0 + 64, pt, :].r("p (s c) -> p s c", c=n), ALU.mult, eng="pool")
                                S.tt(dst, tmp, xx, ALU.subtract, eng="pool")
                            else:
                                S.stt(dst, src, 1.0 / w, xx, ALU.mult, ALU.subtract)

                        S.tt(Sa[:, :, 1:LEN], Xp[:, :, 1:LEN], Xp[:, :, 0:LEN - 1], ALU.add, eng="pool")
                        dgrp(Sa, 0)
                        S.tt(Sb_[:, :, 3:LEN], Sa[:, :, 3:LEN], Sa[:, :, 1:LEN - 2], ALU.add, eng="pool")
                        dgrp(Sb_, 1)
                        S.tt(Sa[:, :, 7:LEN], Sb_[:, :, 7:LEN], Sb_[:, :, 3:LEN - 4], ALU.add, eng="pool")
                        dgrp(Sa, 2)
                        S.tt(Sb_[:, :, 15:LEN], Sa[:, :, 15:LEN], Sa[:, :, 7:LEN - 8], ALU.add, eng="pool")
                        dgrp(Sb_, 3)
                    bk = nb()
                    for pt in range(2):
                        S.mm(bk[:, pt * 128:(pt + 1) * 128], PWBD[:, pt, :], DPL[:, pt, :])
                    S.tt(MIX[0], bk[:, 0:256].r("p (a c) -> p a c", c=128), PSC.us(2).bc([128, 2, 128]), ALU.mult)
                    if not is_s:
                        S.copy(PX3[:, :, 0:16], PX3[:, :, 128:144], eng="pool")
                    if t >= 15:
                        bk = nb()
                        proj_tm(bk[:, 0:256], 0, 128, C_PX, 256)
                        PXT = TT[3][:, 0:256]
                        S.copy(PXT, bk[:, 0:256], eng="act")
                        if t == 15:
                            out_ops.append(S.dma([(pool_p[l, 0], PXT[113:128, :])]))
                        else:
                            out_ops.append(S.dma([(pool_s[l, s, 7:15, :], PXT[8 * s:8 * s + 8, :]) for s in range(NSEQ)]))
                            out_ops.append(S.dma([(pool_s[l, :, 0:7, :], state_pool[l, :, 8:15, :])]))
                else:
                    S.memset(MIX[0], 0.0)
                st_P.__exit__(None, None, None)

                st_G = S.stream()
                lst_G = st_G.__enter__()
                bank_set[0] = [0, 1]
                F, FB, TT, TB, SMT = G_F, G_FB, G_TT, G_TB, G_SMT
                if cfg["gla"]:
                    p1g = nb()
                    for j in range(2):
                        proj_fm(p1g[:, j * 128:(j + 1) * 128], WQK[:, :, j * 128:(j + 1) * 128])
                    S.act(GQ, p1g[:, 0:256].r("p (a c) -> p a c", c=128), AF.Identity, scale=32.0 ** -0.5)
                    p2 = nb()
                    for j in range(2):
                        proj_fm(p2[:, j * 128:(j + 1) * 128], WQK[:, :, 256 + j * 128:256 + (j + 1) * 128])
                    proj_fm(p2[:, 256:384], WIN[:, :, C_GLR:C_GLR + 128])
                    S.copy(GK, p2[:, 0:256].r("p (a c) -> p a c", c=128), eng="act")
                    GLR = G_FB[2][0:16, 0, :]
                    S.copy(GLR, p2[0:16, 256:384])
                    p3 = nb()
                    for pt in range(2):
                        S.mm(p3[:, pt * 128:(pt + 1) * 128], GWG[0:16, pt * 128:(pt + 1) * 128], GLR)
                    for pt in range(2):
                        S.act(GSP[:, pt, :], p3[:, pt * 128:(pt + 1) * 128], AF.Exp, bias=NGB[:, l, pt:pt + 1], scale=-1.0)
                    S.act(GSP, GSP, AF.Ln, bias=ONE1)


                    def gl_in(s):
                        S.dma([(GLS[h2 * 64:h2 * 64 + 32, :, :], state_gla[l, s, h2::2].rearrange("t k v -> k t v")) for h2 in range(2)])
                        for h2 in range(2):
                            S.copy(GLSb[h2 * 64:h2 * 64 + 64, :, h2 * 64:h2 * 64 + 64], GLS[h2 * 64:h2 * 64 + 64, :, :], eng="act")

                    def gl_out(s):
                        out_ops.append(S.dma([(gla_s[l, s, h2::2].rearrange("t k v -> k t v"), GLS[h2 * 64:h2 * 64 + 32, :, :]) for h2 in range(2)]))

                    def gl_pout():
                        out_ops.append(S.dma([(gla_p[l, 0, h2::2].rearrange("t k v -> k t v"), GLS[h2 * 64:h2 * 64 + 32, :, :]) for h2 in range(2)]))

                    gla_like("gla", t, GQ, GK, GSP, -1.0 / 16, C_GV, GLS, GLSb, GN, 2, 8 if is_s else 128, gl_in, gl_out, gl_pout)
                else:
                    S.memset(MIX[1], 0.0)

                if cfg["hgrn"]:
                    p4 = nb()
                    for j in range(4):
                        proj_fm(p4[:, j * 128:(j + 1) * 128], WIN[:, :, C_RQ + j * 128:C_RQ + (j + 1) * 128])
                    p43 = p4.r("p (a c) -> p a c", c=128)
                    E4, R4 = F[0], F[1]
                    S.act(E4, p43, AF.Exp, scale=-1.0)
                    S.act(R4, E4, AF.Ln, bias=ONE1)
                    S.act(R4, R4, AF.Exp, scale=-1.0)
                    S.tt(HQ, p43[:, 0:2, :], R4[:, 0:2, :], ALU.mult)
                    for pt in range(2):
                        S.act(HS[:, pt, :], R4[:, 2 + pt, :], AF.Ln, bias=LBv[:, l, pt:pt + 1], scale=OMLv[:, l, pt:pt + 1])
                        S.stt(HK[:, pt, :], E4[:, 2 + pt, :], OMLv[:, l, pt:pt + 1], R4[:, 2 + pt, :], ALU.mult, ALU.mult)

                    def hg_in(s):
                        S.dma([(HGS, state_hgrn[l, s].rearrange("(t h2) k v -> (h2 k) t v", t=2))])
                        for h2 in range(2):
                            S.copy(HGSb[h2 * 64:h2 * 64 + 64, :, h2 * 64:h2 * 64 + 64], HGS[h2 * 64:h2 * 64 + 64, :, :], eng="act")

                    def hg_out(s):
                        out_ops.append(S.dma([(hg_s[l, s].rearrange("(t h2) k v -> (h2 k) t v", t=2), HGS)]))

                    def hg_pout():
                        out_ops.append(S.dma([(hg_p[l, 0].rearrange("(t h2) k v -> (h2 k) t v", t=2), HGS)]))

                    gla_like("hgrn", t, HQ, HK, HS, 1.0, C_RI, HGS, HGSb, HN, 4, 8 if is_s else 128, hg_in, hg_out, hg_pout,
                             Ls=(8 if is_s else 64))
                else:
                    S.memset(MIX[2], 0.0)
                st_G.__exit__(None, None, None)

                st_S = S.stream()
                lst_S = st_S.__enter__()
                bank_set[0] = [3, 4, 5]
                F, FB, TT, TB, SMT = S_F, S_FB, S_TT, S_TB, S_SMT
                if is_s and cfg["ssd"]:
                    HCa, HCb = UAf[0:48, 0:512], DDf[0:48, 0:256]
                    scv = state_conv[l].rearrange("s r c -> (s r) c")
                    S.dma([(HCa, scv[:, 0:512]), (HCb, scv[:, 512:768])])
                    for half in range(2):
                        bk = nb()
                        for jj in range(3):
                            j = half * 3 + jj
                            hsrc = HCa[:, j * 128:(j + 1) * 128] if j < 4 else HCb[:, (j - 4) * 128:(j - 3) * 128]
                            S.tr(bk[:, jj * 48:(jj + 1) * 48], hsrc, IDENT[0:48, 0:48])
                        for jj in range(3):
                            j = half * 3 + jj
                            S.copy(XB4[:, j, :, 0:3], bk[:, jj * 48:(jj + 1) * 48].r("p (s r) -> p s r", r=3))

                if cfg["ssd"]:
                    p5, p6 = nb(), nb()
                    for j in range(6):
                        dstb = p5[:, j * 128:(j + 1) * 128] if j < 4 else p6[:, (j - 4) * 128:(j - 3) * 128]
                        proj_fm(dstb, WIN[:, :, C_XBC + j * 128:C_XBC + (j + 1) * 128])
                    if is_s:
                        for j in range(6):
                            srcb = p5[:, j * 128:(j + 1) * 128] if j < 4 else p6[:, (j - 4) * 128:(j - 3) * 128]
                            S.copy(XB4[:, j, :, 3:11], srcb.r("p (s c) -> p s c", c=8), eng=("act" if j % 2 else "dve"))
                        NSq, LEN = 16, 11
                    else:
                        S.copy(XB3[:, 0:4, 3:131], p5.r("p (a c) -> p a c", c=128), eng="act")
                        S.copy(XB3[:, 4:6, 3:131], p6[:, 0:256].r("p (a c) -> p a c", c=128))
                        NSq, LEN = 1, 131
                    n = LEN - 3
                    for j in range(6):
                        xb = XBC[:, j * NSq * LEN:(j + 1) * NSq * LEN].r("p (s c) -> p s c", c=LEN)
                        cv = CV[:, j, :].r("p (s c) -> p s c", c=n)
                        S.ts2(cv, xb[:, :, 0:n], CW[:, 0, j:j + 1], CB[:, j:j + 1], ALU.mult, ALU.add)
                        for d in range(1, 4):
                            S.stt(cv, xb[:, :, d:d + n], CW[:, d, j:j + 1], cv, ALU.mult, ALU.add)
                    EA, EBf = UAf, DDf
                    CVf = CV.r("p a b -> p (a b)")
                    for hh, Eh in enumerate((EA, EBf)):
                        cvh = CVf[:, hh * 384:(hh + 1) * 384]
                        eh = Eh[:, 0:384]
                        S.act(eh, cvh, AF.Exp, scale=-1.0)
                        S.act(eh, eh, AF.Ln, bias=ONE1)
                        S.act(eh, eh, AF.Exp, scale=-1.0)
                        S.tt(cvh, cvh, eh, ALU.mult)
                    if not is_s:
                        S.copy(XB3[:, :, 0:3], XB3[:, :, 128:131], eng="pool")
                    if t >= 15:
                        bk1, bk2 = nb(), nb()
                        proj_tm(bk1[:, 0:512], 0, 128, C_XBC, 512)
                        proj_tm(bk2[:, 0:256], 0, 128, C_XBC + 512, 256)
                        XBTa, XBTb = UAf, DDf[:, 0:256]
                        S.copy(XBTa, bk1, eng="act")
                        S.copy(XBTb, bk2[:, 0:256], eng="act")
                        if t == 15:
                            out_ops.append(S.dma([(conv_p[l, 0, :, 0:512], XBTa[125:128, :]), (conv_p[l, 0, :, 512:768], XBTb[125:128, :])]))
                        else:
                            out_ops.append(S.dma([(conv_s[l, s, :, 0:512], XBTa[8 * s + 5:8 * s + 8, :]) for s in range(NSEQ)]))
                            out_ops.append(S.dma([(conv_s[l, s, :, 512:768], XBTb[8 * s + 5:8 * s + 8, :]) for s in range(NSEQ)]))
                    ssd(t, 8 if is_s else 128)
                else:
                    S.memset(MIX[3], 0.0)
                st_S.__exit__(None, None, None)
                lst_N = []
                if t + 1 < ntile:
                    with S.stream() as lst_N:
                        bank_set[0] = [2]
                        tile_norm(t + 1)

                lst_W = pend_W[0]
                pend_W[0] = []
                bank_set[0] = list(range(8))
                S.merge([lst_P, lst_G, lst_S, lst_N, lst_W], weights=MERGE_W)
                with S.stream() as lw:
                    bank_set[0] = [7]
                    mixt = MIXs[t % 2]
                    for hb in range(2):
                        bk = nb()
                        for j in range(4):
                            m = hb * 4 + j
                            for c in range(8):
                                S.mm(bk[:, j * 128:(j + 1) * 128], WOUT[:, c, m * 128:(m + 1) * 128], mixt[c // 2][:, c % 2, :], start=(c == 0), stop=(c == 7))
                        xv = Xv(hb * 4, hb * 4 + 4, c0, 128)
                        S.tt(xv, xv, bk.r("p (a c) -> p a c", c=128), ALU.add)
                pend_W[0] = lw
                bank_set[0] = list(range(8))
            S.merge([pend_W[0]])
            barrier()

        for l in range(cfg["layers"]):
            if cfg["ffn1"]:
                ffn(l, 0)
            if cfg["mixer"]:
                mixer(l)
            if cfg["ffn2"]:
                ffn(l, 1)

        A = arena("fin")
        NFS = 3
        FSQ = [A.alloc(f"SQ{i}", [128, 8, 128], BF16) for i in range(NFS)]
        FRS = [A.alloc(f"RSTD{i}", [128, 128]) for i in range(NFS)]
        FXG = [[A.alloc(f"XG{i}{h}", [128, 4, 128]) for h in range(2)] for i in range(NFS)]
        FYO = [A.alloc(f"YO{i}", [128, D]) for i in range(NFS)]
        fstreams = []
        for i in range(NFS):
            with S.stream() as fl:
                bank_set[0] = [[0, 1], [2, 3], [4, 5]][i]
                SQ, RSTD, XG, yo = FSQ[i], FRS[i], FXG[i], FYO[i]
                for t in range(i, NT, NFS):
                    c0 = t * 128
                    S.act(SQ, Xv(0, 8, c0, 128), AF.Square)
                    bk = nb()
                    for k in range(8):
                        S.mm(bk[:, 0:128], ONESB, SQ[:, k, :], start=(k == 0), stop=(k == 7))
                    S.act(RSTD, bk[:, 0:128], AF.Ln, bias=EPS6, scale=1.0 / D)
                    S.act(RSTD, RSTD, AF.Exp, scale=-0.5)
                    for hb in range(2):
                        S.tt(XG[hb], Xv(hb * 4, hb * 4 + 4, c0, 128), PV[:, R_FIN + hb * 4:R_FIN + hb * 4 + 4].us(2).bc([128, 4, 128]), ALU.mult)
                        S.tt(XG[hb], XG[hb], RSTD.us(1).bc([128, 4, 128]), ALU.mult)
                        bk = nb()
                        for j in range(4):
                            S.tr(bk[:, j * 128:(j + 1) * 128], XG[hb][:, j, :], IDENT)
                        S.copy(yo[:, hb * 512:(hb + 1) * 512], bk, eng="act" if hb == 0 else "dve")
                    dst = y_p[t * 128:(t + 1) * 128, :] if t < 16 else y_s[:, :]
                    out_ops.append(S.dma([(dst, yo)]))
            fstreams.append(fl)
        bank_set[0] = list(range(8))
        S.merge(fstreams)

        S.emit(final_wait_ops=out_ops)
    return nc


def _consts():
    c = np.zeros((128, 9, 128), np.float32)
    c[:, 0, :] = np.eye(128, dtype=np.float32)
    j = np.arange(128)[:, None]
    i = np.arange(128)[None, :]
    c[:, 1, :] = (j <= i).astype(np.float32)
    c[:, 2, :] = np.where(j <= i, 0.0, -30000.0).astype(np.float32)
    for idx, L in ((3, 128), (4, 64), (5, 8)):
        c[:, idx, :] = (np.arange(128) % L != 0).astype(np.float32)[None, :]
    c[:, 8, :] = ((j <= i) & ((j // 64) == (i // 64))).astype(np.float32)
    tpos = np.arange(128)[None, :] + 1.0
    p = np.arange(128)[:, None]
    for pt in range(2):
        w = np.where(p < 64, (2, 8)[pt], (4, 16)[pt]).astype(np.float32)
        c[:, 6 + pt, :] = 1.0 / np.minimum(tpos, w)
    return c


_NC_CACHE = {}


def kernel(**inputs):
    if "nc" not in _NC_CACHE:
        _NC_CACHE["nc"] = build()
    nc = _NC_CACHE["nc"]
    f = lambda a: np.ascontiguousarray(np.asarray(a, dtype=np.float32))
    inp = {k: f(v) for k, v in inputs.items()}
    cst = _consts()
    in_maps = []
    for c in range(NCORES):
        m = {}
        for k, v in inp.items():
            if k == "x_prompt":
                m[k] = np.ascontiguousarray(v[c])
            elif k == "x_sample":
                m[k] = np.ascontiguousarray(v[NSEQ * c:NSEQ * (c + 1)].reshape(NSEQ * TSQ, D))
            elif k.startswith("state_"):
                m[k] = np.ascontiguousarray(v[:, NSEQ * c:NSEQ * (c + 1)])
            else:
                m[k] = v
        m["c_all"] = cst
        in_maps.append(m)
    res = run_bass_kernel_spmd(nc, in_maps, core_ids=list(range(NCORES)))
    R = res.results
    cat = lambda name, ax: np.concatenate([np.asarray(R[c][name]) for c in range(NCORES)], axis=ax)
    y_prompt = np.stack([np.asarray(R[c]["y_p"]) for c in range(NCORES)], axis=0)
    y_sample = np.concatenate([np.asarray(R[c]["y_s"]).reshape(NSEQ, TSQ, D) for c in range(NCORES)], axis=0)
    outs = (y_prompt, y_sample,
            cat("pool_p", 1), cat("pool_s", 1), cat("gla_p", 1), cat("gla_s", 1),
            cat("hg_p", 1), cat("hg_s", 1), cat("ssm_p", 1), cat("ssm_s", 1),
            cat("conv_p", 1), cat("conv_s", 1))
    return tuple(np.ascontiguousarray(o.astype(np.float32)) for o in outs)
```

```python
import contextlib
import numpy as np
import concourse.bass as bass
import concourse.mybir as mybir
from concourse.bass_utils import run_bass_kernel_spmd

F32 = mybir.dt.float32
BF16 = mybir.dt.bfloat16
AF = mybir.ActivationFunctionType
ALU = mybir.AluOpType
AX = mybir.AxisListType

NCORES = 8
D = 1024
DFF = 2816
NIN = 3092
TP = 2048
NSEQ = 16
TSQ = 8
T = TP + NSEQ * TSQ
NT = T // 128
EPS = 1e-6
DEPTH = 2
C_GQ, C_GK = 256, 384
NINW = NIN - 256
C_PX, C_GV, C_GR, C_GLR = 0, 256, 512, 768
C_RQ, C_RF, C_RI, C_RG, C_SZ, C_XBC, C_DT = 784, 1040, 1296, 1552, 1808, 2064, 2832


class V:
    __slots__ = ("ap", "keys")

    def __init__(self, ap, keys):
        self.ap = ap
        self.keys = tuple(keys) if isinstance(keys, list) else (keys,)

    def _mk(self, ap):
        return V(ap, list(self.keys))

    def __getitem__(self, idx):
        return self._mk(self.ap[idx])

    def v(self, fn):
        return self._mk(fn(self.ap))

    def k(self, *keys):
        return V(self.ap, list(keys))

    def r(self, pat, **kw):
        return self._mk(self.ap.rearrange(pat, **kw))

    def bc(self, shape):
        return self._mk(self.ap.broadcast_to(list(shape)))

    def us(self, axis):
        return self._mk(self.ap.unsqueeze(axis))


def _ap(x):
    return x.ap if isinstance(x, V) else x


def _keys(xs):
    ks = []
    for x in xs:
        if isinstance(x, V):
            ks.extend(x.keys)
    return ks


class Ref:
    __slots__ = ("id",)


class Sched:
    ENG = ("pe", "act", "dve", "pool", "sp")

    def __init__(self, nc, sem_epoch=12000, n_dma_sems=8):
        self.nc = nc
        self.ops = []
        self.last_w = {}
        self.readers = {}
        self.sem_epoch = sem_epoch
        self.n_dma_sems = n_dma_sems
        self.dma_open = []
        self.strict = STRICT_SAME_ENGINE
        self.cur = None

    def add(self, eng, fn, reads=(), writes=(), dma=0, extra=()):
        if POOL_AS_DVE and eng == "pool" and not dma:
            eng = "dve"
        if self.cur is not None:
            ref = Ref()
            self.cur.append((eng, fn, reads, writes, dma, extra, ref))
            return ref
        return self._add(eng, fn, reads, writes, dma, extra)

    @contextlib.contextmanager
    def stream(self):
        lst = []
        prev, self.cur = self.cur, lst
        try:
            yield lst
        finally:
            self.cur = prev

    def merge(self, streams, weights=None):
        if weights is None:
            weights = [1.0] * len(streams)
        keep = [i for i, st in enumerate(streams) if st]
        weights = [weights[i] for i in keep]
        streams = [streams[i] for i in keep]
        pos = [0] * len(streams)
        total = sum(len(st) for st in streams)
        for _ in range(total):
            bi, bv = -1, None
            for i, st in enumerate(streams):
                if pos[i] < len(st):
                    v = (pos[i] + 0.5) / len(st) * weights[i]
                    if bv is None or v < bv:
                        bi, bv = i, v
            eng, fn, reads, writes, dma, extra, ref = streams[bi][pos[bi]]
            pos[bi] += 1
            ref.id = self._add(eng, fn, reads, writes, dma, extra)

    def _add(self, eng, fn, reads=(), writes=(), dma=0, extra=()):
        rk = _keys(reads)
        wk = _keys(writes)
        if PSUM_EXCLUSIVE:
            pk = [k for k in rk if isinstance(k, tuple) and k[0] == "ps"]
            if pk:
                rk = [k for k in rk if k not in pk]
                wk = list(wk) + [k for k in pk if k not in wk]
        deps = {}
        for d in extra:
            deps[d.id if isinstance(d, Ref) else d] = True
        for k in rk:
            w = self.last_w.get(k)
            if w is not None:
                deps[w] = True
        for k in wk:
            w = self.last_w.get(k)
            if w is not None:
                deps.setdefault(w, False)
            for r in self.readers.get(k, ()):
                deps.setdefault(r, False)
        oid = len(self.ops)
        self.ops.append(dict(eng=eng, fn=fn, deps=deps, dma=dma, sig=None, needed=False))
        for k in wk:
            self.last_w[k] = oid
            self.readers[k] = []
        for k in rk:
            if k not in wk:
                lst = self.readers.setdefault(k, [])
                if not dma:
                    for j in range(len(lst)):
                        oj = self.ops[lst[j]]
                        if oj["eng"] == eng and not oj["dma"]:
                            lst[j] = oid
                            break
                    else:
                        lst.append(oid)
                else:
                    lst.append(oid)
        if dma:
            self.dma_open.append(oid)
        return oid

    def emit(self, final_wait_ops=()):
        nc = self.nc
        ops = self.ops
        final_wait_ops = [r.id if isinstance(r, Ref) else r for r in final_wait_ops]
        for i, op in enumerate(ops):
            for d, raw in op["deps"].items():
                od = ops[d]
                if od["dma"] or op["dma"]:
                    od["needed"] = True
                elif od["eng"] != op["eng"]:
                    od["needed"] = True
                elif (raw or self.strict) and op["eng"] in ("act", "dve", "pool"):
                    od["needed"] = True
        for i in final_wait_ops:
            ops[i]["needed"] = True
        stack = contextlib.ExitStack()
        semcount = [0]

        def newsem(name):
            semcount[0] += 1
            return stack.enter_context(nc.semaphore(f"{name}_{semcount[0]}"))

        with stack:
            cur = {e: None for e in self.ENG}
            cnt = {e: 0 for e in self.ENG}
            dma_sems = {}
            dma_rr = {}
            for i, op in enumerate(ops):
                e = op["eng"]
                if op["dma"]:
                    if e not in dma_sems:
                        dma_sems[e] = [[newsem("dq" + e), 0] for _ in range(self.n_dma_sems)]
                        dma_rr[e] = 0
                    slot = dma_sems[e][dma_rr[e] % self.n_dma_sems]
                    dma_rr[e] += 1
                    prev = slot[1]
                    slot[1] += 16 * op["dma"]
                    op["sig"] = (slot[0], slot[1])
                    op["dma_prev"] = (slot[0], prev)
                elif op["needed"]:
                    if cur[e] is None or cnt[e] >= self.sem_epoch:
                        cur[e] = newsem("s" + e)
                        cnt[e] = 0
                    cnt[e] += 1
                    op["sig"] = (cur[e], cnt[e])
            self.n_sems = semcount[0]
            per_eng = {e: [i for i, op in enumerate(ops) if op["eng"] == e] for e in self.ENG}

            def run_engine(e, eng):
                seen = {}

                def wait(sem, val):
                    if val <= 0:
                        return
                    key = id(sem)
                    if seen.get(key, 0) >= val:
                        return
                    seen[key] = val
                    eng.wait_ge(sem, val)

                for i in per_eng[e]:
                    op = ops[i]
                    for d, raw in op["deps"].items():
                        od = ops[d]
                        if od["sig"] is None:
                            continue
                        same = (od["eng"] == e) and not od["dma"]
                        if same and not op["dma"]:
                            if e == "pe" or not (raw or self.strict):
                                continue
                        wait(*od["sig"])
                    if op["dma"]:
                        wait(*op["dma_prev"])
                    r = op["fn"](eng)
                    if op["dma"]:
                        sem, _ = op["sig"]
                        for ins in r:
                            ins.then_inc(sem, 16)
                    elif op["sig"] is not None:
                        r.then_inc(op["sig"][0], 1)
                if e == "sp":
                    for i in final_wait_ops:
                        wait(*ops[i]["sig"])

            with nc.Block() as block:
                block.tensor(lambda eng: run_engine("pe", eng))
                block.scalar(lambda eng: run_engine("act", eng))
                block.vector(lambda eng: run_engine("dve", eng))
                block.gpsimd(lambda eng: run_engine("pool", eng))
                block.sync(lambda eng: run_engine("sp", eng))

    def mm(self, out, lhsT, rhs, start=True, stop=True):
        return self.add("pe", lambda e: e.matmul(_ap(out), _ap(lhsT), _ap(rhs), start=start, stop=stop),
                        reads=[lhsT, rhs], writes=[out])

    def tr(self, out, in_, ident):
        return self.add("pe", lambda e: e.transpose(_ap(out), _ap(in_), _ap(ident)),
                        reads=[in_, ident], writes=[out])

    def act(self, out, in_, func, bias=None, scale=1.0):
        kw = {}
        if bias is not None:
            kw["bias"] = _ap(bias)
        sc = _ap(scale)
        return self.add("act", lambda e: e.activation(_ap(out), _ap(in_), func, scale=sc, **kw),
                        reads=[in_, bias, scale], writes=[out])

    def tt(self, out, in0, in1, op, eng="dve"):
        return self.add(eng, lambda e: e.tensor_tensor(_ap(out), _ap(in0), _ap(in1), op),
                        reads=[in0, in1], writes=[out])

    def ts(self, out, in0, s1, op0, eng="dve"):
        return self.add(eng, lambda e: e.tensor_scalar(_ap(out), _ap(in0), _ap(s1), None, op0),
                        reads=[in0, s1], writes=[out])

    def ts2(self, out, in0, s1, s2, op0, op1, eng="dve"):
        a, b, c, d = _ap(out), _ap(in0), _ap(s1), _ap(s2)
        return self.add(eng, lambda e: e.tensor_scalar(a, b, c, d, op0, op1), reads=[in0, s1, s2], writes=[out])

    def stt(self, out, in0, scalar, in1, op0, op1, eng="dve"):
        return self.add(eng, lambda e: e.scalar_tensor_tensor(_ap(out), _ap(in0), _ap(scalar), _ap(in1), op0, op1),
                        reads=[in0, scalar, in1], writes=[out])

    def copy(self, out, in_, eng="dve"):
        if eng == "act":
            return self.add(eng, lambda e: e.copy(_ap(out), _ap(in_)), reads=[in_], writes=[out])
        return self.add(eng, lambda e: e.tensor_copy(_ap(out), _ap(in_)), reads=[in_], writes=[out])

    def recip(self, out, in_):
        return self.add("dve", lambda e: e.reciprocal(_ap(out), _ap(in_)), reads=[in_], writes=[out])

    def rsum(self, out, in_, eng="dve"):
        return self.add(eng, lambda e: e.reduce_sum(_ap(out), _ap(in_), AX.X), reads=[in_], writes=[out])

    def scan(self, out, d0, d1, init, op0, op1, eng="dve"):
        return self.add(eng, lambda e: e.tensor_tensor_scan(_ap(out), _ap(d0), _ap(d1), init, op0, op1),
                        reads=[d0, d1], writes=[out])

    def memset(self, out, val, eng="dve"):
        return self.add(eng, lambda e: e.memset(_ap(out), val), reads=[], writes=[out])

    def dma(self, pairs, eng="sp", reads=(), writes=()):
        rd = list(reads) + [p[1] for p in pairs]
        wr = list(writes) + [p[0] for p in pairs]

        def fn(e):
            return [e.dma_start(out=_ap(o), in_=_ap(i)) for (o, i) in pairs]
        return self.add(eng, fn, reads=rd, writes=wr, dma=len(pairs))


class Arena:
    def __init__(self, h32, hbf, nbytes, tag):
        self.h32, self.hbf, self.nbytes, self.tag = h32, hbf, nbytes, tag
        self.off = 0
        self.offs = {}

    def alloc(self, name, shape, dt=F32):
        esz = 4 if dt == F32 else 2
        n = 1
        for s in shape[1:]:
            n *= s
        nb = (n * esz + 3) // 4 * 4
        assert self.off + nb <= self.nbytes, (self.tag, name, self.off, nb, self.nbytes)
        self.offs[name] = self.off
        if dt == F32:
            ap = self.h32[:, self.off // 4: self.off // 4 + n]
        else:
            ap = self.hbf[:, self.off // 2: self.off // 2 + n]
        self.off += nb
        return self._view(ap, shape, (self.tag, name))

    def alias(self, base, shape, dt, keys):
        off = self.offs[base]
        n = 1
        for s_ in shape[1:]:
            n *= s_
        if dt == F32:
            ap = self.h32[:, off // 4: off // 4 + n]
        else:
            ap = self.hbf[:, off // 2: off // 2 + n]
        return self._view(ap, shape, keys)

    def _view(self, ap, shape, keys):
        if len(shape) == 3:
            ap = ap.rearrange("p (a b) -> p a b", b=shape[2])
        elif len(shape) == 4:
            ap = ap.rearrange("p (a b c) -> p a b c", b=shape[2], c=shape[3])
        if shape[0] < 128:
            ap = ap[0:shape[0]]
        if isinstance(keys, list):
            return V(ap, keys)
        return V(ap, [keys, self.tag] if SER_ARENA else keys)


ARENA_BYTES = 62464 + 4096
STRICT_SAME_ENGINE = True
import os as _os
SER_ARENA = bool(int(_os.environ.get('SER_ARENA', '0')))
POOL_AS_DVE = bool(int(_os.environ.get('POOL_AS_DVE', '0')))
PSUM_EXCLUSIVE = bool(int(_os.environ.get('PSUM_EXCLUSIVE', '1')))
MERGE_W = [float(x) for x in _os.environ.get('MERGE_W', '1,1,1,1,1').split(',')]
SER_PS = bool(int(_os.environ.get('SER_PS', '0')))
DEFAULT_CFG = dict(stage=99, tmax=17, layers=2, ffn1=True, mixer=True, ffn2=True, pool=True, gla=True, hgrn=True, ssd=True)


def build(cfg=None):
    cfg = dict(DEFAULT_CFG, **(cfg or {}))
    nc = bass.Bass("TRN2", target_bir_lowering=False)
    din = {}

    def inp(name, shape):
        din[name] = nc.dram_tensor(name, list(shape), F32, kind="ExternalInput").ap()
        return din[name]

    def outp(name, shape):
        return nc.dram_tensor(name, list(shape), F32, kind="ExternalOutput").ap()

    x_prompt = inp("x_prompt", [TP, D])
    x_sample = inp("x_sample", [NSEQ * TSQ, D])
    state_pool = inp("state_pool", [DEPTH, NSEQ, 15, 256])
    state_gla = inp("state_gla", [DEPTH, NSEQ, 4, 32, 64])
    state_hgrn = inp("state_hgrn", [DEPTH, NSEQ, 4, 64, 64])
    state_ssm = inp("state_ssm", [DEPTH, NSEQ, 4, 64, 128])
    state_conv = inp("state_conv", [DEPTH, NSEQ, 3, 768])
    ffn_norm = [inp("ffn1_norm", [DEPTH, D]), inp("ffn2_norm", [DEPTH, D])]
    ffn_wg = [inp("ffn1_w_gate", [DEPTH, D, DFF]), inp("ffn2_w_gate", [DEPTH, D, DFF])]
    ffn_wu = [inp("ffn1_w_up", [DEPTH, D, DFF]), inp("ffn2_w_up", [DEPTH, D, DFF])]
    ffn_wd = [inp("ffn1_w_down", [DEPTH, DFF, D]), inp("ffn2_w_down", [DEPTH, DFF, D])]
    mix_norm = inp("mix_norm", [DEPTH, D])
    w_in = inp("w_in", [DEPTH, D, NIN])
    pool_w = inp("pool_w", [DEPTH, 4, 64, 64])
    pool_scale = inp("pool_scale", [DEPTH, 256])
    gla_w_gate = inp("gla_w_gate", [DEPTH, 16, 128])
    gla_gate_bias = inp("gla_gate_bias", [DEPTH, 128])
    gla_norm = inp("gla_norm", [DEPTH, 64])
    hgrn_lb_logits = inp("hgrn_lb_logits", [DEPTH, 256])
    hgrn_norm = inp("hgrn_norm", [DEPTH, 64])
    ssm_conv_w = inp("ssm_conv_w", [DEPTH, 4, 768])
    ssm_conv_b = inp("ssm_conv_b", [DEPTH, 768])
    ssm_dt_bias = inp("ssm_dt_bias", [DEPTH, 4])
    ssm_A_log = inp("ssm_A_log", [DEPTH, 4])
    ssm_D = inp("ssm_D", [DEPTH, 4])
    ssm_norm = inp("ssm_norm", [DEPTH, 256])
    w_out = inp("w_out", [DEPTH, D, D])
    final_norm = inp("final_norm", [D])
    c_all = inp("c_all", [128, 9, 128])

    y_p = outp("y_p", [TP, D])
    y_s = outp("y_s", [NSEQ * TSQ, D])
    pool_p = outp("pool_p", [DEPTH, 1, 15, 256])
    pool_s = outp("pool_s", [DEPTH, NSEQ, 15, 256])
    gla_p = outp("gla_p", [DEPTH, 1, 4, 32, 64])
    gla_s = outp("gla_s", [DEPTH, NSEQ, 4, 32, 64])
    hg_p = outp("hg_p", [DEPTH, 1, 4, 64, 64])
    hg_s = outp("hg_s", [DEPTH, NSEQ, 4, 64, 64])
    ssm_p = outp("ssm_p", [DEPTH, 1, 4, 64, 128])
    ssm_s = outp("ssm_s", [DEPTH, NSEQ, 4, 64, 128])
    conv_p = outp("conv_p", [DEPTH, 1, 3, 768])
    conv_s = outp("conv_s", [DEPTH, NSEQ, 3, 768])

    S = Sched(nc)
    out_ops = []
    st = contextlib.ExitStack()
    with st:
        def sbt(name, shape, dt=F32):
            return st.enter_context(nc.sbuf_tensor(name, shape, dt))

        Xh = sbt("X", [128, 8, T])
        RBh = sbt("RB", [128, 8 * NINW], BF16)
        WOUTh = sbt("WOUT", [128, 8, D], BF16)
        WQKh = sbt("WQK", [128, 8, 512], BF16)
        ARh = sbt("AR", [128, ARENA_BYTES // 4])
        CONh = sbt("CON", [128, 9, 128])
        PVh = sbt("PV", [128, 128])
        MISCh = sbt("MISC", [128, 64])
        BARh = sbt("BAR", [128, 8])
        ONESBh = sbt("ONESB", [128, 128], BF16)
        ONESFh = sbt("ONESF", [128, 128])
        IDBh = sbt("IDB", [128, 128], BF16)
        PSh = [st.enter_context(nc.psum_tensor(f"ps{i}", [128, 512], F32)) for i in range(8)]

        ARbf = ARh.bitcast(BF16)

        def Xv(k0, k1, c0, n):
            keys = [("X", t) for t in range(c0 // 128, (c0 + n + 127) // 128)]
            return V(Xh[:, k0:k1, c0:c0 + n], keys)

        XNap = RBh[:, 0:8 * T].rearrange("p (k t) -> p k t", t=T)

        def XNv(k, c0, n):
            keys = [("XN", t) for t in range(c0 // 128, (c0 + n + 127) // 128)]
            return V(XNap[:, k, c0:c0 + n], keys)

        WIN = V(RBh[:, :].rearrange("p (k n) -> p k n", n=NINW), "WIN")
        WOUT = V(WOUTh[:], "WOUT")
        WQK = V(WQKh[:], "WQK")
        CON = V(CONh[:], "CON")
        IDENT = CON[:, 0, :]
        UF = CON[:, 1, :]
        NEGM = CON[:, 2, :]
        RMS = {128: CON[:, 3, :], 64: CON[:, 4, :], 8: CON[:, 5, :]}
        INVT = CON[:, 6:8, :]
        BD64 = CON[:, 8, :]
        PV = V(PVh[:], "PV")
        MISC = V(MISCh[:], "MISC")
        ONESB = V(ONESBh[:], "ONESB")
        ONESF = V(ONESFh[:], "ONESF")
        IDB = V(IDBh[:], "IDB")
        EPS6 = MISC[:, 0:1]
        ONE1 = MISC[:, 1:2]
        LBv = MISC[:, 4:8].r("p (l t) -> p l t", t=2)
        OMLv = MISC[:, 8:12].r("p (l t) -> p l t", t=2)
        NGB = MISC[:, 12:16].r("p (l t) -> p l t", t=2)
        GBraw = MISC[:, 16:20].r("p (l t) -> p l t", t=2)
        BARS = V(BARh[:], "BAR")
        PS = [V(PSh[i][:], [("ps", i), "ps"] if SER_PS else ("ps", i)) for i in range(8)]
        bank_set = [list(range(8))]
        psn = {}

        def nb():
            bs = bank_set[0]
            k = tuple(bs)
            i = psn.get(k, 0)
            psn[k] = i + 1
            return PS[bs[i % len(bs)]]

        barn = [0]

        def barrier():
            n = barn[0]
            barn[0] += 1
            marks = []
            marks.append(S.add("pe", lambda e: e.matmul(PSh[7][:, 0:1], ONESBh[:, 0:128], ONESBh[:, 0:1], start=True, stop=True),
                               reads=[ONESB], writes=[PS[7]]))
            marks.append(S.add("act", lambda e: e.copy(BARh[:, 0:1], MISCh[:, 0:1]), reads=[MISC], writes=[BARS[:, 0:1].k(("bar", "act"))]))
            marks.append(S.add("dve", lambda e: e.memset(BARh[:, 1:2], 0.0), writes=[BARS[:, 1:2].k(("bar", "dve"))]))
            marks.append(S.add("pool", lambda e: e.memset(BARh[:, 2:3], 0.0), writes=[BARS[:, 2:3].k(("bar", "pool"))]))
            ext = marks + [d for d in S.dma_open]
            S.dma_open = []
            S.add("pe", lambda e: e.matmul(PSh[7][:, 0:1], ONESBh[:, 0:128], ONESBh[:, 0:1], start=True, stop=True),
                  reads=[ONESB], writes=[PS[7]], extra=ext)
            S.add("act", lambda e: e.copy(BARh[:, 3:4], MISCh[:, 0:1]), reads=[MISC], writes=[BARS[:, 3:4].k(("bar2", "act"))], extra=ext)
            S.add("dve", lambda e: e.memset(BARh[:, 4:5], 0.0), writes=[BARS[:, 4:5].k(("bar2", "dve"))], extra=ext)
            S.add("pool", lambda e: e.memset(BARh[:, 5:6], 0.0), writes=[BARS[:, 5:6].k(("bar2", "pool"))], extra=ext)
            S.add("sp", lambda e: e.nop(), extra=ext)

        def arena(tag):
            return Arena(ARh, ARbf, ARENA_BYTES, tag)

        S.dma([(CON, c_all)])
        S.memset(MISC, 0.0)
        S.memset(EPS6, EPS)
        S.memset(ONE1, 1.0)
        S.memset(ONESB, 1.0)
        S.memset(ONESF, 1.0)
        S.copy(IDB, IDENT)
        A0 = arena("setup")
        RAW = A0.alloc("RAW", [128, 128])
        S.memset(RAW, 0.0)
        rows = []
        R_FFN = {}
        r = 0
        for l in range(DEPTH):
            for w in range(2):
                rows.append((RAW[r:r + 8, :], ffn_norm[w][l].rearrange("(k p) -> k p", p=128)))
                R_FFN[(l, w)] = r
                r += 8
        R_MIX = {}
        for l in range(DEPTH):
            rows.append((RAW[r:r + 8, :], mix_norm[l].rearrange("(k p) -> k p", p=128)))
            R_MIX[l] = r
            r += 8
        rows.append((RAW[r:r + 8, :], final_norm.rearrange("(k p) -> k p", p=128)))
        R_FIN = r
        r += 8
        R_PSC = r
        rows.append((RAW[r:r + 4, :], pool_scale.rearrange("l (t p) -> (l t) p", p=128)))
        r += 4
        R_CB = r
        rows.append((RAW[r:r + 12, :], ssm_conv_b.rearrange("l (j p) -> (l j) p", p=128)))
        r += 12
        R_CW = r
        rows.append((RAW[r:r + 48, :], ssm_conv_w.rearrange("l d (j p) -> (l d j) p", p=128)))
        r += 48
        R_LB = r
        rows.append((RAW[r:r + 4, :], hgrn_lb_logits.rearrange("l (t p) -> (l t) p", p=128)))
        r += 4
        assert r <= 128
        S.dma(rows)
        bk = nb()
        S.tr(bk[:, 0:128], RAW, IDENT)
        S.copy(PV, bk[:, 0:128])
        gbp = []
        for l in range(DEPTH):
            for h in range(4):
                pt, h2 = h // 2, h % 2
                gbp.append((GBraw[h2 * 64:h2 * 64 + 32, l, pt:pt + 1],
                            gla_gate_bias[l, h * 32:(h + 1) * 32].rearrange("(p o) -> p o", o=1)))
        S.dma(gbp)
        S.ts(NGB, GBraw, -1.0, ALU.mult)
        S.memset(LBv[:, 0, :], 0.0)
        S.memset(OMLv[:, 0, :], 1.0)
        TMPL = MISC[:, 20:22]
        S.tt(TMPL, PV[:, R_LB:R_LB + 2], PV[:, R_LB + 2:R_LB + 4], ALU.subtract)
        S.act(TMPL, TMPL, AF.Exp)
        S.ts(TMPL, TMPL, 1.0, ALU.add)
        S.recip(LBv[:, 1, :], TMPL)
        S.stt(OMLv[:, 1, :], LBv[:, 1, :], -1.0, ONE1.bc([128, 2]), ALU.mult, ALU.add)

        STG = [A0.alloc(f"STG{i}", [128, D]) for i in range(4)]
        for t in range(NT):
            src = x_prompt[t * 128:(t + 1) * 128, :] if t < 16 else x_sample[:, :]
            sg = STG[t % 4]
            S.dma([(sg, src)])
            for hb in range(2):
                bk = nb()
                for j in range(4):
                    k = hb * 4 + j
                    S.tr(bk[:, j * 128:(j + 1) * 128], sg[:, k * 128:(k + 1) * 128], IDENT)
                S.copy(Xv(hb * 4, hb * 4 + 4, t * 128, 128), bk.r("p (a b) -> p a b", b=128),
                       eng="act" if hb == 0 else "dve")
        barrier()

        TGS = [(0, 512), (512, 512), (1024, 512), (1536, 512), (2048, 128)]

        def norm_all(SQ, LNT, RSTD, grow):
            for (c0, n) in TGS:
                S.act(SQ[:, :, 0:n], Xv(0, 8, c0, n), AF.Square)
                bk = nb()
                for k in range(8):
                    S.mm(bk[:, 0:n], ONESB, SQ[:, k, 0:n], start=(k == 0), stop=(k == 7))
                S.act(LNT[:, 0:n], bk[:, 0:n], AF.Ln, bias=EPS6, scale=1.0 / D)
                S.act(RSTD[:, 0:n], LNT[:, 0:n], AF.Exp, scale=-0.5)
                for k in range(8):
                    S.stt(XNv(k, c0, n), Xv(k, k + 1, c0, n)[:, 0, :], PV[:, grow + k:grow + k + 1],
                          RSTD[:, 0:n], ALU.mult, ALU.mult)

        GRP = [(g * 512, 4) for g in range(5)] + [(2560, 2)]

        def ffn(l, w):
            A = arena(f"ffn{l}{w}")
            WG = [A.alloc(f"WG{i}", [128, 8, 512], BF16) for i in range(2)]
            WU = [A.alloc(f"WU{i}", [128, 8, 512], BF16) for i in range(2)]
            WD = [A.alloc(f"WD{i}", [128, 4, D], BF16) for i in range(2)]
            wg_d = ffn_wg[w][l].rearrange("(k p) n -> p k n", p=128)
            wu_d = ffn_wu[w][l].rearrange("(k p) n -> p k n", p=128)
            wd_d = ffn_wd[w][l]

            def load(g):
                c0, nch = GRP[g]
                s = g % 2
                ncol = nch * 128
                S.dma([(WG[s][:, :, 0:ncol], wg_d[:, :, c0:c0 + ncol]),
                       (WU[s][:, :, 0:ncol], wu_d[:, :, c0:c0 + ncol]),
                       (WD[s][:, 0:nch, :], wd_d[c0:c0 + ncol, :].rearrange("(c p) n -> p c n", p=128))],
                      eng="pool")

            HBall = A.alloc("HB", [128, 8, 512], BF16)
            HB = [HBall[:, 4 * i:4 * i + 4, :].k((A.tag, "HB%d" % i)) for i in range(2)]
            SGT = [A.alloc(f"SG{i}", [128, 512]) for i in range(2)]
            load(0)
            norm_all(HBall.k((A.tag, "HB0"), (A.tag, "HB1")), SGT[0], SGT[1], R_FFN[(l, w)])
            for g in range(len(GRP)):
                if g + 1 < len(GRP):
                    load(g + 1)
                c0g, nch = GRP[g]
                s = g % 2
                for tgi, (c0, n) in enumerate(TGS):
                    hb = HB[tgi % 2]
                    for c in range(nch):
                        pg = nb()
                        pu = nb()
                        for k in range(8):
                            S.mm(pg[:, 0:n], WG[s][:, k, c * 128:(c + 1) * 128], XNv(k, c0, n), start=(k == 0), stop=(k == 7))
                        for k in range(8):
                            S.mm(pu[:, 0:n], WU[s][:, k, c * 128:(c + 1) * 128], XNv(k, c0, n), start=(k == 0), stop=(k == 7))
                        sgt = SGT[c % 2]
                        S.act(sgt[:, 0:n], pg[:, 0:n], AF.Silu)
                        S.tt(hb[:, c, 0:n], sgt[:, 0:n], pu[:, 0:n], ALU.mult)
                    for m in range(8):
                        py = nb()
                        for c in range(nch):
                            S.mm(py[:, 0:n], WD[s][:, c, m * 128:(m + 1) * 128], hb[:, c, 0:n], start=(c == 0), stop=(c == nch - 1))
                        xv = Xv(m, m + 1, c0, n)[:, 0, :]
                        S.stt(xv, py[:, 0:n], 0.5, xv, ALU.mult, ALU.add)
            barrier()

        def mixer(l):
            A = arena(f"mix{l}")
            for k in range(8):
                S.dma([(WIN[:, k, 0:256], w_in[l, k * 128:(k + 1) * 128, 0:256]),
                       (WIN[:, k, 256:NINW], w_in[l, k * 128:(k + 1) * 128, 512:NIN])], eng="pool")
            S.memset(WQK, 0.0, eng="pool")
            for k in range(8):
                S.dma([(WQK[:, k, 0:256].r("p (h c) -> p h c", c=64)[:, :, 0:32],
                        w_in[l, k * 128:(k + 1) * 128, C_GQ:C_GQ + 128].rearrange("p (h c) -> p h c", c=32)),
                       (WQK[:, k, 256:512].r("p (h c) -> p h c", c=64)[:, :, 0:32],
                        w_in[l, k * 128:(k + 1) * 128, C_GK:C_GK + 128].rearrange("p (h c) -> p h c", c=32))],
                      eng="pool")
            S.dma([(WOUT, w_out[l].rearrange("(k p) n -> p k n", p=128))], eng="pool")
            PWBD = A.alloc("PWBD", [128, 2, 128], BF16)
            S.memset(PWBD, 0.0)
            S.dma([(PWBD[g2 * 64:(g2 + 1) * 64, pt, g2 * 64:(g2 + 1) * 64], pool_w[l, 2 * pt + g2])
                   for pt in range(2) for g2 in range(2)], eng="pool")
            GWG = A.alloc("GWG", [128, 256], BF16)
            S.memset(GWG, 0.0)
            S.dma([(GWG[0:16, :].r("p (h c) -> p h c", c=64)[:, :, 0:32],
                    gla_w_gate[l].rearrange("p (h c) -> p h c", c=32))], eng="pool")
            GN = A.alloc("GN", [128, 64])
            HN = A.alloc("HN", [128, 64])
            SN = A.alloc("SN", [128, 256])
            SM4 = A.alloc("SM4", [128, 16])
            DTB, NEGA, DSK = SM4[:, 0:4], SM4[:, 4:8], SM4[:, 8:12]
            S.dma([(GN, gla_norm[l].partition_broadcast(128)), (HN, hgrn_norm[l].partition_broadcast(128)),
                   (SN, ssm_norm[l].partition_broadcast(128)), (DTB, ssm_dt_bias[l].partition_broadcast(128)),
                   (NEGA, ssm_A_log[l].partition_broadcast(128)), (DSK, ssm_D[l].partition_broadcast(128))])
            S.act(NEGA, NEGA, AF.Exp)
            S.ts(NEGA, NEGA, -1.0, ALU.mult)
            CW = PV[:, R_CW + l * 24:R_CW + (l + 1) * 24].r("p (d j) -> p d j", j=6)
            CB = PV[:, R_CB + l * 6:R_CB + (l + 1) * 6]
            PSC = PV[:, R_PSC + l * 2:R_PSC + (l + 1) * 2]
            GROW = R_MIX[l]

            tg = A.tag
            XNTs = [A.alloc("XNT0", [128, 8, 128], BF16), A.alloc("XNT1", [128, 8, 128], BF16)]
            XNT = XNTs[0]
            NS = A.alloc("NS", [128, 4, 128])
            SQ = A.alias("NS", [128, 8, 128], BF16, (tg, "NS"))
            RSTD = A.alloc("RSTD", [128, 128])
            MIXs = [[A.alloc(f"MIX{b}{i}", [128, 2, 128], BF16) for i in range(4)] for b in range(2)]
            MIX = MIXs[0]
            PXB = A.alloc("PXB", [128, 2 * 16 * 24])
            SA = A.alloc("SA", [128, 2 * 8 * 24])
            SBb = A.alloc("SBb", [128, 2 * 8 * 24])
            DPL = A.alloc("DPL", [128, 2, 128], BF16)
            PT = A.alloc("PT", [128, 2, 128])
            P_F = [None, None, PT]
            P_TT = [SA[:, 0:256], SBb[:, 0:256], None, PT.r("p a b -> p (a b)")]
            G_F = [A.alloc("GF0", [128, 4, 128]), A.alloc("GF1", [128, 4, 128]), A.alloc("GF2", [128, 2, 128]),
                   A.alloc("GF3", [128, 4, 128]), A.alloc("GF4", [128, 2, 128])]
            G_FB = [A.alloc("GFB0", [128, 4, 128], BF16), None, A.alloc("GFB2", [128, 1, 128], BF16), A.alloc("GFB3", [128, 4, 128], BF16)]
            G_TT = [None] + [A.alloc(f"GT{i}", [128, 256]) for i in range(1, 4)]
            G_TT[0] = G_TT[3]
            G_TB = [A.alloc("GTB0", [128, 256], BF16), A.alloc("GTB1", [128, 256], BF16), A.alloc("GTB2", [128, 512], BF16),
                    A.alloc("GTB3", [128, 256], BF16)]
            GQAB = A.alloc("GQAB", [128, 4, 128], BF16)
            SRALL = A.alias("GQAB", [128, 256], F32, (tg, "GQAB"))
            G_SMT = A.alloc("GSMT", [128, 64])
            XBC = A.alloc("XBC", [128, 6 * 16 * 11])
            CV = A.alloc("CV", [128, 6, 128])
            UA_ = A.alloc("UA", [128, 4, 128])
            DD_ = A.alloc("DDM", [128, 4, 128])
            S_F = [UA_, DD_, UA_, UA_, DD_]
            S_FB = [None, A.alloc("SFB1", [128, 4, 128], BF16), A.alloc("SFB2", [128, 4, 128], BF16)]
            S_TT = [A.alloc(f"ST{i}", [128, 256]) for i in range(4)]
            S_TB = [A.alloc("STB0", [128, 256], BF16), A.alloc("STB1", [128, 256], BF16), A.alloc("STB2", [128, 512], BF16),
                    A.alloc("STB3", [128, 256], BF16)]
            S_SMT = A.alloc("SSMT", [128, 64])
            SZALL = A.alloc("SZALL", [128, 256])
            SZTMP = A.alloc("SZTMP", [128, 256])
            SDTAT = A.alloc("SDTAT", [128, 8])
            UAf = UA_.r("p a b -> p (a b)")
            DDf = DD_.r("p a b -> p (a b)")
            GLS = A.alloc("GLS", [128, 2, 64])
            GLSb = A.alloc("GLSb", [128, 2, 128], BF16)
            HGS = A.alloc("HGS", [128, 2, 64])
            HGSb = A.alloc("HGSb", [128, 2, 128], BF16)
            SST = A.alloc("SST", [128, 256])
            SSTb = A.alloc("SSTb", [128, 256], BF16)
            N0 = A.alloc("N0", [128, 2, 128])
            for z in (GLS, GLSb, HGS, HGSb, SST, SSTb, PXB, XBC, G_FB[0], GQAB, G_TB[3]):
                S.memset(z, 0.0, eng="pool")
            F, FB, TT, TB, SMT = G_F, G_FB, G_TT, G_TB, G_SMT

            def proj_fm(dst, wv):
                for k in range(8):
                    S.mm(dst, wv[:, k, :], XNT[:, k, :], start=(k == 0), stop=(k == 7))

            def proj_tm(dst, o0, Lc, c0w, ncol):
                for k in range(8):
                    S.mm(dst, XNT[:, k, o0:o0 + Lc], WIN[:, k, c0w:c0w + ncol], start=(k == 0), stop=(k == 7))

            def silu_tm(dst, src_ps, Lc, tmp):
                S.act(tmp, src_ps, AF.Exp, scale=-1.0)
                S.act(tmp, tmp, AF.Ln, bias=ONE1[0:Lc])
                S.act(tmp, tmp, AF.Exp, scale=-1.0)
                S.tt(dst, src_ps, tmp, ALU.mult)

            def to_fm(res, Lc, o0, mc):
                bk = nb()
                for j in range(2):
                    S.tr(bk[:, j * Lc:(j + 1) * Lc], res[0:Lc, j * 128:(j + 1) * 128], IDENT[0:Lc, 0:Lc])
                S.copy(MIX[mc // 2][:, :, o0:o0 + Lc], bk[:, 0:2 * Lc].r("p (a b) -> p a b", b=Lc), eng="act")

            def gla_like(name, t, Qf, Kf, GSf, sg, c0w, Sst, Sstb, normw, mc, Lc, st_in, st_out, st_pout, Ls=None):
                is_s = (t == 16)
                Ls = Ls or Lc
                nch = 128 // Lc
                nsub = Lc // Ls
                nseg = 128 // Ls
                assert nsub in (1, 2)
                F, FB, TT, TB, SMT = G_F, G_FB, G_TT, G_TB, G_SMT
                BP, EB, ENB, DD, KH = F[0][:, 0:2, :], F[0][:, 2:4, :], F[1][:, 0:2, :], F[1][:, 2:4, :], F[2][:, 0:2, :]
                QT, KT = FB[3][:, 0:2, :], FB[3][:, 2:4, :]
                QBD = FB[0].r("p (a b) c -> p a b c", b=2)
                QTA, QTB = GQAB[:, 0:2, :], GQAB[:, 2:4, :]
                KHBB = TB[3]
                DEC = SMT[:, 0:32]
                RM = RMS[Ls]
                for pt in range(2):
                    S.scan(BP[:, pt, :], RM, GSf[:, pt, :], 0.0, ALU.mult, ALU.add)
                S.act(EB, BP, AF.Exp, scale=sg)
                S.act(ENB, BP, AF.Exp, scale=-sg)
                S.tt(QT, Qf, EB, ALU.mult)
                for h2 in range(2):
                    r0 = h2 * 64
                    S.copy(QBD[r0:r0 + 64, :, h2, :], QT[r0:r0 + 64, :, :], eng="act")
                if nsub == 2:
                    S.copy(QTA[:, :, 0:64], QT[:, :, 0:64], eng="act")
                    S.copy(QTB[:, :, 64:128], QT[:, :, 64:128], eng="act")
                S.tt(KT, Kf, ENB, ALU.mult)
                BPc = BP.r("p a (c l) -> p (a c) l", l=Ls)
                S.tt(DD.r("p a (c l) -> p (a c) l", l=Ls), BPc, BPc[:, :, Ls - 1:Ls].bc([128, 2 * nseg, Ls]), ALU.subtract)
                S.act(DD, DD, AF.Exp, scale=-sg)
                S.tt(KH, Kf, DD, ALU.mult)
                S.act(DEC[:, 0:2 * nseg].us(2), BPc[:, :, Ls - 1:Ls], AF.Exp, scale=sg)
                if is_s:
                    VBALL = TB[3]
                    EGALL = F[1][:, 2:4, :].r("p a b -> p (a b)")
                    paA = nb()
                    proj_tm(paA[:, 0:512], 0, 128, c0w, 512)
                    S.copy(VBALL, paA[:, 0:256], eng="act")
                    silu_tm(SRALL, paA[:, 256:512], 128, EGALL)
                for ci in range(nch):
                    o0 = ci * Lc
                    if is_s:
                        st_in(ci)
                    VB, KHB, ATT = TB[0][0:Lc, 0:256], TB[1][0:Lc, 0:256], TB[2][0:Lc, :]
                    EG, SR, OS, SQO = TT[0][0:Lc, 0:256], TT[1][0:Lc, 0:256], TT[2][0:Lc, 0:256], TT[3][0:Lc, 0:256]
                    pa = nb()
                    if is_s:
                        S.mm(pa[0:Lc, 0:256], IDB[:, o0:o0 + Lc], VBALL)
                        S.mm(pa[0:Lc, 256:512], IDENT[:, o0:o0 + Lc], SRALL)
                        S.copy(VB, pa[0:Lc, 0:256], eng="act")
                        S.copy(SR, pa[0:Lc, 256:512], eng="act")
                    else:
                        proj_tm(pa[0:Lc, 0:512], o0, Lc, c0w, 512)
                        S.copy(VB, pa[0:Lc, 0:256], eng="act")
                        silu_tm(SR, pa[0:Lc, 256:512], Lc, EG)
                    pb = nb()
                    for pt in range(2):
                        S.tr(pb[0:Lc, pt * 128:(pt + 1) * 128], KH[:, pt, o0:o0 + Lc], IDENT)
                    S.copy(KHB, pb[0:Lc, 0:256], eng="act")
                    if nsub == 2:
                        S.copy(KHBB[64:128, :], pb[64:128, 0:256], eng="act")
                    pc = nb()
                    for pt in range(2):
                        S.mm(pc[0:Lc, pt * 2 * Lc:(pt + 1) * 2 * Lc].r("p (a b) -> p a b", b=Lc), KT[:, pt, o0:o0 + Lc], QBD[:, pt, :, o0:o0 + Lc])
                    ATT3 = ATT[:, 0:4 * Lc].r("p (h l) -> p h l", l=Lc)
                    msk = UF[0:Lc, 0:Lc] if nsub == 1 else BD64
                    S.tt(ATT3, pc[0:Lc, 0:4 * Lc].r("p (h l) -> p h l", l=Lc), msk.us(1).bc([Lc, 4, Lc]), ALU.mult)
                    pd = nb()
                    for h in range(4):
                        S.mm(pd[0:Lc, h * 64:(h + 1) * 64], ATT3[:, h, :], VB[:, h * 64:(h + 1) * 64])
                    S.copy(OS, pd[0:Lc, 0:256], eng="act")
                    for si in range(nsub):
                        seg = ci * nsub + si
                        qm = QT if nsub == 1 else (QTA, QTB)[si]
                        pd2 = nb()
                        for pt in range(2):
                            S.mm(pd2[0:Lc, pt * 128:(pt + 1) * 128], qm[:, pt, o0:o0 + Lc], Sstb[:, pt, :])
                        S.tt(OS, OS, pd2[0:Lc, 0:256], ALU.add)
                        pe_ = nb()
                        for pt in range(2):
                            if nsub == 1:
                                kk, vv = KHB[:, pt * 128:(pt + 1) * 128], VB[:, pt * 128:(pt + 1) * 128]
                            elif si == 0:
                                kk, vv = KHB[0:64, pt * 128:(pt + 1) * 128], VB[0:64, pt * 128:(pt + 1) * 128]
                            else:
                                kk, vv = KHBB[:, pt * 128:(pt + 1) * 128], VB[:, pt * 128:(pt + 1) * 128]
                            S.mm(pe_[:, pt * 128:(pt + 1) * 128], kk, vv)
                        for pt in range(2):
                            for h2 in range(2):
                                r0 = h2 * 64
                                sv = Sst[r0:r0 + 64, pt, :]
                                S.stt(sv, sv, DEC[r0:r0 + 64, pt * nseg + seg:pt * nseg + seg + 1],
                                      pe_[r0:r0 + 64, pt * 128 + r0:pt * 128 + r0 + 64], ALU.mult, ALU.add)
                        if is_s:
                            st_out(ci)
                        else:
                            for h2 in range(2):
                                r0 = h2 * 64
                                S.copy(Sstb[r0:r0 + 64, :, r0:r0 + 64], Sst[r0:r0 + 64, :, :], eng="act")
                            if t == 15 and ci == nch - 1 and si == nsub - 1:
                                st_pout()
                    S.tt(SQO, OS, OS, ALU.mult)
                    SS = SMT[0:Lc, 32:36]
                    S.rsum(SS, SQO.r("p (h v) -> p h v", v=64))
                    S.act(SS, SS, AF.Ln, bias=EPS6[0:Lc], scale=1.0 / 64)
                    S.act(SS, SS, AF.Exp, scale=-0.5)
                    OS3 = OS.r("p (h v) -> p h v", v=64)
                    S.tt(OS3, OS3, SS.us(2).bc([Lc, 4, 64]), ALU.mult)
                    S.tt(OS3, OS3, normw[0:Lc, :].us(1).bc([Lc, 4, 64]), ALU.mult)
                    S.tt(SQO, OS, SR, ALU.mult)
                    to_fm(SQO, Lc, o0, mc)

            def ssd(t, Lc):
                is_s = (t == 16)
                nch = 128 // Lc
                F, FB, TT, TB, SMT = S_F, S_FB, S_TT, S_TB, S_SMT
                BCB = FB[1]
                S.copy(BCB, CV[:, 2:6, :], eng="act")
                if is_s:
                    paA = nb()
                    proj_tm(paA[:, 0:256], 0, 128, C_SZ, 256)
                    proj_tm(paA[:, 256:260], 0, 128, C_DT, 4)
                    DTA, ATA = SDTAT[:, 0:4], SDTAT[:, 4:8]
                    S.tt(DTA, paA[:, 256:260], DTB, ALU.add)
                    S.act(DTA, DTA, AF.Exp)
                    S.act(DTA, DTA, AF.Ln, bias=ONE1)
                    S.tt(ATA, DTA, NEGA, ALU.mult)
                    silu_tm(SZALL, paA[:, 0:256], 128, SZTMP)
                for ci in range(nch):
                    o0 = ci * Lc
                    sl = slice(o0, o0 + Lc)
                    SM = SMT[0:Lc, 36:52]
                    DT, AT, CUMT, WE = SM[:, 0:4], SM[:, 4:8], SM[:, 8:12], SM[:, 12:16]
                    DECE = SMT[:, 52:56].k(("mixs", "DECE"))
                    DCOL = SMT[:, 56:58].k(("mixs", "DCOL"))
                    UA, DDm, EC = F[3], F[4], F[2]
                    ATT, XDT, XDTW, BTB = TB[2][0:Lc, :], TB[0][0:Lc, 0:256], TB[1][0:Lc, 0:256], TB[3][0:Lc, 0:256]
                    CT = FB[2]
                    XST, EG, SR, Y1 = TT[0][0:Lc, 0:256], TT[1][0:Lc, 0:256], TT[2][0:Lc, 0:256], TT[3][0:Lc, 0:256]
                    if is_s:
                        S.dma([(N0, state_ssm[l, ci].rearrange("(g h2) p n -> (h2 p) g n", g=2))])
                        pq = nb()
                        for g in range(2):
                            S.tr(pq[:, g * 128:(g + 1) * 128], N0[:, g, :], IDENT)
                        S.copy(SSTb, pq[:, 0:256], eng="act")
                    pa = nb()
                    if is_s:
                        S.mm(pa[0:Lc, 0:256], IDENT[:, o0:o0 + Lc], SZALL)
                        S.mm(pa[0:Lc, 256:264], IDENT[:, o0:o0 + Lc], SDTAT)
                        S.copy(SR, pa[0:Lc, 0:256], eng="act")
                        S.copy(SM[:, 0:8], pa[0:Lc, 256:264])
                    else:
                        proj_tm(pa[0:Lc, 0:256], o0, Lc, C_SZ, 256)
                        proj_tm(pa[0:Lc, 256:260], o0, Lc, C_DT, 4)
                        S.tt(DT, pa[0:Lc, 256:260], DTB[0:Lc, :], ALU.add)
                        S.act(DT, DT, AF.Exp)
                        S.act(DT, DT, AF.Ln, bias=ONE1[0:Lc])
                        S.tt(AT, DT, NEGA[0:Lc, :], ALU.mult)
                        silu_tm(SR, pa[0:Lc, 0:256], Lc, EG)
                    pb = nb()
                    S.mm(pb[0:Lc, 0:4], UF[0:Lc, 0:Lc], AT)
                    S.copy(CUMT, pb[0:Lc, 0:4])
                    UA3 = UA[0:Lc, :, 0:Lc]
                    S.tt(UA3, UF[0:Lc, 0:Lc].us(1).bc([Lc, 4, Lc]), AT.us(2).bc([Lc, 4, Lc]), ALU.mult)
                    pc = nb()
                    for h in range(4):
                        S.mm(pc[:, h * Lc:(h + 1) * Lc], ONESF[0:Lc, :], UA[0:Lc, h, 0:Lc])
                    pc3 = pc[:, 0:4 * Lc].r("p (h l) -> p h l", l=Lc)
                    for h in range(4):
                        S.stt(DDm[0:Lc, h, 0:Lc], pc3[0:Lc, h, :], CUMT[:, h:h + 1], NEGM[0:Lc, 0:Lc], ALU.subtract, ALU.add)
                    S.act(DDm[0:Lc, :, 0:Lc], DDm[0:Lc, :, 0:Lc], AF.Exp)
                    pd = nb()
                    for g in range(2):
                        S.mm(pd[0:Lc, g * Lc:(g + 1) * Lc], BCB[:, g, sl], BCB[:, 2 + g, sl])
                    ATT3 = ATT[:, 0:4 * Lc].r("p (h l) -> p h l", l=Lc)
                    for g in range(2):
                        S.tt(ATT3[:, 2 * g:2 * g + 2, :], DDm[0:Lc, 2 * g:2 * g + 2, 0:Lc],
                             pd[0:Lc, g * Lc:(g + 1) * Lc].us(1).bc([Lc, 2, Lc]), ALU.mult)
                    pe_ = nb()
                    for j in range(2):
                        S.tr(pe_[0:Lc, j * 128:(j + 1) * 128], CV[:, j, sl], IDENT)
                    for g in range(2):
                        S.tr(pe_[0:Lc, 256 + g * 128:256 + (g + 1) * 128], CV[:, 2 + g, sl], IDENT)
                    S.copy(XST, pe_[0:Lc, 0:256], eng="act")
                    S.copy(BTB, pe_[0:Lc, 256:512], eng="act")
                    XDT3 = XDT.r("p (h v) -> p h v", v=64)
                    S.tt(XDT3, XST.r("p (h v) -> p h v", v=64), DT.us(2).bc([Lc, 4, 64]), ALU.mult)
                    S.tt(WE, pc3[0:Lc, :, Lc - 1], CUMT, ALU.subtract)
                    S.act(WE, WE, AF.Exp)
                    S.tt(XDTW.r("p (h v) -> p h v", v=64), XDT3, WE.us(2).bc([Lc, 4, 64]), ALU.mult)
                    S.act(DECE, pc3[:, :, Lc - 1], AF.Exp)
                    S.act(EC[:, :, 0:Lc], pc3, AF.Exp)
                    for g in range(2):
                        S.tt(CT[:, 2 * g:2 * g + 2, 0:Lc], EC[:, 2 * g:2 * g + 2, 0:Lc],
                             BCB[:, 2 + g, sl].us(1).bc([128, 2, Lc]), ALU.mult)
                    pf = nb()
                    for h in range(4):
                        S.mm(pf[0:Lc, h * 64:(h + 1) * 64], ATT3[:, h, :], XDT[:, h * 64:(h + 1) * 64], start=True, stop=False)
                        S.mm(pf[0:Lc, h * 64:(h + 1) * 64], CT[:, h, 0:Lc], SSTb[:, h * 64:(h + 1) * 64], start=False, stop=True)
                    pg = nb()
                    if not is_s:
                        for g in range(2):
                            S.mm(pg[:, g * 128:(g + 1) * 128], BTB[:, g * 128:(g + 1) * 128], XDTW[:, g * 128:(g + 1) * 128])
                        SST3 = SST.r("p (h v) -> p h v", v=64)
                        S.tt(SST3, SST3, DECE.us(2).bc([128, 4, 64]), ALU.mult)
                        S.tt(SST, SST, pg[:, 0:256], ALU.add)
                        S.copy(SSTb, SST, eng="act")
                        if t == 15:
                            pq = nb()
                            for g in range(2):
                                S.tr(pq[:, g * 128:(g + 1) * 128], SST[:, g * 128:(g + 1) * 128], IDENT)
                            S.copy(N0, pq[:, 0:256].r("p (g n) -> p g n", n=128))
                            out_ops.append(S.dma([(ssm_p[l, 0].rearrange("(g h2) p n -> (h2 p) g n", g=2), N0)]))
                    else:
                        for g in range(2):
                            S.mm(pg[:, g * 128:(g + 1) * 128], XDTW[:, g * 128:(g + 1) * 128], BTB[:, g * 128:(g + 1) * 128])
                        DE2 = DECE.r("p (g h) -> p g h", h=2)
                        S.copy(DCOL[0:64, :], DE2[0:64, :, 0])
                        S.copy(DCOL[64:128, :], DE2[64:128, :, 1])
                        for g in range(2):
                            S.stt(N0[:, g, :], N0[:, g, :], DCOL[:, g:g + 1], pg[:, g * 128:(g + 1) * 128], ALU.mult, ALU.add)
                        out_ops.append(S.dma([(ssm_s[l, ci].rearrange("(g h2) p n -> (h2 p) g n", g=2), N0)]))
                    Y13 = Y1.r("p (h v) -> p h v", v=64)
                    S.tt(Y13, XST.r("p (h v) -> p h v", v=64), DSK[0:Lc, :].us(2).bc([Lc, 4, 64]), ALU.mult)
                    S.tt(Y1, Y1, pf[0:Lc, 0:256], ALU.add)
                    S.tt(Y1, Y1, SR, ALU.mult)
                    S.tt(EG, Y1, Y1, ALU.mult)
                    SS = SMT[0:Lc, 32:34]
                    S.rsum(SS, EG.r("p (g v) -> p g v", v=128))
                    S.act(SS, SS, AF.Ln, bias=EPS6[0:Lc], scale=1.0 / 128)
                    S.act(SS, SS, AF.Exp, scale=-0.5)
                    Y1g = Y1.r("p (g v) -> p g v", v=128)
                    S.tt(Y1g, Y1g, SS.us(2).bc([Lc, 2, 128]), ALU.mult)
                    S.tt(SR, Y1, SN[0:Lc, :], ALU.mult)
                    to_fm(SR, Lc, o0, 6)

            def tile_norm(tt_):
                cc = tt_ * 128
                xo = XNTs[tt_ % 2]
                S.act(SQ, Xv(0, 8, cc, 128), AF.Square)
                bk = nb()
                for k in range(8):
                    S.mm(bk[:, 0:128], ONESB, SQ[:, k, :], start=(k == 0), stop=(k == 7))
                S.act(RSTD, bk[:, 0:128], AF.Ln, bias=EPS6, scale=1.0 / D)
                S.act(RSTD, RSTD, AF.Exp, scale=-0.5)
                for hb in range(2):
                    S.tt(NS, Xv(hb * 4, hb * 4 + 4, cc, 128), PV[:, GROW + hb * 4:GROW + hb * 4 + 4].us(2).bc([128, 4, 128]), ALU.mult)
                    S.tt(xo[:, hb * 4:hb * 4 + 4, :], NS, RSTD.us(1).bc([128, 4, 128]), ALU.mult)

            ntile = min(NT, cfg['tmax'])
            pend_W = [[]]
            tile_norm(0)
            for t in range(ntile):
                is_s = (t == 16)
                c0 = t * 128
                XNT = XNTs[t % 2]
                MIX = MIXs[t % 2]

                if is_s:
                    PX5 = PXB.r("p (h a s c) -> p h a s c", h=2, a=2, s=8, c=24)
                    XB4 = XBC.r("p (j s c) -> p j s c", s=16, c=11)
                else:
                    PX3 = PXB[:, 0:288].r("p (a c) -> p a c", c=144)
                    XB3 = XBC[:, 0:786].r("p (j c) -> p j c", c=131)
                GQ, GK, GSP = G_F[3][:, 0:2, :], G_F[3][:, 2:4, :], G_F[4][:, 0:2, :]
                HQ, HK, HS = G_F[3][:, 0:2, :], G_F[3][:, 2:4, :], G_F[4][:, 0:2, :]
                st_P = S.stream()
                lst_P = st_P.__enter__()
                bank_set[0] = [6]
                F, FB, TT, TB, SMT = P_F, None, P_TT, None, None
                if is_s and cfg["pool"]:
                    for hh in range(2):
                        hp = TT[hh][0:120, 0:256]
                        S.dma([(hp, state_pool[l, hh * 8:(hh + 1) * 8].rearrange("s r c -> (s r) c"))])
                        bk = nb()
                        for pt in range(2):
                            S.tr(bk[:, pt * 120:(pt + 1) * 120], hp[:, pt * 128:(pt + 1) * 128], IDENT[0:120, 0:120])
                        for pt in range(2):
                            S.copy(PX5[:, hh, pt, :, 1:16],
                                   bk[:, pt * 120:(pt + 1) * 120].r("p (s r) -> p s r", r=15))
                p1 = nb()
                if cfg["pool"]:
                    for j in range(2):
                        proj_fm(p1[:, j * 128:(j + 1) * 128], WIN[:, :, C_PX + j * 128:C_PX + (j + 1) * 128])
                    if is_s:
                        for pt in range(2):
                            for hh in range(2):
                                S.copy(PX5[:, hh, pt, :, 16:24],
                                       p1[:, pt * 128 + hh * 64:pt * 128 + (hh + 1) * 64].r("p (s c) -> p s c", c=8), eng="act")
                    else:
                        S.copy(PX3[:, :, 16:144], p1[:, 0:256].r("p (a c) -> p a c", c=128), eng="act")

                if cfg["pool"]:
                    for hh in ((0, 1) if is_s else (None,)):
                        if is_s:
                            NSq, LEN = 8, 24
                            Xp = PXB[:, hh * 384:(hh + 1) * 384].r("p (a c) -> p a c", c=LEN)
                            dcol = slice(hh * 64, (hh + 1) * 64)
                        else:
                            NSq, LEN = 1, 144
                            Xp = PXB[:, 0:288].r("p (a c) -> p a c", c=LEN)
                            dcol = slice(0, 128)
                        nel = 2 * NSq * LEN
                        Sa = SA[:, 0:nel].r("p (a c) -> p a c", c=LEN)
                        Sb_ = SBb[:, 0:nel].r("p (a c) -> p a c", c=LEN)
                        n = LEN - 16

                        def dgrp(Sw, gi):
                            pt, g2 = gi // 2, gi % 2
                            r0 = g2 * 64
                            w = (2, 4, 8, 16)[gi]
                            src = Sw[r0:r0 + 64, pt * NSq:(pt + 1) * NSq, 16:LEN]
                            xx = Xp[r0:r0 + 64, pt * NSq:(pt + 1) * NSq, 16:LEN]
                            dst = DPL[r0:r0 + 64, pt, dcol].r("p (s c) -> p s c", c=n)
                            tmp = PT[r0:r0 + 64, pt, dcol].r("p (s c) -> p s c", c=n)
                            if t == 0:
                                S.tt(tmp, src, INVT[r0:r0 + 64, pt, :].r("p (s c) -> p s c", c=n), ALU.mult, eng="pool")
                                S.tt(dst, tmp, xx, ALU.subtract, eng="pool")
                            else:
                                S.stt(dst, src, 1.0 / w, xx, ALU.mult, ALU.subtract)

                        S.tt(Sa[:, :, 1:LEN], Xp[:, :, 1:LEN], Xp[:, :, 0:LEN - 1], ALU.add, eng="pool")
                        dgrp(Sa, 0)
                        S.tt(Sb_[:, :, 3:LEN], Sa[:, :, 3:LEN], Sa[:, :, 1:LEN - 2], ALU.add, eng="pool")
                        dgrp(Sb_, 1)
                        S.tt(Sa[:, :, 7:LEN], Sb_[:, :, 7:LEN], Sb_[:, :, 3:LEN - 4], ALU.add, eng="pool")
                        dgrp(Sa, 2)
                        S.tt(Sb_[:, :, 15:LEN], Sa[:, :, 15:LEN], Sa[:, :, 7:LEN - 8], ALU.add, eng="pool")
                        dgrp(Sb_, 3)
                    bk = nb()
                    for pt in range(2):
                        S.mm(bk[:, pt * 128:(pt + 1) * 128], PWBD[:, pt, :], DPL[:, pt, :])
                    S.tt(MIX[0], bk[:, 0:256].r("p (a c) -> p a c", c=128), PSC.us(2).bc([128, 2, 128]), ALU.mult)
                    if not is_s:
                        S.copy(PX3[:, :, 0:16], PX3[:, :, 128:144], eng="pool")
                    if t >= 15:
                        bk = nb()
                        proj_tm(bk[:, 0:256], 0, 128, C_PX, 256)
                        PXT = TT[3][:, 0:256]
                        S.copy(PXT, bk[:, 0:256], eng="act")
                        if t == 15:
                            out_ops.append(S.dma([(pool_p[l, 0], PXT[113:128, :])]))
                        else:
                            out_ops.append(S.dma([(pool_s[l, s, 7:15, :], PXT[8 * s:8 * s + 8, :]) for s in range(NSEQ)]))
                            out_ops.append(S.dma([(pool_s[l, :, 0:7, :], state_pool[l, :, 8:15, :])]))
                else:
                    S.memset(MIX[0], 0.0)
                st_P.__exit__(None, None, None)

                st_G = S.stream()
                lst_G = st_G.__enter__()
                bank_set[0] = [0, 1]
                F, FB, TT, TB, SMT = G_F, G_FB, G_TT, G_TB, G_SMT
                if cfg["gla"]:
                    p1g = nb()
                    for j in range(2):
                        proj_fm(p1g[:, j * 128:(j + 1) * 128], WQK[:, :, j * 128:(j + 1) * 128])
                    S.act(GQ, p1g[:, 0:256].r("p (a c) -> p a c", c=128), AF.Identity, scale=32.0 ** -0.5)
                    p2 = nb()
                    for j in range(2):
                        proj_fm(p2[:, j * 128:(j + 1) * 128], WQK[:, :, 256 + j * 128:256 + (j + 1) * 128])
                    proj_fm(p2[:, 256:384], WIN[:, :, C_GLR:C_GLR + 128])
                    S.copy(GK, p2[:, 0:256].r("p (a c) -> p a c", c=128), eng="act")
                    GLR = G_FB[2][0:16, 0, :]
                    S.copy(GLR, p2[0:16, 256:384])
                    p3 = nb()
                    for pt in range(2):
                        S.mm(p3[:, pt * 128:(pt + 1) * 128], GWG[0:16, pt * 128:(pt + 1) * 128], GLR)
                    for pt in range(2):
                        S.act(GSP[:, pt, :], p3[:, pt * 128:(pt + 1) * 128], AF.Exp, bias=NGB[:, l, pt:pt + 1], scale=-1.0)
                    S.act(GSP, GSP, AF.Ln, bias=ONE1)


                    def gl_in(s):
                        S.dma([(GLS[h2 * 64:h2 * 64 + 32, :, :], state_gla[l, s, h2::2].rearrange("t k v -> k t v")) for h2 in range(2)])
                        for h2 in range(2):
                            S.copy(GLSb[h2 * 64:h2 * 64 + 64, :, h2 * 64:h2 * 64 + 64], GLS[h2 * 64:h2 * 64 + 64, :, :], eng="act")

                    def gl_out(s):
                        out_ops.append(S.dma([(gla_s[l, s, h2::2].rearrange("t k v -> k t v"), GLS[h2 * 64:h2 * 64 + 32, :, :]) for h2 in range(2)]))

                    def gl_pout():
                        out_ops.append(S.dma([(gla_p[l, 0, h2::2].rearrange("t k v -> k t v"), GLS[h2 * 64:h2 * 64 + 32, :, :]) for h2 in range(2)]))

                    gla_like("gla", t, GQ, GK, GSP, -1.0 / 16, C_GV, GLS, GLSb, GN, 2, 8 if is_s else 128, gl_in, gl_out, gl_pout)
                else:
                    S.memset(MIX[1], 0.0)

                if cfg["hgrn"]:
                    p4 = nb()
                    for j in range(4):
                        proj_fm(p4[:, j * 128:(j + 1) * 128], WIN[:, :, C_RQ + j * 128:C_RQ + (j + 1) * 128])
                    p43 = p4.r("p (a c) -> p a c", c=128)
                    E4, R4 = F[0], F[1]
                    S.act(E4, p43, AF.Exp, scale=-1.0)
                    S.act(R4, E4, AF.Ln, bias=ONE1)
                    S.act(R4, R4, AF.Exp, scale=-1.0)
                    S.tt(HQ, p43[:, 0:2, :], R4[:, 0:2, :], ALU.mult)
                    for pt in range(2):
                        S.act(HS[:, pt, :], R4[:, 2 + pt, :], AF.Ln, bias=LBv[:, l, pt:pt + 1], scale=OMLv[:, l, pt:pt + 1])
                        S.stt(HK[:, pt, :], E4[:, 2 + pt, :], OMLv[:, l, pt:pt + 1], R4[:, 2 + pt, :], ALU.mult, ALU.mult)

                    def hg_in(s):
                        S.dma([(HGS, state_hgrn[l, s].rearrange("(t h2) k v -> (h2 k) t v", t=2))])
                        for h2 in range(2):
                            S.copy(HGSb[h2 * 64:h2 * 64 + 64, :, h2 * 64:h2 * 64 + 64], HGS[h2 * 64:h2 * 64 + 64, :, :], eng="act")

                    def hg_out(s):
                        out_ops.append(S.dma([(hg_s[l, s].rearrange("(t h2) k v -> (h2 k) t v", t=2), HGS)]))

                    def hg_pout():
                        out_ops.append(S.dma([(hg_p[l, 0].rearrange("(t h2) k v -> (h2 k) t v", t=2), HGS)]))

                    gla_like("hgrn", t, HQ, HK, HS, 1.0, C_RI, HGS, HGSb, HN, 4, 8 if is_s else 128, hg_in, hg_out, hg_pout,
                             Ls=(8 if is_s else 64))
                else:
                    S.memset(MIX[2], 0.0)
                st_G.__exit__(None, None, None)

                st_S = S.stream()
                lst_S = st_S.__enter__()
                bank_set[0] = [3, 4, 5]
                F, FB, TT, TB, SMT = S_F, S_FB, S_TT, S_TB, S_SMT
                if is_s and cfg["ssd"]:
                    HCa, HCb = UAf[0:48, 0:512], DDf[0:48, 0:256]
                    scv = state_conv[l].rearrange("s r c -> (s r) c")
                    S.dma([(HCa, scv[:, 0:512]), (HCb, scv[:, 512:768])])
                    for half in range(2):
                        bk = nb()
                        for jj in range(3):
                            j = half * 3 + jj
                            hsrc = HCa[:, j * 128:(j + 1) * 128] if j < 4 else HCb[:, (j - 4) * 128:(j - 3) * 128]
                            S.tr(bk[:, jj * 48:(jj + 1) * 48], hsrc, IDENT[0:48, 0:48])
                        for jj in range(3):
                            j = half * 3 + jj
                            S.copy(XB4[:, j, :, 0:3], bk[:, jj * 48:(jj + 1) * 48].r("p (s r) -> p s r", r=3))

                if cfg["ssd"]:
                    p5, p6 = nb(), nb()
                    for j in range(6):
                        dstb = p5[:, j * 128:(j + 1) * 128] if j < 4 else p6[:, (j - 4) * 128:(j - 3) * 128]
                        proj_fm(dstb, WIN[:, :, C_XBC + j * 128:C_XBC + (j + 1) * 128])
                    if is_s:
                        for j in range(6):
                            srcb = p5[:, j * 128:(j + 1) * 128] if j < 4 else p6[:, (j - 4) * 128:(j - 3) * 128]
                            S.copy(XB4[:, j, :, 3:11], srcb.r("p (s c) -> p s c", c=8), eng=("act" if j % 2 else "dve"))
                        NSq, LEN = 16, 11
                    else:
                        S.copy(XB3[:, 0:4, 3:131], p5.r("p (a c) -> p a c", c=128), eng="act")
                        S.copy(XB3[:, 4:6, 3:131], p6[:, 0:256].r("p (a c) -> p a c", c=128))
                        NSq, LEN = 1, 131
                    n = LEN - 3
                    for j in range(6):
                        xb = XBC[:, j * NSq * LEN:(j + 1) * NSq * LEN].r("p (s c) -> p s c", c=LEN)
                        cv = CV[:, j, :].r("p (s c) -> p s c", c=n)
                        S.ts2(cv, xb[:, :, 0:n], CW[:, 0, j:j + 1], CB[:, j:j + 1], ALU.mult, ALU.add)
                        for d in range(1, 4):
                            S.stt(cv, xb[:, :, d:d + n], CW[:, d, j:j + 1], cv, ALU.mult, ALU.add)
                    EA, EBf = UAf, DDf
                    CVf = CV.r("p a b -> p (a b)")
                    for hh, Eh in enumerate((EA, EBf)):
                        cvh = CVf[:, hh * 384:(hh + 1) * 384]
                        eh = Eh[:, 0:384]
                        S.act(eh, cvh, AF.Exp, scale=-1.0)
                        S.act(eh, eh, AF.Ln, bias=ONE1)
                        S.act(eh, eh, AF.Exp, scale=-1.0)
                        S.tt(cvh, cvh, eh, ALU.mult)
                    if not is_s:
                        S.copy(XB3[:, :, 0:3], XB3[:, :, 128:131], eng="pool")
                    if t >= 15:
                        bk1, bk2 = nb(), nb()
                        proj_tm(bk1[:, 0:512], 0, 128, C_XBC, 512)
                        proj_tm(bk2[:, 0:256], 0, 128, C_XBC + 512, 256)
                        XBTa, XBTb = UAf, DDf[:, 0:256]
                        S.copy(XBTa, bk1, eng="act")
                        S.copy(XBTb, bk2[:, 0:256], eng="act")
                        if t == 15:
                            out_ops.append(S.dma([(conv_p[l, 0, :, 0:512], XBTa[125:128, :]), (conv_p[l, 0, :, 512:768], XBTb[125:128, :])]))
                        else:
                            out_ops.append(S.dma([(conv_s[l, s, :, 0:512], XBTa[8 * s + 5:8 * s + 8, :]) for s in range(NSEQ)]))
                            out_ops.append(S.dma([(conv_s[l, s, :, 512:768], XBTb[8 * s + 5:8 * s + 8, :]) for s in range(NSEQ)]))
                    ssd(t, 8 if is_s else 128)
                else:
                    S.memset(MIX[3], 0.0)
                st_S.__exit__(None, None, None)
                lst_N = []
                if t + 1 < ntile:
                    with S.stream() as lst_N:
                        bank_set[0] = [2]
                        tile_norm(t + 1)

                lst_W = pend_W[0]
                pend_W[0] = []
                bank_set[0] = list(range(8))
                S.merge([lst_P, lst_G, lst_S, lst_N, lst_W], weights=MERGE_W)
                with S.stream() as lw:
                    bank_set[0] = [7]
                    mixt = MIXs[t % 2]
                    for hb in range(2):
                        bk = nb()
                        for j in range(4):
                            m = hb * 4 + j
                            for c in range(8):
                                S.mm(bk[:, j * 128:(j + 1) * 128], WOUT[:, c, m * 128:(m + 1) * 128], mixt[c // 2][:, c % 2, :], start=(c == 0), stop=(c == 7))
                        xv = Xv(hb * 4, hb * 4 + 4, c0, 128)
                        S.tt(xv, xv, bk.r("p (a c) -> p a c", c=128), ALU.add)
                pend_W[0] = lw
                bank_set[0] = list(range(8))
            S.merge([pend_W[0]])
            barrier()

        for l in range(cfg["layers"]):
            if cfg["ffn1"]:
                ffn(l, 0)
            if cfg["mixer"]:
                mixer(l)
            if cfg["ffn2"]:
                ffn(l, 1)

        A = arena("fin")
        NFS = 3
        FSQ = [A.alloc(f"SQ{i}", [128, 8, 128], BF16) for i in range(NFS)]
        FRS = [A.alloc(f"RSTD{i}", [128, 128]) for i in range(NFS)]
        FXG = [[A.alloc(f"XG{i}{h}", [128, 4, 128]) for h in range(2)] for i in range(NFS)]
        FYO = [A.alloc(f"YO{i}", [128, D]) for i in range(NFS)]
        fstreams = []
        for i in range(NFS):
            with S.stream() as fl:
                bank_set[0] = [[0, 1], [2, 3], [4, 5]][i]
                SQ, RSTD, XG, yo = FSQ[i], FRS[i], FXG[i], FYO[i]
                for t in range(i, NT, NFS):
                    c0 = t * 128
                    S.act(SQ, Xv(0, 8, c0, 128), AF.Square)
                    bk = nb()
                    for k in range(8):
                        S.mm(bk[:, 0:128], ONESB, SQ[:, k, :], start=(k == 0), stop=(k == 7))
                    S.act(RSTD, bk[:, 0:128], AF.Ln, bias=EPS6, scale=1.0 / D)
                    S.act(RSTD, RSTD, AF.Exp, scale=-0.5)
                    for hb in range(2):
                        S.tt(XG[hb], Xv(hb * 4, hb * 4 + 4, c0, 128), PV[:, R_FIN + hb * 4:R_FIN + hb * 4 + 4].us(2).bc([128, 4, 128]), ALU.mult)
                        S.tt(XG[hb], XG[hb], RSTD.us(1).bc([128, 4, 128]), ALU.mult)
                        bk = nb()
                        for j in range(4):
                            S.tr(bk[:, j * 128:(j + 1) * 128], XG[hb][:, j, :], IDENT)
                        S.copy(yo[:, hb * 512:(hb + 1) * 512], bk, eng="act" if hb == 0 else "dve")
                    dst = y_p[t * 128:(t + 1) * 128, :] if t < 16 else y_s[:, :]
                    out_ops.append(S.dma([(dst, yo)]))
            fstreams.append(fl)
        bank_set[0] = list(range(8))
        S.merge(fstreams)

        S.emit(final_wait_ops=out_ops)
    return nc


def _consts():
    c = np.zeros((128, 9, 128), np.float32)
    c[:, 0, :] = np.eye(128, dtype=np.float32)
    j = np.arange(128)[:, None]
    i = np.arange(128)[None, :]
    c[:, 1, :] = (j <= i).astype(np.float32)
    c[:, 2, :] = np.where(j <= i, 0.0, -30000.0).astype(np.float32)
    for idx, L in ((3, 128), (4, 64), (5, 8)):
        c[:, idx, :] = (np.arange(128) % L != 0).astype(np.float32)[None, :]
    c[:, 8, :] = ((j <= i) & ((j // 64) == (i // 64))).astype(np.float32)
    tpos = np.arange(128)[None, :] + 1.0
    p = np.arange(128)[:, None]
    for pt in range(2):
        w = np.where(p < 64, (2, 8)[pt], (4, 16)[pt]).astype(np.float32)
        c[:, 6 + pt, :] = 1.0 / np.minimum(tpos, w)
    return c


_NC_CACHE = {}


def kernel(**inputs):
    if "nc" not in _NC_CACHE:
        _NC_CACHE["nc"] = build()
    nc = _NC_CACHE["nc"]
    f = lambda a: np.ascontiguousarray(np.asarray(a, dtype=np.float32))
    inp = {k: f(v) for k, v in inputs.items()}
    cst = _consts()
    in_maps = []
    for c in range(NCORES):
        m = {}
        for k, v in inp.items():
            if k == "x_prompt":
                m[k] = np.ascontiguousarray(v[c])
            elif k == "x_sample":
                m[k] = np.ascontiguousarray(v[NSEQ * c:NSEQ * (c + 1)].reshape(NSEQ * TSQ, D))
            elif k.startswith("state_"):
                m[k] = np.ascontiguousarray(v[:, NSEQ * c:NSEQ * (c + 1)])
            else:
                m[k] = v
        m["c_all"] = cst
        in_maps.append(m)
    res = run_bass_kernel_spmd(nc, in_maps, core_ids=list(range(NCORES)))
    R = res.results
    cat = lambda name, ax: np.concatenate([np.asarray(R[c][name]) for c in range(NCORES)], axis=ax)
    y_prompt = np.stack([np.asarray(R[c]["y_p"]) for c in range(NCORES)], axis=0)
    y_sample = np.concatenate([np.asarray(R[c]["y_s"]).reshape(NSEQ, TSQ, D) for c in range(NCORES)], axis=0)
    outs = (y_prompt, y_sample,
            cat("pool_p", 1), cat("pool_s", 1), cat("gla_p", 1), cat("gla_s", 1),
            cat("hg_p", 1), cat("hg_s", 1), cat("ssm_p", 1), cat("ssm_s", 1),
            cat("conv_p", 1), cat("conv_s", 1))
    return tuple(np.ascontiguousarray(o.astype(np.float32)) for o in outs)
```

```python
import contextlib
import numpy as np
import concourse.bass as bass
import concourse.mybir as mybir
from concourse.bass_utils import run_bass_kernel_spmd

F32 = mybir.dt.float32
BF16 = mybir.dt.bfloat16
AF = mybir.ActivationFunctionType
ALU = mybir.AluOpType
AX = mybir.AxisListType

NCORES = 8
D = 1024
DFF = 2816
NIN = 3092
TP = 2048
NSEQ = 16
TSQ = 8
T = TP + NSEQ * TSQ
NT = T // 128
EPS = 1e-6
DEPTH = 2
C_GQ, C_GK = 256, 384
NINW = NIN - 256
C_PX, C_GV, C_GR, C_GLR = 0, 256, 512, 768
C_RQ, C_RF, C_RI, C_RG, C_SZ, C_XBC, C_DT = 784, 1040, 1296, 1552, 1808, 2064, 2832


class V:
    __slots__ = ("ap", "keys")

    def __init__(self, ap, keys):
        self.ap = ap
        self.keys = tuple(keys) if isinstance(keys, list) else (keys,)

    def _mk(self, ap):
        return V(ap, list(self.keys))

    def __getitem__(self, idx):
        return self._mk(self.ap[idx])

    def v(self, fn):
        return self._mk(fn(self.ap))

    def k(self, *keys):
        return V(self.ap, list(keys))

    def r(self, pat, **kw):
        return self._mk(self.ap.rearrange(pat, **kw))

    def bc(self, shape):
        return self._mk(self.ap.broadcast_to(list(shape)))

    def us(self, axis):
        return self._mk(self.ap.unsqueeze(axis))


def _ap(x):
    return x.ap if isinstance(x, V) else x


def _keys(xs):
    ks = []
    for x in xs:
        if isinstance(x, V):
            ks.extend(x.keys)
    return ks


class Ref:
    __slots__ = ("id",)


class Sched:
    ENG = ("pe", "act", "dve", "pool", "sp")

    def __init__(self, nc, sem_epoch=12000, n_dma_sems=8):
        self.nc = nc
        self.ops = []
        self.last_w = {}
        self.readers = {}
        self.sem_epoch = sem_epoch
        self.n_dma_sems = n_dma_sems
        self.dma_open = []
        self.strict = STRICT_SAME_ENGINE
        self.cur = None

    def add(self, eng, fn, reads=(), writes=(), dma=0, extra=()):
        if POOL_AS_DVE and eng == "pool" and not dma:
            eng = "dve"
        if self.cur is not None:
            ref = Ref()
            self.cur.append((eng, fn, reads, writes, dma, extra, ref))
            return ref
        return self._add(eng, fn, reads, writes, dma, extra)

    @contextlib.contextmanager
    def stream(self):
        lst = []
        prev, self.cur = self.cur, lst
        try:
            yield lst
        finally:
            self.cur = prev

    def merge(self, streams, weights=None):
        if weights is None:
            weights = [1.0] * len(streams)
        keep = [i for i, st in enumerate(streams) if st]
        weights = [weights[i] for i in keep]
        streams = [streams[i] for i in keep]
        pos = [0] * len(streams)
        total = sum(len(st) for st in streams)
        for _ in range(total):
            bi, bv = -1, None
            for i, st in enumerate(streams):
                if pos[i] < len(st):
                    v = (pos[i] + 0.5) / len(st) * weights[i]
                    if bv is None or v < bv:
                        bi, bv = i, v
            eng, fn, reads, writes, dma, extra, ref = streams[bi][pos[bi]]
            pos[bi] += 1
            ref.id = self._add(eng, fn, reads, writes, dma, extra)

    def _add(self, eng, fn, reads=(), writes=(), dma=0, extra=()):
        rk = _keys(reads)
        wk = _keys(writes)
        if PSUM_EXCLUSIVE:
            pk = [k for k in rk if isinstance(k, tuple) and k[0] == "ps"]
            if pk:
                rk = [k for k in rk if k not in pk]
                wk = list(wk) + [k for k in pk if k not in wk]
        deps = {}
        for d in extra:
            deps[d.id if isinstance(d, Ref) else d] = True
        for k in rk:
            w = self.last_w.get(k)
            if w is not None:
                deps[w] = True
        for k in wk:
            w = self.last_w.get(k)
            if w is not None:
                deps.setdefault(w, False)
            for r in self.readers.get(k, ()):
                deps.setdefault(r, False)
        oid = len(self.ops)
        self.ops.append(dict(eng=eng, fn=fn, deps=deps, dma=dma, sig=None, needed=False))
        for k in wk:
            self.last_w[k] = oid
            self.readers[k] = []
        for k in rk:
            if k not in wk:
                lst = self.readers.setdefault(k, [])
                if not dma:
                    for j in range(len(lst)):
                        oj = self.ops[lst[j]]
                        if oj["eng"] == eng and not oj["dma"]:
                            lst[j] = oid
                            break
                    else:
                        lst.append(oid)
                else:
                    lst.append(oid)
        if dma:
            self.dma_open.append(oid)
        return oid

    def emit(self, final_wait_ops=()):
        nc = self.nc
        ops = self.ops
        final_wait_ops = [r.id if isinstance(r, Ref) else r for r in final_wait_ops]
        for i, op in enumerate(ops):
            for d, raw in op["deps"].items():
                od = ops[d]
                if od["dma"] or op["dma"]:
                    od["needed"] = True
                elif od["eng"] != op["eng"]:
                    od["needed"] = True
                elif (raw or self.strict) and op["eng"] in ("act", "dve", "pool"):
                    od["needed"] = True
        for i in final_wait_ops:
            ops[i]["needed"] = True
        stack = contextlib.ExitStack()
        semcount = [0]

        def newsem(name):
            semcount[0] += 1
            return stack.enter_context(nc.semaphore(f"{name}_{semcount[0]}"))

        with stack:
            cur = {e: None for e in self.ENG}
            cnt = {e: 0 for e in self.ENG}
            dma_sems = {}
            dma_rr = {}
            for i, op in enumerate(ops):
                e = op["eng"]
                if op["dma"]:
                    if e not in dma_sems:
                        dma_sems[e] = [[newsem("dq" + e), 0] for _ in range(self.n_dma_sems)]
                        dma_rr[e] = 0
                    slot = dma_sems[e][dma_rr[e] % self.n_dma_sems]
                    dma_rr[e] += 1
                    prev = slot[1]
                    slot[1] += 16 * op["dma"]
                    op["sig"] = (slot[0], slot[1])
                    op["dma_prev"] = (slot[0], prev)
                elif op["needed"]:
                    if cur[e] is None or cnt[e] >= self.sem_epoch:
                        cur[e] = newsem("s" + e)
                        cnt[e] = 0
                    cnt[e] += 1
                    op["sig"] = (cur[e], cnt[e])
            self.n_sems = semcount[0]
            per_eng = {e: [i for i, op in enumerate(ops) if op["eng"] == e] for e in self.ENG}

            def run_engine(e, eng):
                seen = {}

                def wait(sem, val):
                    if val <= 0:
                        return
                    key = id(sem)
                    if seen.get(key, 0) >= val:
                        return
                    seen[key] = val
                    eng.wait_ge(sem, val)

                for i in per_eng[e]:
                    op = ops[i]
                    for d, raw in op["deps"].items():
                        od = ops[d]
                        if od["sig"] is None:
                            continue
                        same = (od["eng"] == e) and not od["dma"]
                        if same and not op["dma"]:
                            if e == "pe" or not (raw or self.strict):
                                continue
                        wait(*od["sig"])
                    if op["dma"]:
                        wait(*op["dma_prev"])
                    r = op["fn"](eng)
                    if op["dma"]:
                        sem, _ = op["sig"]
                        for ins in r:
                            ins.then_inc(sem, 16)
                    elif op["sig"] is not None:
                        r.then_inc(op["sig"][0], 1)
                if e == "sp":
                    for i in final_wait_ops:
                        wait(*ops[i]["sig"])

            with nc.Block() as block:
                block.tensor(lambda eng: run_engine("pe", eng))
                block.scalar(lambda eng: run_engine("act", eng))
                block.vector(lambda eng: run_engine("dve", eng))
                block.gpsimd(lambda eng: run_engine("pool", eng))
                block.sync(lambda eng: run_engine("sp", eng))

    def mm(self, out, lhsT, rhs, start=True, stop=True):
        return self.add("pe", lambda e: e.matmul(_ap(out), _ap(lhsT), _ap(rhs), start=start, stop=stop),
                        reads=[lhsT, rhs], writes=[out])

    def tr(self, out, in_, ident):
        return self.add("pe", lambda e: e.transpose(_ap(out), _ap(in_), _ap(ident)),
                        reads=[in_, ident], writes=[out])

    def act(self, out, in_, func, bias=None, scale=1.0):
        kw = {}
        if bias is not None:
            kw["bias"] = _ap(bias)
        sc = _ap(scale)
        return self.add("act", lambda e: e.activation(_ap(out), _ap(in_), func, scale=sc, **kw),
                        reads=[in_, bias, scale], writes=[out])

    def tt(self, out, in0, in1, op, eng="dve"):
        return self.add(eng, lambda e: e.tensor_tensor(_ap(out), _ap(in0), _ap(in1), op),
                        reads=[in0, in1], writes=[out])

    def ts(self, out, in0, s1, op0, eng="dve"):
        return self.add(eng, lambda e: e.tensor_scalar(_ap(out), _ap(in0), _ap(s1), None, op0),
                        reads=[in0, s1], writes=[out])

    def ts2(self, out, in0, s1, s2, op0, op1, eng="dve"):
        a, b, c, d = _ap(out), _ap(in0), _ap(s1), _ap(s2)
        return self.add(eng, lambda e: e.tensor_scalar(a, b, c, d, op0, op1), reads=[in0, s1, s2], writes=[out])

    def stt(self, out, in0, scalar, in1, op0, op1, eng="dve"):
        return self.add(eng, lambda e: e.scalar_tensor_tensor(_ap(out), _ap(in0), _ap(scalar), _ap(in1), op0, op1),
                        reads=[in0, scalar, in1], writes=[out])

    def copy(self, out, in_, eng="dve"):
        if eng == "act":
            return self.add(eng, lambda e: e.copy(_ap(out), _ap(in_)), reads=[in_], writes=[out])
        return self.add(eng, lambda e: e.tensor_copy(_ap(out), _ap(in_)), reads=[in_], writes=[out])

    def recip(self, out, in_):
        return self.add("dve", lambda e: e.reciprocal(_ap(out), _ap(in_)), reads=[in_], writes=[out])

    def rsum(self, out, in_, eng="dve"):
        return self.add(eng, lambda e: e.reduce_sum(_ap(out), _ap(in_), AX.X), reads=[in_], writes=[out])

    def scan(self, out, d0, d1, init, op0, op1, eng="dve"):
        return self.add(eng, lambda e: e.tensor_tensor_scan(_ap(out), _ap(d0), _ap(d1), init, op0, op1),
                        reads=[d0, d1], writes=[out])

    def memset(self, out, val, eng="dve"):
        return self.add(eng, lambda e: e.memset(_ap(out), val), reads=[], writes=[out])

    def dma(self, pairs, eng="sp", reads=(), writes=()):
        rd = list(reads) + [p[1] for p in pairs]
        wr = list(writes) + [p[0] for p in pairs]

        def fn(e):
            return [e.dma_start(out=_ap(o), in_=_ap(i)) for (o, i) in pairs]
        return self.add(eng, fn, reads=rd, writes=wr, dma=len(pairs))


class Arena:
    def __init__(self, h32, hbf, nbytes, tag):
        self.h32, self.hbf, self.nbytes, self.tag = h32, hbf, nbytes, tag
        self.off = 0
        self.offs = {}

    def alloc(self, name, shape, dt=F32):
        esz = 4 if dt == F32 else 2
        n = 1
        for s in shape[1:]:
            n *= s
        nb = (n * esz + 3) // 4 * 4
        assert self.off + nb <= self.nbytes, (self.tag, name, self.off, nb, self.nbytes)
        self.offs[name] = self.off
        if dt == F32:
            ap = self.h32[:, self.off // 4: self.off // 4 + n]
        else:
            ap = self.hbf[:, self.off // 2: self.off // 2 + n]
        self.off += nb
        return self._view(ap, shape, (self.tag, name))

    def alias(self, base, shape, dt, keys):
        off = self.offs[base]
        n = 1
        for s_ in shape[1:]:
            n *= s_
        if dt == F32:
            ap = self.h32[:, off // 4: off // 4 + n]
        else:
            ap = self.hbf[:, off // 2: off // 2 + n]
        return self._view(ap, shape, keys)

    def _view(self, ap, shape, keys):
        if len(shape) == 3:
            ap = ap.rearrange("p (a b) -> p a b", b=shape[2])
        elif len(shape) == 4:
            ap = ap.rearrange("p (a b c) -> p a b c", b=shape[2], c=shape[3])
        if shape[0] < 128:
            ap = ap[0:shape[0]]
        if isinstance(keys, list):
            return V(ap, keys)
        return V(ap, [keys, self.tag] if SER_ARENA else keys)


ARENA_BYTES = 62464 + 4096
STRICT_SAME_ENGINE = True
import os as _os
SER_ARENA = bool(int(_os.environ.get('SER_ARENA', '0')))
POOL_AS_DVE = bool(int(_os.environ.get('POOL_AS_DVE', '0')))
PSUM_EXCLUSIVE = bool(int(_os.environ.get('PSUM_EXCLUSIVE', '1')))
MERGE_W = [float(x) for x in _os.environ.get('MERGE_W', '1,1,1,1,1').split(',')]
SER_PS = bool(int(_os.environ.get('SER_PS', '0')))
DEFAULT_CFG = dict(stage=99, tmax=17, layers=2, ffn1=True, mixer=True, ffn2=True, pool=True, gla=True, hgrn=True, ssd=True)


def build(cfg=None):
    cfg = dict(DEFAULT_CFG, **(cfg or {}))
    nc = bass.Bass("TRN2", target_bir_lowering=False)
    din = {}

    def inp(name, shape):
        din[name] = nc.dram_tensor(name, list(shape), F32, kind="ExternalInput").ap()
        return din[name]

    def outp(name, shape):
        return nc.dram_tensor(name, list(shape), F32, kind="ExternalOutput").ap()

    x_prompt = inp("x_prompt", [TP, D])
    x_sample = inp("x_sample", [NSEQ * TSQ, D])
    state_pool = inp("state_pool", [DEPTH, NSEQ, 15, 256])
    state_gla = inp("state_gla", [DEPTH, NSEQ, 4, 32, 64])
    state_hgrn = inp("state_hgrn", [DEPTH, NSEQ, 4, 64, 64])
    state_ssm = inp("state_ssm", [DEPTH, NSEQ, 4, 64, 128])
    state_conv = inp("state_conv", [DEPTH, NSEQ, 3, 768])
    ffn_norm = [inp("ffn1_norm", [DEPTH, D]), inp("ffn2_norm", [DEPTH, D])]
    ffn_wg = [inp("ffn1_w_gate", [DEPTH, D, DFF]), inp("ffn2_w_gate", [DEPTH, D, DFF])]
    ffn_wu = [inp("ffn1_w_up", [DEPTH, D, DFF]), inp("ffn2_w_up", [DEPTH, D, DFF])]
    ffn_wd = [inp("ffn1_w_down", [DEPTH, DFF, D]), inp("ffn2_w_down", [DEPTH, DFF, D])]
    mix_norm = inp("mix_norm", [DEPTH, D])
    w_in = inp("w_in", [DEPTH, D, NIN])
    pool_w = inp("pool_w", [DEPTH, 4, 64, 64])
    pool_scale = inp("pool_scale", [DEPTH, 256])
    gla_w_gate = inp("gla_w_gate", [DEPTH, 16, 128])
    gla_gate_bias = inp("gla_gate_bias", [DEPTH, 128])
    gla_norm = inp("gla_norm", [DEPTH, 64])
    hgrn_lb_logits = inp("hgrn_lb_logits", [DEPTH, 256])
    hgrn_norm = inp("hgrn_norm", [DEPTH, 64])
    ssm_conv_w = inp("ssm_conv_w", [DEPTH, 4, 768])
    ssm_conv_b = inp("ssm_conv_b", [DEPTH, 768])
    ssm_dt_bias = inp("ssm_dt_bias", [DEPTH, 4])
    ssm_A_log = inp("ssm_A_log", [DEPTH, 4])
    ssm_D = inp("ssm_D", [DEPTH, 4])
    ssm_norm = inp("ssm_norm", [DEPTH, 256])
    w_out = inp("w_out", [DEPTH, D, D])
    final_norm = inp("final_norm", [D])
    c_all = inp("c_all", [128, 9, 128])

    y_p = outp("y_p", [TP, D])
    y_s = outp("y_s", [NSEQ * TSQ, D])
    pool_p = outp("pool_p", [DEPTH, 1, 15, 256])
    pool_s = outp("pool_s", [DEPTH, NSEQ, 15, 256])
    gla_p = outp("gla_p", [DEPTH, 1, 4, 32, 64])
    gla_s = outp("gla_s", [DEPTH, NSEQ, 4, 32, 64])
    hg_p = outp("hg_p", [DEPTH, 1, 4, 64, 64])
    hg_s = outp("hg_s", [DEPTH, NSEQ, 4, 64, 64])
    ssm_p = outp("ssm_p", [DEPTH, 1, 4, 64, 128])
    ssm_s = outp("ssm_s", [DEPTH, NSEQ, 4, 64, 128])
    conv_p = outp("conv_p", [DEPTH, 1, 3, 768])
    conv_s = outp("conv_s", [DEPTH, NSEQ, 3, 768])

    S = Sched(nc)
    out_ops = []
    st = contextlib.ExitStack()
    with st:
        def sbt(name, shape, dt=F32):
            return st.enter_context(nc.sbuf_tensor(name, shape, dt))

        Xh = sbt("X", [128, 8, T])
        RBh = sbt("RB", [128, 8 * NINW], BF16)
        WOUTh = sbt("WOUT", [128, 8, D], BF16)
        WQKh = sbt("WQK", [128, 8, 512], BF16)
        ARh = sbt("AR", [128, ARENA_BYTES // 4])
        CONh = sbt("CON", [128, 9, 128])
        PVh = sbt("PV", [128, 128])
        MISCh = sbt("MISC", [128, 64])
        BARh = sbt("BAR", [128, 8])
        ONESBh = sbt("ONESB", [128, 128], BF16)
        ONESFh = sbt("ONESF", [128, 128])
        IDBh = sbt("IDB", [128, 128], BF16)
        PSh = [st.enter_context(nc.psum_tensor(f"ps{i}", [128, 512], F32)) for i in range(8)]

        ARbf = ARh.bitcast(BF16)

        def Xv(k0, k1, c0, n):
            keys = [("X", t) for t in range(c0 // 128, (c0 + n + 127) // 128)]
            return V(Xh[:, k0:k1, c0:c0 + n], keys)

        XNap = RBh[:, 0:8 * T].rearrange("p (k t) -> p k t", t=T)

        def XNv(k, c0, n):
            keys = [("XN", t) for t in range(c0 // 128, (c0 + n + 127) // 128)]
            return V(XNap[:, k, c0:c0 + n], keys)

        WIN = V(RBh[:, :].rearrange("p (k n) -> p k n", n=NINW), "WIN")
        WOUT = V(WOUTh[:], "WOUT")
        WQK = V(WQKh[:], "WQK")
        CON = V(CONh[:], "CON")
        IDENT = CON[:, 0, :]
        UF = CON[:, 1, :]
        NEGM = CON[:, 2, :]
        RMS = {128: CON[:, 3, :], 64: CON[:, 4, :], 8: CON[:, 5, :]}
        INVT = CON[:, 6:8, :]
        BD64 = CON[:, 8, :]
        PV = V(PVh[:], "PV")
        MISC = V(MISCh[:], "MISC")
        ONESB = V(ONESBh[:], "ONESB")
        ONESF = V(ONESFh[:], "ONESF")
        IDB = V(IDBh[:], "IDB")
        EPS6 = MISC[:, 0:1]
        ONE1 = MISC[:, 1:2]
        LBv = MISC[:, 4:8].r("p (l t) -> p l t", t=2)
        OMLv = MISC[:, 8:12].r("p (l t) -> p l t", t=2)
        NGB = MISC[:, 12:16].r("p (l t) -> p l t", t=2)
        GBraw = MISC[:, 16:20].r("p (l t) -> p l t", t=2)
        BARS = V(BARh[:], "BAR")
        PS = [V(PSh[i][:], [("ps", i), "ps"] if SER_PS else ("ps", i)) for i in range(8)]
        bank_set = [list(range(8))]
        psn = {}

        def nb():
            bs = bank_set[0]
            k = tuple(bs)
            i = psn.get(k, 0)
            psn[k] = i + 1
            return PS[bs[i % len(bs)]]

        barn = [0]

        def barrier():
            n = barn[0]
            barn[0] += 1
            marks = []
            marks.append(S.add("pe", lambda e: e.matmul(PSh[7][:, 0:1], ONESBh[:, 0:128], ONESBh[:, 0:1], start=True, stop=True),
                               reads=[ONESB], writes=[PS[7]]))
            marks.append(S.add("act", lambda e: e.copy(BARh[:, 0:1], MISCh[:, 0:1]), reads=[MISC], writes=[BARS[:, 0:1].k(("bar", "act"))]))
            marks.append(S.add("dve", lambda e: e.memset(BARh[:, 1:2], 0.0), writes=[BARS[:, 1:2].k(("bar", "dve"))]))
            marks.append(S.add("pool", lambda e: e.memset(BARh[:, 2:3], 0.0), writes=[BARS[:, 2:3].k(("bar", "pool"))]))
            ext = marks + [d for d in S.dma_open]
            S.dma_open = []
            S.add("pe", lambda e: e.matmul(PSh[7][:, 0:1], ONESBh[:, 0:128], ONESBh[:, 0:1], start=True, stop=True),
                  reads=[ONESB], writes=[PS[7]], extra=ext)
            S.add("act", lambda e: e.copy(BARh[:, 3:4], MISCh[:, 0:1]), reads=[MISC], writes=[BARS[:, 3:4].k(("bar2", "act"))], extra=ext)
            S.add("dve", lambda e: e.memset(BARh[:, 4:5], 0.0), writes=[BARS[:, 4:5].k(("bar2", "dve"))], extra=ext)
            S.add("pool", lambda e: e.memset(BARh[:, 5:6], 0.0), writes=[BARS[:, 5:6].k(("bar2", "pool"))], extra=ext)
            S.add("sp", lambda e: e.nop(), extra=ext)

        def arena(tag):
            return Arena(ARh, ARbf, ARENA_BYTES, tag)

        S.dma([(CON, c_all)])
        S.memset(MISC, 0.0)
        S.memset(EPS6, EPS)
        S.memset(ONE1, 1.0)
        S.memset(ONESB, 1.0)
        S.memset(ONESF, 1.0)
        S.copy(IDB, IDENT)
        A0 = arena("setup")
        RAW = A0.alloc("RAW", [128, 128])
        S.memset(RAW, 0.0)
        rows = []
        R_FFN = {}
        r = 0
        for l in range(DEPTH):
            for w in range(2):
                rows.append((RAW[r:r + 8, :], ffn_norm[w][l].rearrange("(k p) -> k p", p=128)))
                R_FFN[(l, w)] = r
                r += 8
        R_MIX = {}
        for l in range(DEPTH):
            rows.append((RAW[r:r + 8, :], mix_norm[l].rearrange("(k p) -> k p", p=128)))
            R_MIX[l] = r
            r += 8
        rows.append((RAW[r:r + 8, :], final_norm.rearrange("(k p) -> k p", p=128)))
        R_FIN = r
        r += 8
        R_PSC = r
        rows.append((RAW[r:r + 4, :], pool_scale.rearrange("l (t p) -> (l t) p", p=128)))
        r += 4
        R_CB = r
        rows.append((RAW[r:r + 12, :], ssm_conv_b.rearrange("l (j p) -> (l j) p", p=128)))
        r += 12
        R_CW = r
        rows.append((RAW[r:r + 48, :], ssm_conv_w.rearrange("l d (j p) -> (l d j) p", p=128)))
        r += 48
        R_LB = r
        rows.append((RAW[r:r + 4, :], hgrn_lb_logits.rearrange("l (t p) -> (l t) p", p=128)))
        r += 4
        assert r <= 128
        S.dma(rows)
        bk = nb()
        S.tr(bk[:, 0:128], RAW, IDENT)
        S.copy(PV, bk[:, 0:128])
        gbp = []
        for l in range(DEPTH):
            for h in range(4):
                pt, h2 = h // 2, h % 2
                gbp.append((GBraw[h2 * 64:h2 * 64 + 32, l, pt:pt + 1],
                            gla_gate_bias[l, h * 32:(h + 1) * 32].rearrange("(p o) -> p o", o=1)))
        S.dma(gbp)
        S.ts(NGB, GBraw, -1.0, ALU.mult)
        S.memset(LBv[:, 0, :], 0.0)
        S.memset(OMLv[:, 0, :], 1.0)
        TMPL = MISC[:, 20:22]
        S.tt(TMPL, PV[:, R_LB:R_LB + 2], PV[:, R_LB + 2:R_LB + 4], ALU.subtract)
        S.act(TMPL, TMPL, AF.Exp)
        S.ts(TMPL, TMPL, 1.0, ALU.add)
        S.recip(LBv[:, 1, :], TMPL)
        S.stt(OMLv[:, 1, :], LBv[:, 1, :], -1.0, ONE1.bc([128, 2]), ALU.mult, ALU.add)

        STG = [A0.alloc(f"STG{i}", [128, D]) for i in range(4)]
        for t in range(NT):
            src = x_prompt[t * 128:(t + 1) * 128, :] if t < 16 else x_sample[:, :]
            sg = STG[t % 4]
            S.dma([(sg, src)])
            for hb in range(2):
                bk = nb()
                for j in range(4):
                    k = hb * 4 + j
                    S.tr(bk[:, j * 128:(j + 1) * 128], sg[:, k * 128:(k + 1) * 128], IDENT)
                S.copy(Xv(hb * 4, hb * 4 + 4, t * 128, 128), bk.r("p (a b) -> p a b", b=128),
                       eng="act" if hb == 0 else "dve")
        barrier()

        TGS = [(0, 512), (512, 512), (1024, 512), (1536, 512), (2048, 128)]

        def norm_all(SQ, LNT, RSTD, grow):
            for (c0, n) in TGS:
                S.act(SQ[:, :, 0:n], Xv(0, 8, c0, n), AF.Square)
                bk = nb()
                for k in range(8):
                    S.mm(bk[:, 0:n], ONESB, SQ[:, k, 0:n], start=(k == 0), stop=(k == 7))
                S.act(LNT[:, 0:n], bk[:, 0:n], AF.Ln, bias=EPS6, scale=1.0 / D)
                S.act(RSTD[:, 0:n], LNT[:, 0:n], AF.Exp, scale=-0.5)
                for k in range(8):
                    S.stt(XNv(k, c0, n), Xv(k, k + 1, c0, n)[:, 0, :], PV[:, grow + k:grow + k + 1],
                          RSTD[:, 0:n], ALU.mult, ALU.mult)

        GRP = [(g * 512, 4) for g in range(5)] + [(2560, 2)]

        def ffn(l, w):
            A = arena(f"ffn{l}{w}")
            WG = [A.alloc(f"WG{i}", [128, 8, 512], BF16) for i in range(2)]
            WU = [A.alloc(f"WU{i}", [128, 8, 512], BF16) for i in range(2)]
            WD = [A.alloc(f"WD{i}", [128, 4, D], BF16) for i in range(2)]
            wg_d = ffn_wg[w][l].rearrange("(k p) n -> p k n", p=128)
            wu_d = ffn_wu[w][l].rearrange("(k p) n -> p k n", p=128)
            wd_d = ffn_wd[w][l]

            def load(g):
                c0, nch = GRP[g]
                s = g % 2
                ncol = nch * 128
                S.dma([(WG[s][:, :, 0:ncol], wg_d[:, :, c0:c0 + ncol]),
                       (WU[s][:, :, 0:ncol], wu_d[:, :, c0:c0 + ncol]),
                       (WD[s][:, 0:nch, :], wd_d[c0:c0 + ncol, :].rearrange("(c p) n -> p c n", p=128))],
                      eng="pool")

            HBall = A.alloc("HB", [128, 8, 512], BF16)
            HB = [HBall[:, 4 * i:4 * i + 4, :].k((A.tag, "HB%d" % i)) for i in range(2)]
            SGT = [A.alloc(f"SG{i}", [128, 512]) for i in range(2)]
            load(0)
            norm_all(HBall.k((A.tag, "HB0"), (A.tag, "HB1")), SGT[0], SGT[1], R_FFN[(l, w)])
            for g in range(len(GRP)):
                if g + 1 < len(GRP):
                    load(g + 1)
                c0g, nch = GRP[g]
                s = g % 2
                for tgi, (c0, n) in enumerate(TGS):
                    hb = HB[tgi % 2]
                    for c in range(nch):
                        pg = nb()
                        pu = nb()
                        for k in range(8):
                            S.mm(pg[:, 0:n], WG[s][:, k, c * 128:(c + 1) * 128], XNv(k, c0, n), start=(k == 0), stop=(k == 7))
                        for k in range(8):
                            S.mm(pu[:, 0:n], WU[s][:, k, c * 128:(c + 1) * 128], XNv(k, c0, n), start=(k == 0), stop=(k == 7))
                        sgt = SGT[c % 2]
                        S.act(sgt[:, 0:n], pg[:, 0:n], AF.Silu)
                        S.tt(hb[:, c, 0:n], sgt[:, 0:n], pu[:, 0:n], ALU.mult)
                    for m in range(8):
                        py = nb()
                        for c in range(nch):
                            S.mm(py[:, 0:n], WD[s][:, c, m * 128:(m + 1) * 128], hb[:, c, 0:n], start=(c == 0), stop=(c == nch - 1))
                        xv = Xv(m, m + 1, c0, n)[:, 0, :]
                        S.stt(xv, py[:, 0:n], 0.5, xv, ALU.mult, ALU.add)
            barrier()

        def mixer(l):
            A = arena(f"mix{l}")
            for k in range(8):
                S.dma([(WIN[:, k, 0:256], w_in[l, k * 128:(k + 1) * 128, 0:256]),
                       (WIN[:, k, 256:NINW], w_in[l, k * 128:(k + 1) * 128, 512:NIN])], eng="pool")
            S.memset(WQK, 0.0, eng="pool")
            for k in range(8):
                S.dma([(WQK[:, k, 0:256].r("p (h c) -> p h c", c=64)[:, :, 0:32],
                        w_in[l, k * 128:(k + 1) * 128, C_GQ:C_GQ + 128].rearrange("p (h c) -> p h c", c=32)),
                       (WQK[:, k, 256:512].r("p (h c) -> p h c", c=64)[:, :, 0:32],
                        w_in[l, k * 128:(k + 1) * 128, C_GK:C_GK + 128].rearrange("p (h c) -> p h c", c=32))],
                      eng="pool")
            S.dma([(WOUT, w_out[l].rearrange("(k p) n -> p k n", p=128))], eng="pool")
            PWBD = A.alloc("PWBD", [128, 2, 128], BF16)
            S.memset(PWBD, 0.0)
            S.dma([(PWBD[g2 * 64:(g2 + 1) * 64, pt, g2 * 64:(g2 + 1) * 64], pool_w[l, 2 * pt + g2])
                   for pt in range(2) for g2 in range(2)], eng="pool")
            GWG = A.alloc("GWG", [128, 256], BF16)
            S.memset(GWG, 0.0)
            S.dma([(GWG[0:16, :].r("p (h c) -> p h c", c=64)[:, :, 0:32],
                    gla_w_gate[l].rearrange("p (h c) -> p h c", c=32))], eng="pool")
            GN = A.alloc("GN", [128, 64])
            HN = A.alloc("HN", [128, 64])
            SN = A.alloc("SN", [128, 256])
            SM4 = A.alloc("SM4", [128, 16])
            DTB, NEGA, DSK = SM4[:, 0:4], SM4[:, 4:8], SM4[:, 8:12]
            S.dma([(GN, gla_norm[l].partition_broadcast(128)), (HN, hgrn_norm[l].partition_broadcast(128)),
                   (SN, ssm_norm[l].partition_broadcast(128)), (DTB, ssm_dt_bias[l].partition_broadcast(128)),
                   (NEGA, ssm_A_log[l].partition_broadcast(128)), (DSK, ssm_D[l].partition_broadcast(128))])
            S.act(NEGA, NEGA, AF.Exp)
            S.ts(NEGA, NEGA, -1.0, ALU.mult)
            CW = PV[:, R_CW + l * 24:R_CW + (l + 1) * 24].r("p (d j) -> p d j", j=6)
            CB = PV[:, R_CB + l * 6:R_CB + (l + 1) * 6]
            PSC = PV[:, R_PSC + l * 2:R_PSC + (l + 1) * 2]
            GROW = R_MIX[l]

            tg = A.tag
            XNTs = [A.alloc("XNT0", [128, 8, 128], BF16), A.alloc("XNT1", [128, 8, 128], BF16)]
            XNT = XNTs[0]
            NS = A.alloc("NS", [128, 4, 128])
            SQ = A.alias("NS", [128, 8, 128], BF16, (tg, "NS"))
            RSTD = A.alloc("RSTD", [128, 128])
            MIXs = [[A.alloc(f"MIX{b}{i}", [128, 2, 128], BF16) for i in range(4)] for b in range(2)]
            MIX = MIXs[0]
            PXB = A.alloc("PXB", [128, 2 * 16 * 24])
            SA = A.alloc("SA", [128, 2 * 8 * 24])
            SBb = A.alloc("SBb", [128, 2 * 8 * 24])
            DPL = A.alloc("DPL", [128, 2, 128], BF16)
            PT = A.alloc("PT", [128, 2, 128])
            P_F = [None, None, PT]
            P_TT = [SA[:, 0:256], SBb[:, 0:256], None, PT.r("p a b -> p (a b)")]
            G_F = [A.alloc("GF0", [128, 4, 128]), A.alloc("GF1", [128, 4, 128]), A.alloc("GF2", [128, 2, 128]),
                   A.alloc("GF3", [128, 4, 128]), A.alloc("GF4", [128, 2, 128])]
            G_FB = [A.alloc("GFB0", [128, 4, 128], BF16), None, A.alloc("GFB2", [128, 1, 128], BF16), A.alloc("GFB3", [128, 4, 128], BF16)]
            G_TT = [None] + [A.alloc(f"GT{i}", [128, 256]) for i in range(1, 4)]
            G_TT[0] = G_TT[3]
            G_TB = [A.alloc("GTB0", [128, 256], BF16), A.alloc("GTB1", [128, 256], BF16), A.alloc("GTB2", [128, 512], BF16),
                    A.alloc("GTB3", [128, 256], BF16)]
            GQAB = A.alloc("GQAB", [128, 4, 128], BF16)
            SRALL = A.alias("GQAB", [128, 256], F32, (tg, "GQAB"))
            G_SMT = A.alloc("GSMT", [128, 64])
            XBC = A.alloc("XBC", [128, 6 * 16 * 11])
            CV = A.alloc("CV", [128, 6, 128])
            UA_ = A.alloc("UA", [128, 4, 128])
            DD_ = A.alloc("DDM", [128, 4, 128])
            S_F = [UA_, DD_, UA_, UA_, DD_]
            S_FB = [None, A.alloc("SFB1", [128, 4, 128], BF16), A.alloc("SFB2", [128, 4, 128], BF16)]
            S_TT = [A.alloc(f"ST{i}", [128, 256]) for i in range(4)]
            S_TB = [A.alloc("STB0", [128, 256], BF16), A.alloc("STB1", [128, 256], BF16), A.alloc("STB2", [128, 512], BF16),
                    A.alloc("STB3", [128, 256], BF16)]
            S_SMT = A.alloc("SSMT", [128, 64])
            SZALL = A.alloc("SZALL", [128, 256])
            SZTMP = A.alloc("SZTMP", [128, 256])
            SDTAT = A.alloc("SDTAT", [128, 8])
            UAf = UA_.r("p a b -> p (a b)")
            DDf = DD_.r("p a b -> p (a b)")
            GLS = A.alloc("GLS", [128, 2, 64])
            GLSb = A.alloc("GLSb", [128, 2, 128], BF16)
            HGS = A.alloc("HGS", [128, 2, 64])
            HGSb = A.alloc("HGSb", [128, 2, 128], BF16)
            SST = A.alloc("SST", [128, 256])
            SSTb = A.alloc("SSTb", [128, 256], BF16)
            N0 = A.alloc("N0", [128, 2, 128])
            for z in (GLS, GLSb, HGS, HGSb, SST, SSTb, PXB, XBC, G_FB[0], GQAB, G_TB[3]):
                S.memset(z, 0.0, eng="pool")
            F, FB, TT, TB, SMT = G_F, G_FB, G_TT, G_TB, G_SMT

            def proj_fm(dst, wv):
                for k in range(8):
                    S.mm(dst, wv[:, k, :], XNT[:, k, :], start=(k == 0), stop=(k == 7))

            def proj_tm(dst, o0, Lc, c0w, ncol):
                for k in range(8):
                    S.mm(dst, XNT[:, k, o0:o0 + Lc], WIN[:, k, c0w:c0w + ncol], start=(k == 0), stop=(k == 7))

            def silu_tm(dst, src_ps, Lc, tmp):
                S.act(tmp, src_ps, AF.Exp, scale=-1.0)
                S.act(tmp, tmp, AF.Ln, bias=ONE1[0:Lc])
                S.act(tmp, tmp, AF.Exp, scale=-1.0)
                S.tt(dst, src_ps, tmp, ALU.mult)

            def to_fm(res, Lc, o0, mc):
                bk = nb()
                for j in range(2):
                    S.tr(bk[:, j * Lc:(j + 1) * Lc], res[0:Lc, j * 128:(j + 1) * 128], IDENT[0:Lc, 0:Lc])
                S.copy(MIX[mc // 2][:, :, o0:o0 + Lc], bk[:, 0:2 * Lc].r("p (a b) -> p a b", b=Lc), eng="act")

            def gla_like(name, t, Qf, Kf, GSf, sg, c0w, Sst, Sstb, normw, mc, Lc, st_in, st_out, st_pout, Ls=None):
                is_s = (t == 16)
                Ls = Ls or Lc
                nch = 128 // Lc
                nsub = Lc // Ls
                nseg = 128 // Ls
                assert nsub in (1, 2)
                F, FB, TT, TB, SMT = G_F, G_FB, G_TT, G_TB, G_SMT
                BP, EB, ENB, DD, KH = F[0][:, 0:2, :], F[0][:, 2:4, :], F[1][:, 0:2, :], F[1][:, 2:4, :], F[2][:, 0:2, :]
                QT, KT = FB[3][:, 0:2, :], FB[3][:, 2:4, :]
                QBD = FB[0].r("p (a b) c -> p a b c", b=2)
                QTA, QTB = GQAB[:, 0:2, :], GQAB[:, 2:4, :]
                KHBB = TB[3]
                DEC = SMT[:, 0:32]
                RM = RMS[Ls]
                for pt in range(2):
                    S.scan(BP[:, pt, :], RM, GSf[:, pt, :], 0.0, ALU.mult, ALU.add)
                S.act(EB, BP, AF.Exp, scale=sg)
                S.act(ENB, BP, AF.Exp, scale=-sg)
                S.tt(QT, Qf, EB, ALU.mult)
                for h2 in range(2):
                    r0 = h2 * 64
                    S.copy(QBD[r0:r0 + 64, :, h2, :], QT[r0:r0 + 64, :, :], eng="act")
                if nsub == 2:
                    S.copy(QTA[:, :, 0:64], QT[:, :, 0:64], eng="act")
                    S.copy(QTB[:, :, 64:128], QT[:, :, 64:128], eng="act")
                S.tt(KT, Kf, ENB, ALU.mult)
                BPc = BP.r("p a (c l) -> p (a c) l", l=Ls)
                S.tt(DD.r("p a (c l) -> p (a c) l", l=Ls), BPc, BPc[:, :, Ls - 1:Ls].bc([128, 2 * nseg, Ls]), ALU.subtract)
                S.act(DD, DD, AF.Exp, scale=-sg)
                S.tt(KH, Kf, DD, ALU.mult)
                S.act(DEC[:, 0:2 * nseg].us(2), BPc[:, :, Ls - 1:Ls], AF.Exp, scale=sg)
                if is_s:
                    VBALL = TB[3]
                    EGALL = F[1][:, 2:4, :].r("p a b -> p (a b)")
                    paA = nb()
                    proj_tm(paA[:, 0:512], 0, 128, c0w, 512)
                    S.copy(VBALL, paA[:, 0:256], eng="act")
                    silu_tm(SRALL, paA[:, 256:512], 128, EGALL)
                    SRA3 = SRALL.r("p (h v) -> p h v", v=64)
                    S.tt(SRA3, SRA3, normw[:, :].us(1).bc([128, 4, 64]), ALU.mult)
                for ci in range(nch):
                    o0 = ci * Lc
                    if is_s:
                        st_in(ci)
                    VB, KHB, ATT = TB[0][0:Lc, 0:256], TB[1][0:Lc, 0:256], TB[2][0:Lc, :]
                    EG, SR, OS, SQO = TT[0][0:Lc, 0:256], TT[1][0:Lc, 0:256], TT[2][0:Lc, 0:256], TT[3][0:Lc, 0:256]
                    pa = nb()
                    if is_s:
                        S.mm(pa[0:Lc, 0:256], IDB[:, o0:o0 + Lc], VBALL)
                        S.mm(pa[0:Lc, 256:512], IDENT[:, o0:o0 + Lc], SRALL)
                        S.copy(VB, pa[0:Lc, 0:256], eng="act")
                        S.copy(SR, pa[0:Lc, 256:512], eng="act")
                    else:
                        proj_tm(pa[0:Lc, 0:512], o0, Lc, c0w, 512)
                        S.copy(VB, pa[0:Lc, 0:256], eng="act")
                        silu_tm(SR, pa[0:Lc, 256:512], Lc, EG)
                        SR3 = SR.r("p (h v) -> p h v", v=64)
                        S.tt(SR3, SR3, normw[0:Lc, :].us(1).bc([Lc, 4, 64]), ALU.mult)
                    pb = nb()
                    for pt in range(2):
                        S.tr(pb[0:Lc, pt * 128:(pt + 1) * 128], KH[:, pt, o0:o0 + Lc], IDENT)
                    S.copy(KHB, pb[0:Lc, 0:256], eng="act")
                    if nsub == 2:
                        S.copy(KHBB[64:128, :], pb[64:128, 0:256], eng="act")
                    pc = nb()
                    for pt in range(2):
                        S.mm(pc[0:Lc, pt * 2 * Lc:(pt + 1) * 2 * Lc].r("p (a b) -> p a b", b=Lc), KT[:, pt, o0:o0 + Lc], QBD[:, pt, :, o0:o0 + Lc])
                    ATT3 = ATT[:, 0:4 * Lc].r("p (h l) -> p h l", l=Lc)
                    msk = UF[0:Lc, 0:Lc] if nsub == 1 else BD64
                    S.tt(ATT3, pc[0:Lc, 0:4 * Lc].r("p (h l) -> p h l", l=Lc), msk.us(1).bc([Lc, 4, Lc]), ALU.mult)
                    pd = nb()
                    for h in range(4):
                        S.mm(pd[0:Lc, h * 64:(h + 1) * 64], ATT3[:, h, :], VB[:, h * 64:(h + 1) * 64])
                    S.copy(OS, pd[0:Lc, 0:256], eng="act")
                    for si in range(nsub):
                        seg = ci * nsub + si
                        qm = QT if nsub == 1 else (QTA, QTB)[si]
                        pd2 = nb()
                        for pt in range(2):
                            S.mm(pd2[0:Lc, pt * 128:(pt + 1) * 128], qm[:, pt, o0:o0 + Lc], Sstb[:, pt, :])
                        S.tt(OS, OS, pd2[0:Lc, 0:256], ALU.add)
                        pe_ = nb()
                        for pt in range(2):
                            if nsub == 1:
                                kk, vv = KHB[:, pt * 128:(pt + 1) * 128], VB[:, pt * 128:(pt + 1) * 128]
                            elif si == 0:
                                kk, vv = KHB[0:64, pt * 128:(pt + 1) * 128], VB[0:64, pt * 128:(pt + 1) * 128]
                            else:
                                kk, vv = KHBB[:, pt * 128:(pt + 1) * 128], VB[:, pt * 128:(pt + 1) * 128]
                            S.mm(pe_[:, pt * 128:(pt + 1) * 128], kk, vv)
                        for pt in range(2):
                            for h2 in range(2):
                                r0 = h2 * 64
                                sv = Sst[r0:r0 + 64, pt, :]
                                S.stt(sv, sv, DEC[r0:r0 + 64, pt * nseg + seg:pt * nseg + seg + 1],
                                      pe_[r0:r0 + 64, pt * 128 + r0:pt * 128 + r0 + 64], ALU.mult, ALU.add)
                        if is_s:
                            st_out(ci)
                        else:
                            for h2 in range(2):
                                r0 = h2 * 64
                                S.copy(Sstb[r0:r0 + 64, :, r0:r0 + 64], Sst[r0:r0 + 64, :, :], eng="act")
                            if t == 15 and ci == nch - 1 and si == nsub - 1:
                                st_pout()
                    S.act(SQO, OS, AF.Square)
                    SS = SMT[0:Lc, 32:36]
                    S.rsum(SS, SQO.r("p (h v) -> p h v", v=64))
                    S.act(SS, SS, AF.Ln, bias=EPS6[0:Lc], scale=1.0 / 64)
                    S.act(SS, SS, AF.Exp, scale=-0.5)
                    OS3 = OS.r("p (h v) -> p h v", v=64)
                    S.tt(OS3, OS3, SS.us(2).bc([Lc, 4, 64]), ALU.mult)
                    S.tt(SQO, OS, SR, ALU.mult)
                    to_fm(SQO, Lc, o0, mc)

            def ssd(t, Lc):
                is_s = (t == 16)
                nch = 128 // Lc
                F, FB, TT, TB, SMT = S_F, S_FB, S_TT, S_TB, S_SMT
                BCB = FB[1]
                S.copy(BCB, CV[:, 2:6, :], eng="act")
                if is_s:
                    paA = nb()
                    proj_tm(paA[:, 0:256], 0, 128, C_SZ, 256)
                    proj_tm(paA[:, 256:260], 0, 128, C_DT, 4)
                    DTA, ATA = SDTAT[:, 0:4], SDTAT[:, 4:8]
                    S.tt(DTA, paA[:, 256:260], DTB, ALU.add)
                    S.act(DTA, DTA, AF.Exp)
                    S.act(DTA, DTA, AF.Ln, bias=ONE1)
                    S.tt(ATA, DTA, NEGA, ALU.mult)
                    silu_tm(SZALL, paA[:, 0:256], 128, SZTMP)
                for ci in range(nch):
                    o0 = ci * Lc
                    sl = slice(o0, o0 + Lc)
                    SM = SMT[0:Lc, 36:52]
                    DT, AT, CUMT, WE = SM[:, 0:4], SM[:, 4:8], SM[:, 8:12], SM[:, 12:16]
                    DECE = SMT[:, 52:56].k(("mixs", "DECE"))
                    DCOL = SMT[:, 56:58].k(("mixs", "DCOL"))
                    UA, DDm, EC = F[3], F[4], F[2]
                    ATT, XDT, XDTW, BTB = TB[2][0:Lc, :], TB[0][0:Lc, 0:256], TB[1][0:Lc, 0:256], TB[3][0:Lc, 0:256]
                    CT = FB[2]
                    XST, EG, SR, Y1 = TT[0][0:Lc, 0:256], TT[1][0:Lc, 0:256], TT[2][0:Lc, 0:256], TT[3][0:Lc, 0:256]
                    if is_s:
                        S.dma([(N0, state_ssm[l, ci].rearrange("(g h2) p n -> (h2 p) g n", g=2))])
                        pq = nb()
                        for g in range(2):
                            S.tr(pq[:, g * 128:(g + 1) * 128], N0[:, g, :], IDENT)
                        S.copy(SSTb, pq[:, 0:256], eng="act")
                    pa = nb()
                    if is_s:
                        S.mm(pa[0:Lc, 0:256], IDENT[:, o0:o0 + Lc], SZALL)
                        S.mm(pa[0:Lc, 256:264], IDENT[:, o0:o0 + Lc], SDTAT)
                        S.copy(SR, pa[0:Lc, 0:256], eng="act")
                        S.copy(SM[:, 0:8], pa[0:Lc, 256:264])
                    else:
                        proj_tm(pa[0:Lc, 0:256], o0, Lc, C_SZ, 256)
                        proj_tm(pa[0:Lc, 256:260], o0, Lc, C_DT, 4)
                        S.tt(DT, pa[0:Lc, 256:260], DTB[0:Lc, :], ALU.add)
                        S.act(DT, DT, AF.Exp)
                        S.act(DT, DT, AF.Ln, bias=ONE1[0:Lc])
                        S.tt(AT, DT, NEGA[0:Lc, :], ALU.mult)
                        silu_tm(SR, pa[0:Lc, 0:256], Lc, EG)
                    pb = nb()
                    S.mm(pb[0:Lc, 0:4], UF[0:Lc, 0:Lc], AT)
                    S.copy(CUMT, pb[0:Lc, 0:4])
                    UA3 = UA[0:Lc, :, 0:Lc]
                    S.tt(UA3, UF[0:Lc, 0:Lc].us(1).bc([Lc, 4, Lc]), AT.us(2).bc([Lc, 4, Lc]), ALU.mult)
                    pc = nb()
                    for h in range(4):
                        S.mm(pc[:, h * Lc:(h + 1) * Lc], ONESF[0:Lc, :], UA[0:Lc, h, 0:Lc])
                    pc3 = pc[:, 0:4 * Lc].r("p (h l) -> p h l", l=Lc)
                    for h in range(4):
                        S.stt(DDm[0:Lc, h, 0:Lc], pc3[0:Lc, h, :], CUMT[:, h:h + 1], NEGM[0:Lc, 0:Lc], ALU.subtract, ALU.add)
                    S.act(DDm[0:Lc, :, 0:Lc], DDm[0:Lc, :, 0:Lc], AF.Exp)
                    pd = nb()
                    for g in range(2):
                        S.mm(pd[0:Lc, g * Lc:(g + 1) * Lc], BCB[:, g, sl], BCB[:, 2 + g, sl])
                    ATT3 = ATT[:, 0:4 * Lc].r("p (h l) -> p h l", l=Lc)
                    for g in range(2):
                        S.tt(ATT3[:, 2 * g:2 * g + 2, :], DDm[0:Lc, 2 * g:2 * g + 2, 0:Lc],
                             pd[0:Lc, g * Lc:(g + 1) * Lc].us(1).bc([Lc, 2, Lc]), ALU.mult)
                    pe_ = nb()
                    for j in range(2):
                        S.tr(pe_[0:Lc, j * 128:(j + 1) * 128], CV[:, j, sl], IDENT)
                    for g in range(2):
                        S.tr(pe_[0:Lc, 256 + g * 128:256 + (g + 1) * 128], CV[:, 2 + g, sl], IDENT)
                    S.copy(XST, pe_[0:Lc, 0:256], eng="act")
                    S.copy(BTB, pe_[0:Lc, 256:512], eng="act")
                    XDT3 = XDT.r("p (h v) -> p h v", v=64)
                    S.tt(XDT3, XST.r("p (h v) -> p h v", v=64), DT.us(2).bc([Lc, 4, 64]), ALU.mult)
                    S.tt(WE, pc3[0:Lc, :, Lc - 1], CUMT, ALU.subtract)
                    S.act(WE, WE, AF.Exp)
                    S.tt(XDTW.r("p (h v) -> p h v", v=64), XDT3, WE.us(2).bc([Lc, 4, 64]), ALU.mult)
                    S.act(DECE, pc3[:, :, Lc - 1], AF.Exp)
                    S.act(EC[:, :, 0:Lc], pc3, AF.Exp)
                    for g in range(2):
                        S.tt(CT[:, 2 * g:2 * g + 2, 0:Lc], EC[:, 2 * g:2 * g + 2, 0:Lc],
                             BCB[:, 2 + g, sl].us(1).bc([128, 2, Lc]), ALU.mult)
                    pf = nb()
                    for h in range(4):
                        S.mm(pf[0:Lc, h * 64:(h + 1) * 64], ATT3[:, h, :], XDT[:, h * 64:(h + 1) * 64], start=True, stop=False)
                        S.mm(pf[0:Lc, h * 64:(h + 1) * 64], CT[:, h, 0:Lc], SSTb[:, h * 64:(h + 1) * 64], start=False, stop=True)
                    pg = nb()
                    if not is_s:
                        for g in range(2):
                            S.mm(pg[:, g * 128:(g + 1) * 128], BTB[:, g * 128:(g + 1) * 128], XDTW[:, g * 128:(g + 1) * 128])
                        SST3 = SST.r("p (h v) -> p h v", v=64)
                        S.tt(SST3, SST3, DECE.us(2).bc([128, 4, 64]), ALU.mult)
                        S.tt(SST, SST, pg[:, 0:256], ALU.add)
                        S.copy(SSTb, SST, eng="act")
                        if t == 15:
                            pq = nb()
                            for g in range(2):
                                S.tr(pq[:, g * 128:(g + 1) * 128], SST[:, g * 128:(g + 1) * 128], IDENT)
                            S.copy(N0, pq[:, 0:256].r("p (g n) -> p g n", n=128))
                            out_ops.append(S.dma([(ssm_p[l, 0].rearrange("(g h2) p n -> (h2 p) g n", g=2), N0)]))
                    else:
                        for g in range(2):
                            S.mm(pg[:, g * 128:(g + 1) * 128], XDTW[:, g * 128:(g + 1) * 128], BTB[:, g * 128:(g + 1) * 128])
                        DE2 = DECE.r("p (g h) -> p g h", h=2)
                        S.copy(DCOL[0:64, :], DE2[0:64, :, 0])
                        S.copy(DCOL[64:128, :], DE2[64:128, :, 1])
                        for g in range(2):
                            S.stt(N0[:, g, :], N0[:, g, :], DCOL[:, g:g + 1], pg[:, g * 128:(g + 1) * 128], ALU.mult, ALU.add)
                        out_ops.append(S.dma([(ssm_s[l, ci].rearrange("(g h2) p n -> (h2 p) g n", g=2), N0)]))
                    Y13 = Y1.r("p (h v) -> p h v", v=64)
                    S.tt(Y13, XST.r("p (h v) -> p h v", v=64), DSK[0:Lc, :].us(2).bc([Lc, 4, 64]), ALU.mult)
                    S.tt(Y1, Y1, pf[0:Lc, 0:256], ALU.add)
                    S.tt(Y1, Y1, SR, ALU.mult)
                    S.act(EG, Y1, AF.Square)
                    SS = SMT[0:Lc, 32:34]
                    S.rsum(SS, EG.r("p (g v) -> p g v", v=128))
                    S.act(SS, SS, AF.Ln, bias=EPS6[0:Lc], scale=1.0 / 128)
                    S.act(SS, SS, AF.Exp, scale=-0.5)
                    Y1g = Y1.r("p (g v) -> p g v", v=128)
                    S.tt(Y1g, Y1g, SS.us(2).bc([Lc, 2, 128]), ALU.mult)
                    S.tt(SR, Y1, SN[0:Lc, :], ALU.mult)
                    to_fm(SR, Lc, o0, 6)

            def tile_norm(tt_):
                cc = tt_ * 128
                xo = XNTs[tt_ % 2]
                S.act(SQ, Xv(0, 8, cc, 128), AF.Square)
                bk = nb()
                for k in range(8):
                    S.mm(bk[:, 0:128], ONESB, SQ[:, k, :], start=(k == 0), stop=(k == 7))
                S.act(RSTD, bk[:, 0:128], AF.Ln, bias=EPS6, scale=1.0 / D)
                S.act(RSTD, RSTD, AF.Exp, scale=-0.5)
                for hb in range(2):
                    S.tt(NS, Xv(hb * 4, hb * 4 + 4, cc, 128), PV[:, GROW + hb * 4:GROW + hb * 4 + 4].us(2).bc([128, 4, 128]), ALU.mult)
                    S.tt(xo[:, hb * 4:hb * 4 + 4, :], NS, RSTD.us(1).bc([128, 4, 128]), ALU.mult)

            ntile = min(NT, cfg['tmax'])
            pend_W = [[]]
            tile_norm(0)
            for t in range(ntile):
                is_s = (t == 16)
                c0 = t * 128
                XNT = XNTs[t % 2]
                MIX = MIXs[t % 2]

                if is_s:
                    PX5 = PXB.r("p (h a s c) -> p h a s c", h=2, a=2, s=8, c=24)
                    XB4 = XBC.r("p (j s c) -> p j s c", s=16, c=11)
                else:
                    PX3 = PXB[:, 0:288].r("p (a c) -> p a c", c=144)
                    XB3 = XBC[:, 0:786].r("p (j c) -> p j c", c=131)
                GQ, GK, GSP = G_F[3][:, 0:2, :], G_F[3][:, 2:4, :], G_F[4][:, 0:2, :]
                HQ, HK, HS = G_F[3][:, 0:2, :], G_F[3][:, 2:4, :], G_F[4][:, 0:2, :]
                st_P = S.stream()
                lst_P = st_P.__enter__()
                bank_set[0] = [6]
                F, FB, TT, TB, SMT = P_F, None, P_TT, None, None
                if is_s and cfg["pool"]:
                    for hh in range(2):
                        hp = TT[hh][0:120, 0:256]
                        S.dma([(hp, state_pool[l, hh * 8:(hh + 1) * 8].rearrange("s r c -> (s r) c"))])
                        bk = nb()
                        for pt in range(2):
                            S.tr(bk[:, pt * 120:(pt + 1) * 120], hp[:, pt * 128:(pt + 1) * 128], IDENT[0:120, 0:120])
                        for pt in range(2):
                            S.copy(PX5[:, hh, pt, :, 1:16],
                                   bk[:, pt * 120:(pt + 1) * 120].r("p (s r) -> p s r", r=15))
                p1 = nb()
                if cfg["pool"]:
                    for j in range(2):
                        proj_fm(p1[:, j * 128:(j + 1) * 128], WIN[:, :, C_PX + j * 128:C_PX + (j + 1) * 128])
                    if is_s:
                        for pt in range(2):
                            for hh in range(2):
                                S.copy(PX5[:, hh, pt, :, 16:24],
                                       p1[:, pt * 128 + hh * 64:pt * 128 + (hh + 1) * 64].r("p (s c) -> p s c", c=8), eng="act")
                    else:
                        S.copy(PX3[:, :, 16:144], p1[:, 0:256].r("p (a c) -> p a c", c=128), eng="act")

                if cfg["pool"]:
                    for hh in ((0, 1) if is_s else (None,)):
                        if is_s:
                            NSq, LEN = 8, 24
                            Xp = PXB[:, hh * 384:(hh + 1) * 384].r("p (a c) -> p a c", c=LEN)
                            dcol = slice(hh * 64, (hh + 1) * 64)
                        else:
                            NSq, LEN = 1, 144
                            Xp = PXB[:, 0:288].r("p (a c) -> p a c", c=LEN)
                            dcol = slice(0, 128)
                        nel = 2 * NSq * LEN
                        Sa = SA[:, 0:nel].r("p (a c) -> p a c", c=LEN)
                        Sb_ = SBb[:, 0:nel].r("p (a c) -> p a c", c=LEN)
                        n = LEN - 16

                        def dgrp(Sw, gi):
                            pt, g2 = gi // 2, gi % 2
                            r0 = g2 * 64
                            w = (2, 4, 8, 16)[gi]
                            src = Sw[r0:r0 + 64, pt * NSq:(pt + 1) * NSq, 16:LEN]
                            xx = Xp[r0:r0 + 64, pt * NSq:(pt + 1) * NSq, 16:LEN]
                            dst = DPL[r0:r0 + 64, pt, dcol].r("p (s c) -> p s c", c=n)
                            tmp = PT[r0:r0 + 64, pt, dcol].r("p (s c) -> p s c", c=n)
                            if t == 0:
                                S.tt(tmp, src, INVT[r0:r0 + 64, pt, :].r("p (s c) -> p s c", c=n), ALU.mult, eng="pool")
                                S.tt(dst, tmp, xx, ALU.subtract, eng="pool")
                            else:
                                S.stt(dst, src, 1.0 / w, xx, ALU.mult, ALU.subtract)

                        S.tt(Sa[:, :, 1:LEN], Xp[:, :, 1:LEN], Xp[:, :, 0:LEN - 1], ALU.add, eng="pool")
                        dgrp(Sa, 0)
                        S.tt(Sb_[:, :, 3:LEN], Sa[:, :, 3:LEN], Sa[:, :, 1:LEN - 2], ALU.add, eng="pool")
                        dgrp(Sb_, 1)
                        S.tt(Sa[:, :, 7:LEN], Sb_[:, :, 7:LEN], Sb_[:, :, 3:LEN - 4], ALU.add, eng="pool")
                        dgrp(Sa, 2)
                        S.tt(Sb_[:, :, 15:LEN], Sa[:, :, 15:LEN], Sa[:, :, 7:LEN - 8], ALU.add, eng="pool")
                        dgrp(Sb_, 3)
                    bk = nb()
                    for pt in range(2):
                        S.mm(bk[:, pt * 128:(pt + 1) * 128], PWBD[:, pt, :], DPL[:, pt, :])
                    S.tt(MIX[0], bk[:, 0:256].r("p (a c) -> p a c", c=128), PSC.us(2).bc([128, 2, 128]), ALU.mult)
                    if not is_s:
                        S.copy(PX3[:, :, 0:16], PX3[:, :, 128:144], eng="pool")
                    if t >= 15:
                        bk = nb()
                        proj_tm(bk[:, 0:256], 0, 128, C_PX, 256)
                        PXT = TT[3][:, 0:256]
                        S.copy(PXT, bk[:, 0:256], eng="act")
                        if t == 15:
                            out_ops.append(S.dma([(pool_p[l, 0], PXT[113:128, :])]))
                        else:
                            out_ops.append(S.dma([(pool_s[l, s, 7:15, :], PXT[8 * s:8 * s + 8, :]) for s in range(NSEQ)]))
                            out_ops.append(S.dma([(pool_s[l, :, 0:7, :], state_pool[l, :, 8:15, :])]))
                else:
                    S.memset(MIX[0], 0.0)
                st_P.__exit__(None, None, None)

                st_G = S.stream()
                lst_G = st_G.__enter__()
                bank_set[0] = [0, 1]
                F, FB, TT, TB, SMT = G_F, G_FB, G_TT, G_TB, G_SMT
                if cfg["gla"]:
                    p1g = nb()
                    for j in range(2):
                        proj_fm(p1g[:, j * 128:(j + 1) * 128], WQK[:, :, j * 128:(j + 1) * 128])
                    S.act(GQ, p1g[:, 0:256].r("p (a c) -> p a c", c=128), AF.Identity, scale=32.0 ** -0.5)
                    p2 = nb()
                    for j in range(2):
                        proj_fm(p2[:, j * 128:(j + 1) * 128], WQK[:, :, 256 + j * 128:256 + (j + 1) * 128])
                    proj_fm(p2[:, 256:384], WIN[:, :, C_GLR:C_GLR + 128])
                    S.copy(GK, p2[:, 0:256].r("p (a c) -> p a c", c=128), eng="act")
                    GLR = G_FB[2][0:16, 0, :]
                    S.copy(GLR, p2[0:16, 256:384])
                    p3 = nb()
                    for pt in range(2):
                        S.mm(p3[:, pt * 128:(pt + 1) * 128], GWG[0:16, pt * 128:(pt + 1) * 128], GLR)
                    for pt in range(2):
                        S.act(GSP[:, pt, :], p3[:, pt * 128:(pt + 1) * 128], AF.Exp, bias=NGB[:, l, pt:pt + 1], scale=-1.0)
                    S.act(GSP, GSP, AF.Ln, bias=ONE1)


                    def gl_in(s):
                        S.dma([(GLS[h2 * 64:h2 * 64 + 32, :, :], state_gla[l, s, h2::2].rearrange("t k v -> k t v")) for h2 in range(2)])
                        for h2 in range(2):
                            S.copy(GLSb[h2 * 64:h2 * 64 + 64, :, h2 * 64:h2 * 64 + 64], GLS[h2 * 64:h2 * 64 + 64, :, :], eng="act")

                    def gl_out(s):
                        out_ops.append(S.dma([(gla_s[l, s, h2::2].rearrange("t k v -> k t v"), GLS[h2 * 64:h2 * 64 + 32, :, :]) for h2 in range(2)]))

                    def gl_pout():
                        out_ops.append(S.dma([(gla_p[l, 0, h2::2].rearrange("t k v -> k t v"), GLS[h2 * 64:h2 * 64 + 32, :, :]) for h2 in range(2)]))

                    gla_like("gla", t, GQ, GK, GSP, -1.0 / 16, C_GV, GLS, GLSb, GN, 2, 8 if is_s else 128, gl_in, gl_out, gl_pout)
                else:
                    S.memset(MIX[1], 0.0)

                if cfg["hgrn"]:
                    p4 = nb()
                    for j in range(4):
                        proj_fm(p4[:, j * 128:(j + 1) * 128], WIN[:, :, C_RQ + j * 128:C_RQ + (j + 1) * 128])
                    p43 = p4.r("p (a c) -> p a c", c=128)
                    E4, R4 = F[0], F[1]
                    S.act(E4, p43, AF.Exp, scale=-1.0)
                    S.act(R4, E4, AF.Ln, bias=ONE1)
                    S.act(R4, R4, AF.Exp, scale=-1.0)
                    S.tt(HQ, p43[:, 0:2, :], R4[:, 0:2, :], ALU.mult)
                    for pt in range(2):
                        S.act(HS[:, pt, :], R4[:, 2 + pt, :], AF.Ln, bias=LBv[:, l, pt:pt + 1], scale=OMLv[:, l, pt:pt + 1])
                        S.stt(HK[:, pt, :], E4[:, 2 + pt, :], OMLv[:, l, pt:pt + 1], R4[:, 2 + pt, :], ALU.mult, ALU.mult)

                    def hg_in(s):
                        S.dma([(HGS, state_hgrn[l, s].rearrange("(t h2) k v -> (h2 k) t v", t=2))])
                        for h2 in range(2):
                            S.copy(HGSb[h2 * 64:h2 * 64 + 64, :, h2 * 64:h2 * 64 + 64], HGS[h2 * 64:h2 * 64 + 64, :, :], eng="act")

                    def hg_out(s):
                        out_ops.append(S.dma([(hg_s[l, s].rearrange("(t h2) k v -> (h2 k) t v", t=2), HGS)]))

                    def hg_pout():
                        out_ops.append(S.dma([(hg_p[l, 0].rearrange("(t h2) k v -> (h2 k) t v", t=2), HGS)]))

                    gla_like("hgrn", t, HQ, HK, HS, 1.0, C_RI, HGS, HGSb, HN, 4, 8 if is_s else 128, hg_in, hg_out, hg_pout,
                             Ls=(8 if is_s else 64))
                else:
                    S.memset(MIX[2], 0.0)
                st_G.__exit__(None, None, None)

                st_S = S.stream()
                lst_S = st_S.__enter__()
                bank_set[0] = [3, 4, 5]
                F, FB, TT, TB, SMT = S_F, S_FB, S_TT, S_TB, S_SMT
                if is_s and cfg["ssd"]:
                    HCa, HCb = UAf[0:48, 0:512], DDf[0:48, 0:256]
                    scv = state_conv[l].rearrange("s r c -> (s r) c")
                    S.dma([(HCa, scv[:, 0:512]), (HCb, scv[:, 512:768])])
                    for half in range(2):
                        bk = nb()
                        for jj in range(3):
                            j = half * 3 + jj
                            hsrc = HCa[:, j * 128:(j + 1) * 128] if j < 4 else HCb[:, (j - 4) * 128:(j - 3) * 128]
                            S.tr(bk[:, jj * 48:(jj + 1) * 48], hsrc, IDENT[0:48, 0:48])
                        for jj in range(3):
                            j = half * 3 + jj
                            S.copy(XB4[:, j, :, 0:3], bk[:, jj * 48:(jj + 1) * 48].r("p (s r) -> p s r", r=3))

                if cfg["ssd"]:
                    p5, p6 = nb(), nb()
                    for j in range(6):
                        dstb = p5[:, j * 128:(j + 1) * 128] if j < 4 else p6[:, (j - 4) * 128:(j - 3) * 128]
                        proj_fm(dstb, WIN[:, :, C_XBC + j * 128:C_XBC + (j + 1) * 128])
                    if is_s:
                        for j in range(6):
                            srcb = p5[:, j * 128:(j + 1) * 128] if j < 4 else p6[:, (j - 4) * 128:(j - 3) * 128]
                            S.copy(XB4[:, j, :, 3:11], srcb.r("p (s c) -> p s c", c=8), eng=("act" if j % 2 else "dve"))
                        NSq, LEN = 16, 11
                    else:
                        S.copy(XB3[:, 0:4, 3:131], p5.r("p (a c) -> p a c", c=128), eng="act")
                        S.copy(XB3[:, 4:6, 3:131], p6[:, 0:256].r("p (a c) -> p a c", c=128))
                        NSq, LEN = 1, 131
                    n = LEN - 3
                    for j in range(6):
                        xb = XBC[:, j * NSq * LEN:(j + 1) * NSq * LEN].r("p (s c) -> p s c", c=LEN)
                        cv = CV[:, j, :].r("p (s c) -> p s c", c=n)
                        S.ts2(cv, xb[:, :, 0:n], CW[:, 0, j:j + 1], CB[:, j:j + 1], ALU.mult, ALU.add)
                        for d in range(1, 4):
                            S.stt(cv, xb[:, :, d:d + n], CW[:, d, j:j + 1], cv, ALU.mult, ALU.add)
                    EA, EBf = UAf, DDf
                    CVf = CV.r("p a b -> p (a b)")
                    for hh, Eh in enumerate((EA, EBf)):
                        cvh = CVf[:, hh * 384:(hh + 1) * 384]
                        eh = Eh[:, 0:384]
                        S.act(eh, cvh, AF.Exp, scale=-1.0)
                        S.act(eh, eh, AF.Ln, bias=ONE1)
                        S.act(eh, eh, AF.Exp, scale=-1.0)
                        S.tt(cvh, cvh, eh, ALU.mult)
                    if not is_s:
                        S.copy(XB3[:, :, 0:3], XB3[:, :, 128:131], eng="pool")
                    if t >= 15:
                        bk1, bk2 = nb(), nb()
                        proj_tm(bk1[:, 0:512], 0, 128, C_XBC, 512)
                        proj_tm(bk2[:, 0:256], 0, 128, C_XBC + 512, 256)
                        XBTa, XBTb = UAf, DDf[:, 0:256]
                        S.copy(XBTa, bk1, eng="act")
                        S.copy(XBTb, bk2[:, 0:256], eng="act")
                        if t == 15:
                            out_ops.append(S.dma([(conv_p[l, 0, :, 0:512], XBTa[125:128, :]), (conv_p[l, 0, :, 512:768], XBTb[125:128, :])]))
                        else:
                            out_ops.append(S.dma([(conv_s[l, s, :, 0:512], XBTa[8 * s + 5:8 * s + 8, :]) for s in range(NSEQ)]))
                            out_ops.append(S.dma([(conv_s[l, s, :, 512:768], XBTb[8 * s + 5:8 * s + 8, :]) for s in range(NSEQ)]))
                    ssd(t, 8 if is_s else 128)
                else:
                    S.memset(MIX[3], 0.0)
                st_S.__exit__(None, None, None)
                lst_N = []
                if t + 1 < ntile:
                    with S.stream() as lst_N:
                        bank_set[0] = [2]
                        tile_norm(t + 1)

                lst_W = pend_W[0]
                pend_W[0] = []
                bank_set[0] = list(range(8))
                S.merge([lst_P, lst_G, lst_S, lst_N, lst_W], weights=MERGE_W)
                with S.stream() as lw:
                    bank_set[0] = [7]
                    mixt = MIXs[t % 2]
                    for hb in range(2):
                        bk = nb()
                        for j in range(4):
                            m = hb * 4 + j
                            for c in range(8):
                                S.mm(bk[:, j * 128:(j + 1) * 128], WOUT[:, c, m * 128:(m + 1) * 128], mixt[c // 2][:, c % 2, :], start=(c == 0), stop=(c == 7))
                        xv = Xv(hb * 4, hb * 4 + 4, c0, 128)
                        S.tt(xv, xv, bk.r("p (a c) -> p a c", c=128), ALU.add)
                pend_W[0] = lw
                bank_set[0] = list(range(8))
            S.merge([pend_W[0]])
            barrier()

        for l in range(cfg["layers"]):
            if cfg["ffn1"]:
                ffn(l, 0)
            if cfg["mixer"]:
                mixer(l)
            if cfg["ffn2"]:
                ffn(l, 1)

        A = arena("fin")
        NFS = 3
        FSQ = [A.alloc(f"SQ{i}", [128, 8, 128], BF16) for i in range(NFS)]
        FRS = [A.alloc(f"RSTD{i}", [128, 128]) for i in range(NFS)]
        FXG = [[A.alloc(f"XG{i}{h}", [128, 4, 128]) for h in range(2)] for i in range(NFS)]
        FYO = [A.alloc(f"YO{i}", [128, D]) for i in range(NFS)]
        fstreams = []
        for i in range(NFS):
            with S.stream() as fl:
                bank_set[0] = [[0, 1], [2, 3], [4, 5]][i]
                SQ, RSTD, XG, yo = FSQ[i], FRS[i], FXG[i], FYO[i]
                for t in range(i, NT, NFS):
                    c0 = t * 128
                    S.act(SQ, Xv(0, 8, c0, 128), AF.Square)
                    bk = nb()
                    for k in range(8):
                        S.mm(bk[:, 0:128], ONESB, SQ[:, k, :], start=(k == 0), stop=(k == 7))
                    S.act(RSTD, bk[:, 0:128], AF.Ln, bias=EPS6, scale=1.0 / D)
                    S.act(RSTD, RSTD, AF.Exp, scale=-0.5)
                    for hb in range(2):
                        S.tt(XG[hb], Xv(hb * 4, hb * 4 + 4, c0, 128), PV[:, R_FIN + hb * 4:R_FIN + hb * 4 + 4].us(2).bc([128, 4, 128]), ALU.mult)
                        S.tt(XG[hb], XG[hb], RSTD.us(1).bc([128, 4, 128]), ALU.mult)
                        bk = nb()
                        for j in range(4):
                            S.tr(bk[:, j * 128:(j + 1) * 128], XG[hb][:, j, :], IDENT)
                        S.copy(yo[:, hb * 512:(hb + 1) * 512], bk, eng="act" if hb == 0 else "dve")
                    dst = y_p[t * 128:(t + 1) * 128, :] if t < 16 else y_s[:, :]
                    out_ops.append(S.dma([(dst, yo)]))
            fstreams.append(fl)
        bank_set[0] = list(range(8))
        S.merge(fstreams)

        S.emit(final_wait_ops=out_ops)
    return nc


def _consts():
    c = np.zeros((128, 9, 128), np.float32)
    c[:, 0, :] = np.eye(128, dtype=np.float32)
    j = np.arange(128)[:, None]
    i = np.arange(128)[None, :]
    c[:, 1, :] = (j <= i).astype(np.float32)
    c[:, 2, :] = np.where(j <= i, 0.0, -30000.0).astype(np.float32)
    for idx, L in ((3, 128), (4, 64), (5, 8)):
        c[:, idx, :] = (np.arange(128) % L != 0).astype(np.float32)[None, :]
    c[:, 8, :] = ((j <= i) & ((j // 64) == (i // 64))).astype(np.float32)
    tpos = np.arange(128)[None, :] + 1.0
    p = np.arange(128)[:, None]
    for pt in range(2):
        w = np.where(p < 64, (2, 8)[pt], (4, 16)[pt]).astype(np.float32)
        c[:, 6 + pt, :] = 1.0 / np.minimum(tpos, w)
    return c


_NC_CACHE = {}


def kernel(**inputs):
    if "nc" not in _NC_CACHE:
        _NC_CACHE["nc"] = build()
    nc = _NC_CACHE["nc"]
    f = lambda a: np.ascontiguousarray(np.asarray(a, dtype=np.float32))
    inp = {k: f(v) for k, v in inputs.items()}
    cst = _consts()
    in_maps = []
    for c in range(NCORES):
        m = {}
        for k, v in inp.items():
            if k == "x_prompt":
                m[k] = np.ascontiguousarray(v[c])
            elif k == "x_sample":
                m[k] = np.ascontiguousarray(v[NSEQ * c:NSEQ * (c + 1)].reshape(NSEQ * TSQ, D))
            elif k.startswith("state_"):
                m[k] = np.ascontiguousarray(v[:, NSEQ * c:NSEQ * (c + 1)])
            else:
                m[k] = v
        m["c_all"] = cst
        in_maps.append(m)
    res = run_bass_kernel_spmd(nc, in_maps, core_ids=list(range(NCORES)))
    R = res.results
    cat = lambda name, ax: np.concatenate([np.asarray(R[c][name]) for c in range(NCORES)], axis=ax)
    y_prompt = np.stack([np.asarray(R[c]["y_p"]) for c in range(NCORES)], axis=0)
    y_sample = np.concatenate([np.asarray(R[c]["y_s"]).reshape(NSEQ, TSQ, D) for c in range(NCORES)], axis=0)
    outs = (y_prompt, y_sample,
            cat("pool_p", 1), cat("pool_s", 1), cat("gla_p", 1), cat("gla_s", 1),
            cat("hg_p", 1), cat("hg_s", 1), cat("ssm_p", 1), cat("ssm_s", 1),
            cat("conv_p", 1), cat("conv_s", 1))
    return tuple(np.ascontiguousarray(o.astype(np.float32)) for o in outs)
```
